# Optimizing a Trainium2 kernel written in Bass

```python
import math
import jax, jax.numpy as jnp
from jax import lax
import numpy as np

D_MODEL = 2048
BATCH = 2
SEQ = 16384
DEPTH = 1
DEC_BATCH = 16
DEC_SEQ = 16
PAST_LEN = 2048

CHUNK = 64
Q_BLOCK = 128
POOL_WINDOWS = (2, 4, 8, 16)
POOL_GROUPS = 4
POOL_GROUP_DIM = D_MODEL // 16
POOL_WIDTH = POOL_GROUPS * POOL_GROUP_DIM
POOL_HIST = max(POOL_WINDOWS) - 1
ATTN_WIDTH = D_MODEL - POOL_WIDTH
N_HEADS = 12
V_HEAD = ATTN_WIDTH // N_HEADS
D_HEAD = V_HEAD // 2
MIX_WIDTH = POOL_WIDTH + ATTN_WIDTH
IN_WIDTH = 2 * POOL_WIDTH + 4 * ATTN_WIDTH
NORM_EPS = 1e-6
SUBLN_EPS = 1e-5
MASK_VALUE = -1e30

kernel_name = "hybrid_pool_diffattn_stream_step"


def alibi_slopes(n):
    def pow2_slopes(m):
        start = 2.0 ** (-8.0 / m)
        return [start ** (i + 1) for i in range(m)]
    if math.log2(n).is_integer():
        s = pow2_slopes(n)
    else:
        c = 2 ** int(math.floor(math.log2(n)))
        s = pow2_slopes(c) + pow2_slopes(2 * c)[0::2][: n - c]
    return jnp.asarray(np.array(s, dtype=np.float32))


def rmsnorm(x, w, eps=NORM_EPS):
    x32 = x.astype(jnp.float32)
    y = x32 * lax.rsqrt(jnp.mean(x32 * x32, axis=-1, keepdims=True) + eps) * w.astype(jnp.float32)
    return y.astype(x.dtype)


def split_in(h, w_in):
    b, t, _ = h.shape
    z = jnp.einsum('btd,de->bte', h, w_in)
    o = np.cumsum([0, POOL_WIDTH, POOL_WIDTH, ATTN_WIDTH, ATTN_WIDTH, ATTN_WIDTH, ATTN_WIDTH])
    u = z[..., o[0]:o[1]]
    g_pool = z[..., o[1]:o[2]]
    q = z[..., o[2]:o[3]].reshape(b, t, N_HEADS, 2, D_HEAD)
    k = z[..., o[3]:o[4]].reshape(b, t, N_HEADS, 2, D_HEAD)
    v = z[..., o[4]:o[5]].reshape(b, t, N_HEADS, V_HEAD)
    g_attn = z[..., o[5]:o[6]]
    return u, g_pool, q, k, v, g_attn


def pool_mixer(u, hist, hist_valid, w_pool, pool_scale):
    b, t, _ = u.shape
    ext = jnp.concatenate([hist, u], axis=1)
    ext32 = ext.astype(jnp.float32)
    cs = jnp.concatenate([jnp.zeros_like(ext32[:, :1]), lax.cumsum(ext32, axis=1)], axis=1)
    end = cs[:, POOL_HIST + 1:]
    j = jnp.arange(t)
    u32 = u.astype(jnp.float32)
    diffs = []
    for g, w in enumerate(POOL_WINDOWS):
        sl = slice(g * POOL_GROUP_DIM, (g + 1) * POOL_GROUP_DIM)
        start = cs[:, POOL_HIST + 1 - w:POOL_HIST + 1 - w + t, sl]
        count = jnp.minimum(w, hist_valid + j + 1).astype(jnp.float32)
        mean = (end[..., sl] - start) / count[None, :, None]
        diffs.append(mean - u32[..., sl])
    d = jnp.stack(diffs, axis=2)
    y = jnp.einsum('btgc,gcd->btgd', d, w_pool.astype(jnp.float32)).reshape(b, t, POOL_WIDTH)
    y = y * pool_scale.astype(jnp.float32)
    return y.astype(u.dtype), ext[:, -POOL_HIST:]


def diff_attend(q, k, v, qpos, kpos, slopes, lam, lam_init, subln_w):
    s = jnp.einsum('bqhcd,bkhcd->bhcqk', q, k, preferred_element_type=jnp.float32) * (D_HEAD ** -0.5)
    rel = jnp.abs(qpos[:, None] - kpos[None, :]).astype(jnp.float32)
    bias = -slopes[:, None, None] * rel[None]
    allowed = (kpos[None, :] // CHUNK) <= (qpos[:, None] // CHUNK)
    s = jnp.where(allowed[None, None, None], s + bias[None, :, None], MASK_VALUE)
    p = jax.nn.softmax(s, axis=-1)
    pd = p[:, :, 0] - lam * p[:, :, 1]
    o = jnp.einsum('bhqk,bkhe->bqhe', pd.astype(v.dtype), v, preferred_element_type=jnp.float32)
    o = o * lax.rsqrt(jnp.mean(o * o, axis=-1, keepdims=True) + SUBLN_EPS) * subln_w.astype(jnp.float32)
    return o * (1.0 - lam_init)


def merge(pool_out, g_pool, attn_out, g_attn, w_out):
    mix = jnp.concatenate([jax.nn.silu(g_pool) * pool_out, jax.nn.silu(g_attn) * attn_out], axis=-1)
    return jnp.einsum('btm,md->btd', mix, w_out)


def setup_inputs(seed: int = 0) -> dict:
    key = jax.random.key(seed)
    ks = jax.random.split(key, 16)
    f32 = jnp.float32
    return {
        "x_prompt": jax.random.normal(ks[0], (BATCH, SEQ, D_MODEL), f32),
        "x_sample": jax.random.normal(ks[1], (DEC_BATCH, DEC_SEQ, D_MODEL), f32),
        "cache_k": jax.random.normal(ks[2], (DEPTH, DEC_BATCH, PAST_LEN, N_HEADS, 2 * D_HEAD), f32),
        "cache_v": jax.random.normal(ks[3], (DEPTH, DEC_BATCH, PAST_LEN, N_HEADS, V_HEAD), f32),
        "state_pool": jax.random.normal(ks[4], (DEPTH, DEC_BATCH, POOL_HIST, POOL_WIDTH), f32),
        "norm_w": 1.0 + 0.1 * jax.random.normal(ks[5], (DEPTH, D_MODEL), f32),
        "w_in": jax.random.normal(ks[6], (DEPTH, D_MODEL, IN_WIDTH), f32) * D_MODEL ** -0.5,
        "w_pool": jax.random.normal(ks[7], (DEPTH, POOL_GROUPS, POOL_GROUP_DIM, POOL_GROUP_DIM), f32) * POOL_GROUP_DIM ** -0.5,
        "pool_scale": 1.0 + 0.1 * jax.random.normal(ks[8], (DEPTH, POOL_WIDTH), f32),
        "lambda_q1": 0.1 * jax.random.normal(ks[9], (DEPTH, D_HEAD), f32),
        "lambda_k1": 0.1 * jax.random.normal(ks[10], (DEPTH, D_HEAD), f32),
        "lambda_q2": 0.1 * jax.random.normal(ks[11], (DEPTH, D_HEAD), f32),
        "lambda_k2": 0.1 * jax.random.normal(ks[12], (DEPTH, D_HEAD), f32),
        "subln_w": 1.0 + 0.1 * jax.random.normal(ks[13], (DEPTH, V_HEAD), f32),
        "w_out": jax.random.normal(ks[14], (DEPTH, MIX_WIDTH, D_MODEL), f32) * MIX_WIDTH ** -0.5,
        "final_norm_w": 1.0 + 0.1 * jax.random.normal(ks[15], (D_MODEL,), f32),
    }


def reference(x_prompt, x_sample, cache_k, cache_v, state_pool, norm_w, w_in, w_pool, pool_scale,
              lambda_q1, lambda_k1, lambda_q2, lambda_k2, subln_w, w_out, final_norm_w):
    f32 = jnp.float32
    slopes = alibi_slopes(N_HEADS)
    b, s, _ = x_prompt.shape
    db, t, _ = x_sample.shape
    past = cache_k.shape[2]
    nb = s // Q_BLOCK
    hp, hs = x_prompt, x_sample
    kp_l, vp_l, pp_l, ks_l, vs_l, ps_l = [], [], [], [], [], []
    for l in range(DEPTH):
        lam_init = 0.8 - 0.6 * math.exp(-0.3 * l)
        lam = (jnp.exp(jnp.sum(lambda_q1[l].astype(f32) * lambda_k1[l].astype(f32)))
               - jnp.exp(jnp.sum(lambda_q2[l].astype(f32) * lambda_k2[l].astype(f32))) + lam_init)

        u, gpool, q, k, v, gattn = split_in(rmsnorm(hp, norm_w[l]), w_in[l])
        pool_out, pool_state_p = pool_mixer(u, jnp.zeros((b, POOL_HIST, POOL_WIDTH), u.dtype), 0,
                                            w_pool[l], pool_scale[l])
        kpos = jnp.arange(s)
        qblocks = q.reshape(b, nb, Q_BLOCK, N_HEADS, 2, D_HEAD).swapaxes(0, 1)

        def block_fn(args, k=k, v=v, lam=lam, lam_init=lam_init, sw=subln_w[l]):
            qb, i = args
            qpos = i * Q_BLOCK + jnp.arange(Q_BLOCK)
            return diff_attend(qb, k, v, qpos, kpos, slopes, lam, lam_init, sw).astype(v.dtype)

        ob = lax.map(block_fn, (qblocks, jnp.arange(nb)))
        attn_out = ob.swapaxes(0, 1).reshape(b, s, ATTN_WIDTH)
        hp = hp + merge(pool_out, gpool, attn_out, gattn, w_out[l])
        kp_l.append(k.reshape(b, s, N_HEADS, 2 * D_HEAD))
        vp_l.append(v)
        pp_l.append(pool_state_p)

        u2, gpool2, q2, k2, v2, gattn2 = split_in(rmsnorm(hs, norm_w[l]), w_in[l])
        pool_out2, pool_state_s = pool_mixer(u2, state_pool[l].astype(u2.dtype), POOL_HIST,
                                             w_pool[l], pool_scale[l])
        k_all = jnp.concatenate([cache_k[l].reshape(db, past, N_HEADS, 2, D_HEAD).astype(k2.dtype), k2], axis=1)
        v_all = jnp.concatenate([cache_v[l].astype(v2.dtype), v2], axis=1)
        qpos2 = past + jnp.arange(t)
        kpos2 = jnp.arange(past + t)
        attn2 = diff_attend(q2, k_all, v_all, qpos2, kpos2, slopes, lam, lam_init, subln_w[l])
        attn2 = attn2.astype(v2.dtype).reshape(db, t, ATTN_WIDTH)
        hs = hs + merge(pool_out2, gpool2, attn2, gattn2, w_out[l])
        ks_l.append(k2.reshape(db, t, N_HEADS, 2 * D_HEAD))
        vs_l.append(v2)
        ps_l.append(pool_state_s)

    y_prompt = rmsnorm(hp, final_norm_w)
    y_sample = rmsnorm(hs, final_norm_w)
    k_prompt = jnp.stack(kp_l)
    v_prompt = jnp.stack(vp_l)
    pool_prompt = jnp.stack(pp_l)
    k_sample = jnp.stack(ks_l)
    v_sample = jnp.stack(vs_l)
    pool_sample = jnp.stack(ps_l)
    return (y_prompt, y_sample, k_prompt, v_prompt, pool_prompt, k_sample, v_sample, pool_sample)
```

```python
import math
import os
from contextlib import ExitStack
import numpy as np
import ml_dtypes
import concourse.bass as bass
import concourse.mybir as mybir
from concourse.bass_utils import run_bass_kernel_spmd

F32 = mybir.dt.float32
BF16 = mybir.dt.bfloat16
AF = mybir.ActivationFunctionType
ALU = mybir.AluOpType

D = 2048
H = 12
U0, GP0, Q0, K0, V0, G0 = 0, 512, 1024, 2560, 4096, 5632
WINS = (2, 4, 8, 16)
NEG = -30000.0
SKIP_THRESH = None


def _slopes(n=12):
    def p2(m):
        st = 2.0 ** (-8.0 / m)
        return [st ** (i + 1) for i in range(m)]
    c = 2 ** int(math.floor(math.log2(n)))
    return p2(c) + p2(2 * c)[0::2][: n - c]


SIG = [float(np.float32(s)) for s in _slopes()]


class Buf:
    def __init__(self, t, ro=False):
        self.t = t
        self.wr = None
        self.rd = []
        self.ro = ro

    def __getitem__(self, k):
        return self.t[k]


class Prog:
    ENGS = (("pe", "tensor"), ("act", "scalar"), ("dve", "vector"), ("pool", "gpsimd"), ("sp", "sync"))

    def __init__(self, nc, es, tag):
        self.nc, self.es, self.tag = nc, es, tag
        self.ops = {e: [] for e, _ in self.ENGS}
        self.esem = {e: es.enter_context(nc.semaphore("%s_e_%s" % (tag, e))) for e in ("pe", "act", "dve", "pool")}
        self.ecnt = {e: 0 for e in self.esem}
        self.dsem, self.dcnt = {}, {}

    def sb(self, name, shape, dt, ro=False):
        return Buf(self.es.enter_context(self.nc.sbuf_tensor(self.tag + name, shape, dt)), ro)

    def ps(self, name, shape, dt=F32):
        return Buf(self.es.enter_context(self.nc.psum_tensor(self.tag + name, shape, dt)))

    def _deps(self, reads, writes):
        d = [b.wr for b in reads if b.wr is not None]
        for b in writes:
            d += b.rd
            if b.wr is not None:
                d.append(b.wr)
        return d

    def _upd(self, tok, reads, writes):
        for b in writes:
            b.wr = tok
            b.rd = []
        for b in reads:
            if not b.ro:
                b.rd.append(tok)

    def grp(self, eng, fns, reads=(), writes=()):
        if not isinstance(fns, (list, tuple)):
            fns = [fns]
        deps = self._deps(reads, writes)
        self.ecnt[eng] += 1
        tok = (self.esem[eng], self.ecnt[eng], eng)
        n = len(fns)
        for i, fn in enumerate(fns):
            self.ops[eng].append((fn, deps if i == 0 else [], tok if i == n - 1 else None, 1))
        self._upd(tok, reads, writes)
        return tok

    def dma(self, q, fn, sname, reads=(), writes=()):
        if sname not in self.dsem:
            self.dsem[sname] = self.es.enter_context(self.nc.semaphore("%s_d_%s" % (self.tag, sname)))
            self.dcnt[sname] = 0
        deps = self._deps(reads, writes)
        self.dcnt[sname] += 16
        tok = (self.dsem[sname], self.dcnt[sname], "dma")
        self.ops[q].append((fn, deps, tok, 16))
        self._upd(tok, reads, writes)
        return tok

    def emit(self):
        finals = [(self.esem[e], self.ecnt[e], e) for e in self.esem if self.ecnt[e] > 0]
        finals += [(self.dsem[s], self.dcnt[s], "dma") for s in self.dsem]
        for e, _ in self.ENGS:
            self.ops[e].append((None, [f for f in finals if f[2] != e], None, 0))
        with self.nc.Block() as block:
            for ename, attr in self.ENGS:
                def body(eng, ename=ename):
                    waited = {}
                    for fn, deps, tok, amt in self.ops[ename]:
                        mx = {}
                        for (s, v, src) in deps:
                            if src == "pe" and ename == "pe":
                                continue
                            k = id(s)
                            if k not in mx or mx[k][1] < v:
                                mx[k] = (s, v)
                        for k, (s, v) in mx.items():
                            if waited.get(k, 0) < v:
                                eng.wait_ge(s, v)
                                waited[k] = v
                        if fn is None:
                            continue
                        ins = fn(eng)
                        if tok is not None:
                            ins.then_inc(tok[0], amt)
                getattr(block, attr)(body)


def _mm(out, lhsT, rhs, start, stop):
    return lambda e: e.matmul(out, lhsT, rhs, start=start, stop=stop)


class Alt:
    def __init__(self):
        self.i = 0

    def copy(self, p, out, in_, reads, writes):
        self.i += 1
        if self.i % 2:
            return p.grp("act", lambda e: e.copy(out, in_), reads, writes)
        return p.grp("dve", lambda e: e.tensor_copy(out, in_), reads, writes)


def load_weight_bf16(p, io_w, col0, ncols, Wb, stg, alt, qeng="sp"):
    for ch in range(16):
        s = stg[ch % len(stg)]
        src = io_w[ch * 128:(ch + 1) * 128, col0:col0 + ncols]
        p.dma(qeng, lambda e, s=s, src=src: e.dma_start(out=s[:, 0:ncols], in_=src), "wst%d" % (ch % len(stg)), writes=[s])
        eng = "pool" if ch % 2 else "dve"
        p.grp(eng, lambda e, s=s, ch=ch: e.tensor_copy(Wb[:, ch, :], s[:, 0:ncols]), reads=[s], writes=[Wb])


def rmsnorm_T(p, xsrc_fn, nsub, ntok, xbufs, xns, normw, xnT, tps, ident, ss, rstd, alt, eps, cnt):
    for sub in range(nsub):
        i = cnt[0]
        cnt[0] += 1
        xb = xbufs[i % len(xbufs)]
        xn = xns[i % len(xns)]
        p.dma("sp", lambda e, xb=xb, sub=sub: e.dma_start(out=xb[0:ntok, :], in_=xsrc_fn(sub)), "x%d" % (i % len(xbufs)), writes=[xb])
        p.grp("act", lambda e, xb=xb, xn=xn: e.activation(xn[0:ntok, :], xb[0:ntok, :], AF.Square, accum_out=ss[0:ntok, :]),
              reads=[xb], writes=[xn, ss])
        p.grp("act", lambda e: e.activation(rstd[0:ntok, :], ss[0:ntok, :], AF.Ln, bias=eps, scale=1.0 / D), reads=[ss], writes=[rstd])
        p.grp("act", lambda e: e.activation(rstd[0:ntok, :], rstd[0:ntok, :], AF.Exp, scale=-0.5), reads=[rstd], writes=[rstd])
        p.grp("dve", lambda e, xb=xb, xn=xn: e.scalar_tensor_tensor(xn[0:ntok, :], xb[0:ntok, :], rstd[0:ntok, 0:1], normw[0:ntok, :], ALU.mult, ALU.mult),
              reads=[xb, rstd, normw], writes=[xn])
        tp = tps[i % len(tps)]
        fns = []
        for ch in range(16):
            fns.append(lambda e, tp=tp, xn=xn, ch=ch: e.transpose(tp[ch // 8][:, (ch % 8) * 128:(ch % 8) * 128 + ntok],
                                                                   xn[0:ntok, ch * 128:(ch + 1) * 128], ident[0:ntok, 0:ntok]))
        p.grp("pe", fns, reads=[xn, ident], writes=[tp[0], tp[1]])
        for hf in range(2):
            src = tp[hf][:, :].rearrange("p (c t) -> p c t", c=8)[:, :, 0:ntok]
            dst = xnT[:, hf * 8:(hf + 1) * 8, sub * ntok:(sub + 1) * ntok]
            alt.copy(p, dst, src, reads=[tp[hf]], writes=[xnT])


def build(NT):
    S = NT * 512
    NO = NT * 128
    NB = NT // 4
    KTOK = S + 48
    QTOK = NO + 48
    nc = bass.Bass("TRN2", target_bir_lowering=False)

    def din(name, shape, dt=F32):
        return nc.dram_tensor(name, list(shape), dt, kind="ExternalInput").ap()

    def dout(name, shape, dt=F32):
        return nc.dram_tensor(name, list(shape), dt, kind="ExternalOutput").ap()

    def dscr(name, shape, dt):
        return nc.dram_tensor(name, list(shape), dt).ap()

    xa = din("xa", [NT, 4, 128, D])
    xh = din("xh", [NT, 16, D])
    xs = din("xs", [48, D])
    ck = din("ck", [2, 2048, H, 128])
    cv = din("cv", [2, 2048, H, 128])
    spool = din("spool", [2, 15, 512])
    w_in = din("w_in", [D, 7168])
    w_out = din("w_out", [D, D])
    w_pool = din("w_pool", [4, 128, 128])
    normw_d = din("normw", [128, D])
    finw_d = din("finw", [128, D])
    pscale_d = din("pscale", [128, 4])
    lam_d = din("lam", [128, 4, 64])
    swl_d = din("swl", [128, 128])
    tfar_d = din("tfar", [128, H, 512])
    tband_d = din("tband", [128, H, 512])
    tsamp_d = din("tsamp", [128, H, 256])
    tsnew_d = din("tsnew", [16, H, 16])
    invc_d = din("invc", [128, 4, 16])
    identb_d = din("identb", [128, 128], BF16)
    identf_d = din("identf", [128, 128])

    y_own = dout("y_own", [NT, 128, D])
    y_s = dout("y_s", [48, D])
    k_own = dout("k_own", [NT, 128, 1536])
    v_own = dout("v_own", [NT, 128, 1536])
    pool_own = dout("pool_own", [15, 512])
    k_s = dout("k_s", [48, 1536])
    v_s = dout("v_s", [48, 1536])
    pool_s = dout("pool_s", [2, 15, 512])

    KT = dscr("KT_scr", [H, 128, KTOK], BF16)
    V1 = dscr("V1_scr", [H, KTOK, 136], BF16)
    QT = dscr("QT_scr", [H, 128, QTOK], BF16)
    GS = dscr("G_scr", [QTOK, 1536], F32)
    MIXT = dscr("MIXT_scr", [16, 128, QTOK], BF16)

    eps = 1e-6

    PASSES = os.environ.get('DBG_PASSES', 'ABCD')
    with ExitStack() as es:
      if 'A' in PASSES:
        p = Prog(nc, es, "A")
        alt = Alt()
        Wk = p.sb("Wk", [128, 16, 1536], BF16)
        Wv = p.sb("Wv", [128, 16, 1536], BF16)
        xbufs = [p.sb("xb%d" % i, [128, D], F32) for i in range(2)]
        xns = [p.sb("xn%d" % i, [128, D], BF16) for i in range(2)]
        xnT = p.sb("xnT", [128, 16, 512], BF16)
        kT_st = p.sb("kTst", [128, H, 512], BF16)
        v1_st = p.sb("v1st", [128, 4, H, 136], BF16)
        fo_st = [p.sb("fost%d" % i, [128, 1536], F32) for i in range(2)]
        normw = p.sb("normw", [128, D], F32, ro=True)
        ident = p.sb("ident", [128, 128], BF16, ro=True)
        ss = p.sb("ss", [128, 1], F32)
        rstd = p.sb("rstd", [128, 1], F32)
        tps = [[p.ps("tp%d%d" % (i, k), [128, 1024], BF16) for k in range(2)] for i in range(2)]
        accs = [p.ps("acc%d" % i, [128, 512], F32) for i in range(4)]
        acc_i = [0]

        def nacc():
            acc_i[0] += 1
            return accs[acc_i[0] % 4]

        p.dma("sp", lambda e: e.dma_start(out=normw[:, :], in_=normw_d[:, :]), "c0", writes=[normw])
        p.dma("sp", lambda e: e.dma_start(out=ident[:, :], in_=identb_d[:, :]), "c1", writes=[ident])
        p.grp("pool", lambda e: e.memset(v1_st[:, :, :, 128:136], 1.0), writes=[v1_st])
        load_weight_bf16(p, w_in, K0, 1536, Wk, xbufs, alt)
        load_weight_bf16(p, w_in, V0, 1536, Wv, xbufs, alt)
        cnt = [0]
        LV = int(os.environ.get('DBG_LV', '9'))
        FO = os.environ.get('DBG_FO', 'ad')
        for T in (range(NT + (0 if os.environ.get('DBG_NOSAMP') else 1)) if LV >= 2 else ()):
            samp = (T == NT)
            nsub, ntok = (1, 48) if samp else (4, 128)
            tokw = nsub * ntok
            tok0 = S if samp else T * 512
            if samp:
                srcf = lambda sub: xs[:, :]
            else:
                srcf = lambda sub, T=T: xa[T, sub, :, :]
            rmsnorm_T(p, srcf, nsub, ntok, xbufs, xns, normw, xnT, tps, ident, ss, rstd, alt, eps, cnt)
            if LV < 3:
                continue
            for h in range(H):
                a = nacc()
                fns = [_mm(a[:, 0:tokw], Wk[:, ch, h * 128:(h + 1) * 128], xnT[:, ch, 0:tokw], ch == 0, ch == 15) for ch in range(16)]
                p.grp("pe", fns, reads=[Wk, xnT], writes=[a])
                alt.copy(p, kT_st[:, h, 0:tokw], a[:, 0:tokw], reads=[a], writes=[kT_st])
            p.dma("pool", lambda e, tok0=tok0, tokw=tokw: e.dma_start(
                out=KT[:, :, tok0:tok0 + tokw].rearrange("h p t -> p h t"), in_=kT_st[:, :, 0:tokw]), "kts", reads=[kT_st])
            if LV < 4:
                continue
            for sub in range(nsub):
                own = samp or sub == 0
                for which in ((0, 1) if own else (0,)):
                    W = Wv if which == 0 else Wk
                    fo = fo_st[which]
                    for cg in range(3):
                        a = nacc()
                        fns = [_mm(a[0:ntok, :], xnT[:, ch, sub * ntok:(sub + 1) * ntok], W[:, ch, cg * 512:(cg + 1) * 512], ch == 0, ch == 15)
                               for ch in range(16)]
                        p.grp("pe", fns, reads=[W, xnT], writes=[a])
                        if which == 0:
                            p.grp("dve", lambda e, a=a, sub=sub, cg=cg, ntok=ntok: e.tensor_copy(
                                v1_st[0:ntok, sub, cg * 4:(cg + 1) * 4, 0:128], a[0:ntok, :].rearrange("p (h e) -> p h e", h=4)),
                                reads=[a], writes=[v1_st])
                        if own and LV >= 5 and 'a' in FO:
                            if os.environ.get('DBG_FOENG', 'dve') == 'act':
                                p.grp("act", lambda e, a=a, fo=fo, cg=cg, ntok=ntok: e.copy(fo[0:ntok, cg * 512:(cg + 1) * 512], a[0:ntok, :]),
                                      reads=[a], writes=[fo])
                            else:
                                p.grp("dve", lambda e, a=a, fo=fo, cg=cg, ntok=ntok: e.tensor_copy(fo[0:ntok, cg * 512:(cg + 1) * 512], a[0:ntok, :]),
                                      reads=[a], writes=[fo])
                    if own and LV >= 5 and 'd' in FO:
                        if samp:
                            dst = (v_s if which == 0 else k_s)[:, :]
                        else:
                            dst = (v_own if which == 0 else k_own)[T, :, :]
                        p.dma("sp", lambda e, dst=dst, fo=fo, ntok=ntok: e.dma_start(out=dst, in_=fo[0:ntok, :]), "fo%d" % which, reads=[fo])
            for sub in (range(nsub) if LV >= 6 else ()):
                p.dma("pool", lambda e, sub=sub, tok0=tok0, ntok=ntok: e.dma_start(
                    out=V1[:, tok0 + sub * ntok:tok0 + (sub + 1) * ntok, :].rearrange("h t e -> t h e"),
                    in_=v1_st[0:ntok, sub, :, :]), "v1s", reads=[v1_st])
        p.emit()

    for part in ((1, 2) if 'B' in PASSES else ()):
        with ExitStack() as es:
            p = Prog(nc, es, "B%d" % part)
            alt = Alt()
            Wq = p.sb("Wq", [128, 16, 1536], BF16) if part == 1 else None
            Wg = p.sb("Wg", [128, 16, 1536], BF16) if part == 1 else None
            Wu = p.sb("Wu", [128, 16, 1024], BF16) if part == 2 else None
            xbufs = [p.sb("xb%d" % i, [128, D], F32) for i in range(2)]
            xns = [p.sb("xn%d" % i, [128, D], BF16) for i in range(2)]
            xnT = p.sb("xnT", [128, 16, 512], BF16)
            qT_st = p.sb("qTst", [128, H, 512], BF16) if part == 1 else None
            g_st = p.sb("gst", [128, 1536], F32) if part == 1 else None
            uh = p.sb("uh", [128, 4, 512], F32) if part == 2 else None
            ext = [p.sb("ext%d" % i, [128, 4, 4, 144], F32) for i in range(3)] if part == 2 else None
            dd = p.sb("dd", [128, 4, 512], F32) if part == 2 else None
            ddb = p.sb("ddb", [128, 4, 512], BF16) if part == 2 else None
            sgp = p.sb("sgp", [128, 4, 512], F32) if part == 2 else None
            mp_st = p.sb("mpst", [128, 4, 512], BF16) if part == 2 else None
            ytmp = p.sb("ytmp", [128, 512], F32) if part == 2 else None
            wp_f = p.sb("wpf", [128, 4, 128], F32) if part == 2 else None
            wp_b = p.sb("wpb", [128, 4, 128], BF16, ro=True) if part == 2 else None
            pscale = p.sb("pscale", [128, 4], F32, ro=True) if part == 2 else None
            invc = p.sb("invc", [128, 4, 16], F32, ro=True) if part == 2 else None
            hist = p.sb("hist", [128, 4, 2, 16], F32) if part == 2 else None
            up_st = p.sb("upst", [128, 4, 16], F32) if part == 2 else None
            normw = p.sb("normw", [128, D], F32, ro=True)
            ident = p.sb("ident", [128, 128], BF16, ro=True)
            ss = p.sb("ss", [128, 1], F32)
            rstd = p.sb("rstd", [128, 1], F32)
            tps = [[p.ps("tp%d%d" % (i, k), [128, 1024], BF16) for k in range(2)] for i in range(2)]
            accs = [p.ps("acc%d" % i, [128, 512], F32) for i in range(4)]
            acc_i = [0]

            def nacc():
                acc_i[0] += 1
                return accs[acc_i[0] % 4]

            p.dma("sp", lambda e: e.dma_start(out=normw[:, :], in_=normw_d[:, :]), "c0", writes=[normw])
            p.dma("sp", lambda e: e.dma_start(out=ident[:, :], in_=identb_d[:, :]), "c1", writes=[ident])
            if part == 2:
              p.dma("sp", lambda e: e.dma_start(out=pscale[:, :], in_=pscale_d[:, :]), "c2", writes=[pscale])
            if part == 2:
              p.dma("sp", lambda e: e.dma_start(out=invc[:, :, :], in_=invc_d[:, :, :]), "c3", writes=[invc])
            if part == 2:
              p.dma("sp", lambda e: e.dma_start(out=wp_f[:, :, :], in_=w_pool.rearrange("g c d -> c g d")), "c4", writes=[wp_f])
            if part == 2:
              p.grp("dve", lambda e: e.tensor_copy(wp_b[:, :, :], wp_f[:, :, :]), reads=[wp_f], writes=[wp_b])
            if part == 2:
              p.grp("pool", lambda e: e.memset(hist[:, :, :, :], 0.0), writes=[hist])
              p.grp("pool", lambda e: e.memset(ext[0][:, :, :, :], 0.0), writes=[ext[0]])
              p.grp("pool", lambda e: e.memset(ext[1][:, :, :, :], 0.0), writes=[ext[1]])
              p.grp("pool", lambda e: e.memset(ext[2][:, :, :, :], 0.0), writes=[ext[2]])
            for s_ in (range(2) if part == 2 else ()):
                for g in range(4):
                    p.dma("sp", lambda e, s_=s_, g=g: e.dma_start(
                        out=hist[:, g, s_, 1:16], in_=spool[s_, :, g * 128:(g + 1) * 128].rearrange("t c -> c t"),
                        allow_slow_non_contiguous=True), "c5", writes=[hist])
            if part == 1:
                load_weight_bf16(p, w_in, Q0, 1536, Wq, xbufs, alt)
                load_weight_bf16(p, w_in, G0, 1536, Wg, xbufs, alt)
            else:
                load_weight_bf16(p, w_in, U0, 1024, Wu, xbufs, alt)
            cnt = [0]
            for t in range(-1 if part == 2 else 0, NB + 1):
                halo, samp = (t == -1), (t == NB)
                nsub, ntok = (1, 48) if samp else (4, 128)
                tokw = nsub * ntok
                if halo:
                    nh = max(1, (NT * 16) // 128)
                    nsub = nh
                    tokw = nsub * 128
                    srcf = lambda sub: xh[sub * 8:(sub + 1) * 8, :, :].rearrange("a t d -> (a t) d")
                elif samp:
                    srcf = lambda sub: xs[:, :]
                else:
                    srcf = lambda sub, t=t: xa[4 * t + sub, 0, :, :]
                rmsnorm_T(p, srcf, nsub, ntok, xbufs, xns, normw, xnT, tps, ident, ss, rstd, alt, eps, cnt)
                if halo:
                    for g in range(4):
                        a = nacc()
                        fns = [_mm(a[:, 0:tokw], Wu[:, ch, g * 128:(g + 1) * 128], xnT[:, ch, 0:tokw], ch == 0, ch == 15) for ch in range(16)]
                        p.grp("pe", fns, reads=[Wu, xnT], writes=[a])
                        alt.copy(p, uh[:, g, 0:tokw], a[:, 0:tokw], reads=[a], writes=[uh])
                    continue
                qtok0 = NO if samp else t * 512
                for h in (range(H) if part == 1 else ()):
                    a = nacc()
                    fns = [_mm(a[:, 0:tokw], Wq[:, ch, h * 128:(h + 1) * 128], xnT[:, ch, 0:tokw], ch == 0, ch == 15) for ch in range(16)]
                    p.grp("pe", fns, reads=[Wq, xnT], writes=[a])
                    alt.copy(p, qT_st[:, h, 0:tokw], a[:, 0:tokw], reads=[a], writes=[qT_st])
                if part == 1:
                  p.dma("pool", lambda e, qtok0=qtok0, tokw=tokw: e.dma_start(
                    out=QT[:, :, qtok0:qtok0 + tokw].rearrange("h p t -> p h t"), in_=qT_st[:, :, 0:tokw]), "qts", reads=[qT_st])
                for sub in (range(nsub) if part == 1 else ()):
                    for cg in range(3):
                        a = nacc()
                        fns = [_mm(a[0:ntok, :], xnT[:, ch, sub * ntok:(sub + 1) * ntok], Wg[:, ch, cg * 512:(cg + 1) * 512], ch == 0, ch == 15)
                               for ch in range(16)]
                        p.grp("pe", fns, reads=[Wg, xnT], writes=[a])
                        p.grp("act", lambda e, a=a, cg=cg, ntok=ntok: e.activation(g_st[0:ntok, cg * 512:(cg + 1) * 512], a[0:ntok, :], AF.Silu),
                              reads=[a], writes=[g_st])
                    p.dma("sp", lambda e, sub=sub, qtok0=qtok0, ntok=ntok: e.dma_start(
                        out=GS[qtok0 + sub * ntok:qtok0 + (sub + 1) * ntok, :], in_=g_st[0:ntok, :]), "gs", reads=[g_st])
                if part == 1:
                    continue
                exA, exB, exC = ext
                for g in range(4):
                    a = nacc()
                    fns = [_mm(a[:, 0:tokw], Wu[:, ch, g * 128:(g + 1) * 128], xnT[:, ch, 0:tokw], ch == 0, ch == 15) for ch in range(16)]
                    p.grp("pe", fns, reads=[Wu, xnT], writes=[a])
                    if samp:
                        for s_ in range(2):
                            p.grp("dve", lambda e, a=a, g=g, s_=s_: e.tensor_copy(exA[:, g, s_, 16:32], a[:, 32 * s_:32 * s_ + 16]), reads=[a], writes=[exA])
                            p.grp("dve", lambda e, g=g, s_=s_: e.tensor_copy(exA[:, g, s_, 0:16], hist[:, g, s_, :]), reads=[hist], writes=[exA])
                    else:
                        p.grp("dve", lambda e, a=a, g=g: e.tensor_copy(exA[:, g, :, 16:144], a[:, :].rearrange("p (s t) -> p s t", s=4)), reads=[a], writes=[exA])
                        p.grp("pool", lambda e, g=g, t=t: e.tensor_copy(
                            exA[:, g, :, 0:16], uh[:, g, t * 64:(t + 1) * 64].rearrange("p (s t) -> p s t", s=4)), reads=[uh], writes=[exA])
                    a2 = nacc()
                    fns = [_mm(a2[:, 0:tokw], Wu[:, ch, 512 + g * 128:512 + (g + 1) * 128], xnT[:, ch, 0:tokw], ch == 0, ch == 15) for ch in range(16)]
                    p.grp("pe", fns, reads=[Wu, xnT], writes=[a2])
                    p.grp("act", lambda e, a2=a2, g=g, tokw=tokw: e.activation(sgp[:, g, 0:tokw], a2[:, 0:tokw], AF.Silu), reads=[a2], writes=[sgp])
                nsx, wx = (2, 32) if samp else (4, 144)
                for g in range(4):
                    cur = exA
                    st, si = 1, 0
                    while st < WINS[g]:
                        oth = (exB, exC)[si % 2]
                        si += 1
                        p.grp("dve", lambda e, cur=cur, oth=oth, st=st, g=g, nsx=nsx, wx=wx: e.tensor_tensor(
                            oth[:, g, 0:nsx, st:wx], cur[:, g, 0:nsx, st:wx], cur[:, g, 0:nsx, 0:wx - st], ALU.add), reads=[cur], writes=[oth])
                        cur = oth
                        st *= 2
                    if samp:
                        dview = dd[:, g, 0:64].rearrange("p (s t) -> p s t", s=2)[:, :, 0:16]
                    else:
                        dview = dd[:, g, :].rearrange("p (s t) -> p s t", s=4)
                    p.grp("dve", lambda e, cur=cur, g=g, dview=dview, nsx=nsx, wx=wx: e.scalar_tensor_tensor(
                        dview, cur[:, g, 0:nsx, 16:wx], 1.0 / WINS[g], exA[:, g, 0:nsx, 16:wx], ALU.mult, ALU.subtract), reads=[cur, exA], writes=[dd])
                    if t == 0:
                        p.grp("dve", lambda e, cur=cur, g=g: e.tensor_tensor(dd[:, g, 0:16], cur[:, g, 0, 16:32], invc[:, g, :], ALU.mult), reads=[cur, invc], writes=[dd])
                        p.grp("dve", lambda e, g=g: e.tensor_tensor(dd[:, g, 0:16], dd[:, g, 0:16], exA[:, g, 0, 16:32], ALU.subtract), reads=[dd, exA], writes=[dd])
                    if samp:
                        p.grp("pool", lambda e, g=g: e.memset(ddb[:, g, 0:64], 0.0), writes=[ddb])
                        p.grp("dve", lambda e, g=g, dview=dview: e.tensor_copy(ddb[:, g, 0:64].rearrange("p (s t) -> p s t", s=2)[:, :, 0:16], dview), reads=[dd], writes=[ddb])
                    else:
                        p.grp("dve", lambda e, g=g: e.tensor_copy(ddb[:, g, :], dd[:, g, :]), reads=[dd], writes=[ddb])
                    a = nacc()
                    p.grp("pe", [_mm(a[:, 0:tokw], wp_b[:, g, :], ddb[:, g, 0:tokw], True, True)], reads=[wp_b, ddb], writes=[a])
                    p.grp("dve", lambda e, a=a, g=g, tokw=tokw: e.scalar_tensor_tensor(
                        mp_st[:, g, 0:tokw], a[:, 0:tokw], pscale[:, g:g + 1], sgp[:, g, 0:tokw], ALU.mult, ALU.mult), reads=[a, sgp, pscale], writes=[mp_st])
                p.dma("pool", lambda e, qtok0=qtok0, tokw=tokw: e.dma_start(
                    out=MIXT[0:4, :, qtok0:qtok0 + tokw].rearrange("g p t -> p g t"), in_=mp_st[:, :, 0:tokw]), "mps", reads=[mp_st])
                if t == NB - 1:
                    p.grp("dve", lambda e: e.tensor_copy(up_st[:, :, 0:15], exA[:, :, 3, 129:144]), reads=[exA], writes=[up_st])
                    for g in range(4):
                        p.dma("pool", lambda e, g=g: e.dma_start(out=pool_own[:, g * 128:(g + 1) * 128].rearrange("t c -> c t"),
                                                                 in_=up_st[:, g, 0:15], allow_slow_non_contiguous=True), "ups", reads=[up_st])
                if samp:
                    for s_ in range(2):
                        for g in range(4):
                            p.dma("pool", lambda e, g=g, s_=s_: e.dma_start(out=pool_s[s_, :, g * 128:(g + 1) * 128].rearrange("t c -> c t"),
                                                                            in_=exA[:, g, s_, 17:32], allow_slow_non_contiguous=True), "ups2", reads=[exA])
            p.emit()

    with ExitStack() as es:
      if 'C' in PASSES:
        p = Prog(nc, es, "C")
        alt = Alt()
        KTh = p.sb("KTh", [128, S], BF16)
        V1h = p.sb("V1h", [128, NT * 4, 136], BF16)
        QTh = p.sb("QTh", [128, QTOK], BF16)
        Gh = p.sb("Gh", [128, NT, 128], F32)
        tfar = p.sb("tfar", [128, H, 512], F32, ro=True)
        tband = p.sb("tband", [128, H, 512], F32, ro=True)
        tsamp = p.sb("tsamp", [128, H, 256], F32, ro=True)
        tsnew = p.sb("tsnew", [16, H, 16], F32, ro=True)
        swl = p.sb("swl", [128, 128], F32, ro=True)
        lam = p.sb("lam", [128, 4, 64], F32)
        lamt = p.sb("lamt", [128, 2, 64], F32)
        lam2 = p.sb("lam2", [128, 2], F32)
        nlam = p.sb("nlam", [128, 1], F32, ro=True)
        identb = p.sb("identb", [128, 128], BF16, ro=True)
        identf = p.sb("identf", [128, 128], F32, ro=True)
        LOOKAHEAD = 2
        NSB = 4
        Sbs = [p.sb("Sb%d" % i, [128, 512], F32) for i in range(NSB)]
        Pts = [p.sb("Pt%d" % i, [128, 512], BF16) for i in range(NSB)]
        rr = p.sb("rr", [128, 2], F32)
        t2 = p.sb("t2", [128, 128], F32)
        oo = p.sb("oo", [128, 128], F32)
        junk = p.sb("junk", [128, 128], F32)
        ssq = p.sb("ssq", [128, 1], F32)
        rs2 = p.sb("rs2", [128, 1], F32)
        mixb = p.sb("mixb", [128, 128], BF16)
        mT_st = p.sb("mTst", [128, 128], BF16)
        kc = p.sb("kc", [128, 16, 128], F32)
        vcf = p.sb("vcf", [128, 16, 128], F32)
        KTc = p.sb("KTc", [128, 2048], BF16)
        V1c = p.sb("V1c", [128, 16, 136], BF16)
        KTn = p.sb("KTn", [128, 16], BF16)
        V1n = p.sb("V1n", [16, 136], BF16)
        Gs = p.sb("Gs", [16, 128], F32)
        Sbn = p.sb("Sbn", [16, 16], F32)
        Ptn = p.sb("Ptn", [16, 16], BF16)
        Sps = [p.ps("Sp%d" % i, [128, 512], F32) for i in range(4)]
        Obs = [p.ps("Ob%d" % i, [128, 512], F32) for i in range(2)]
        tpp = p.ps("tpp", [128, 1024], BF16)
        tpf = p.ps("tpf", [128, 512], F32)

        for i, (dst, src) in enumerate(((tfar, tfar_d), (tband, tband_d), (tsamp, tsamp_d), (tsnew, tsnew_d), (lam, lam_d))):
            p.dma("sp", lambda e, dst=dst, src=src: e.dma_start(out=dst[:, :, :], in_=src[:, :, :]), "c%d" % i, writes=[dst])
        for i, (dst, src) in enumerate(((swl, swl_d), (identb, identb_d), (identf, identf_d))):
            p.dma("sp", lambda e, dst=dst, src=src: e.dma_start(out=dst[:, :], in_=src[:, :]), "d%d" % i, writes=[dst])
        p.grp("dve", lambda e: e.tensor_tensor(lamt[:, 0, :], lam[:, 0, :], lam[:, 1, :], ALU.mult), reads=[lam], writes=[lamt])
        p.grp("dve", lambda e: e.tensor_tensor(lamt[:, 1, :], lam[:, 2, :], lam[:, 3, :], ALU.mult), reads=[lamt, lam], writes=[lamt])
        p.grp("dve", lambda e: e.tensor_reduce(lam2[:, :], lamt[:, :, :], mybir.AxisListType.X, ALU.add), reads=[lamt], writes=[lam2])
        p.grp("act", lambda e: e.activation(lam2[:, :], lam2[:, :], AF.Exp), reads=[lam2], writes=[lam2])
        p.grp("dve", lambda e: e.tensor_tensor(nlam[:, :], lam2[:, 1:2], lam2[:, 0:1], ALU.subtract), reads=[lam2], writes=[nlam])
        p.grp("dve", lambda e: e.tensor_scalar(nlam[:, :], nlam[:, :], -0.2, None, ALU.add), reads=[nlam], writes=[nlam])

        unit = [0]
        fcount = [0]

        def finalize(Ob, nq, Gap, mix_dst_fn):
            O1 = Ob[0][0:nq, 0:128]
            O2 = Ob[1][0:nq, 0:128]
            p.grp("dve", lambda e, nq=nq, Ob=Ob: e.reciprocal(rr[0:nq, 0:1], Ob[0][0:nq, 128:129]), reads=[Ob[0]], writes=[rr])
            p.grp("dve", lambda e, nq=nq, Ob=Ob: e.reciprocal(rr[0:nq, 1:2], Ob[1][0:nq, 128:129]), reads=[Ob[1], rr], writes=[rr])
            p.grp("dve", lambda e, nq=nq: e.tensor_tensor(rr[0:nq, 1:2], rr[0:nq, 1:2], nlam[0:nq, :], ALU.mult), reads=[rr, nlam], writes=[rr])
            p.grp("dve", lambda e, nq=nq, O2=O2: e.tensor_scalar(t2[0:nq, :], O2, rr[0:nq, 1:2], None, ALU.mult), reads=[Ob[1], rr], writes=[t2])
            p.grp("dve", lambda e, nq=nq, O1=O1: e.scalar_tensor_tensor(oo[0:nq, :], O1, rr[0:nq, 0:1], t2[0:nq, :], ALU.mult, ALU.add), reads=[Ob[0], rr, t2], writes=[oo])
            p.grp("dve", lambda e, nq=nq: e.scalar_tensor_tensor(junk[0:nq, :], oo[0:nq, :], 1.0, oo[0:nq, :], ALU.mult, ALU.mult, accum_out=ssq[0:nq, :]),
                  reads=[oo], writes=[junk, ssq])
            p.grp("act", lambda e, nq=nq: e.activation(rs2[0:nq, :], ssq[0:nq, :], AF.Ln, bias=1e-5, scale=1.0 / 128), reads=[ssq], writes=[rs2])
            p.grp("act", lambda e, nq=nq: e.activation(rs2[0:nq, :], rs2[0:nq, :], AF.Exp, scale=-0.5, bias=math.log(0.8)), reads=[rs2], writes=[rs2])
            p.grp("dve", lambda e, nq=nq: e.scalar_tensor_tensor(oo[0:nq, :], oo[0:nq, :], rs2[0:nq, 0:1], swl[0:nq, :], ALU.mult, ALU.mult),
                  reads=[oo, rs2, swl], writes=[oo])
            p.grp("dve", lambda e, nq=nq, Gap=Gap: e.tensor_tensor(mixb[0:nq, :], oo[0:nq, :], Gap, ALU.mult), reads=[oo, Gh, Gs], writes=[mixb])
            p.grp("pe", lambda e, nq=nq: e.transpose(tpp[:, 0:nq], mixb[0:nq, :], identb[0:nq, 0:nq]), reads=[mixb, identb], writes=[tpp])
            p.grp("act", lambda e, nq=nq: e.copy(mT_st[:, 0:nq], tpp[:, 0:nq]), reads=[tpp], writes=[mT_st])
            p.dma("pool", lambda e, nq=nq, mix_dst_fn=mix_dst_fn: e.dma_start(out=mix_dst_fn(), in_=mT_st[:, 0:nq]), "mts", reads=[mT_st])

        for h in range(H):
            sg = SIG[h]
            for qd in range(4):
                w = S // 4
                p.dma("sp", lambda e, h=h, qd=qd, w=w: e.dma_start(out=KTh[:, qd * w:(qd + 1) * w], in_=KT[h, :, qd * w:(qd + 1) * w]), "kth", writes=[KTh])
                nk = NT
                p.dma("sp", lambda e, h=h, qd=qd, nk=nk: e.dma_start(
                    out=V1h[:, qd * nk:(qd + 1) * nk, :], in_=V1[h, qd * nk * 128:(qd + 1) * nk * 128, :].rearrange("(k p) e -> p k e", p=128)),
                    "v1h", writes=[V1h])
            p.dma("sp", lambda e, h=h: e.dma_start(out=QTh[:, :], in_=QT[h, :, :]), "qth", writes=[QTh])
            p.dma("sp", lambda e, h=h: e.dma_start(out=Gh[:, :, :], in_=GS[0:NO, h * 128:(h + 1) * 128].rearrange("(q p) e -> p q e", p=128)), "gh", writes=[Gh])
            Ob = Obs
            allu = []
            for qs in range(NT):
                units = []
                for G in range(qs + 1):
                    if SKIP_THRESH is not None and G < qs and sg * 512 * (qs - G - 1) > SKIP_THRESH:
                        continue
                    for c in range(2):
                        units.append((G, c))
                firstc = {}
                lastc = {}
                for idx, (G, c) in enumerate(units):
                    firstc.setdefault(c, idx)
                    lastc[c] = idx
                for idx, (G, c) in enumerate(units):
                    allu.append((qs, G, c, firstc[c] == idx, lastc[c] == idx, idx == len(units) - 1))

            def stage1_pair(un0, un1, h=h, sg=sg):
                qs, G = un0[0], un0[1]
                assert un1[0] == qs and un1[1] == G and un0[2] == 0 and un1[2] == 1
                u0 = unit[0]
                unit[0] += 2
                bufs = [(Sps[(u0 + c) % 4], Sbs[(u0 + c) % NSB], Pts[(u0 + c) % NSB]) for c in range(2)]
                fns = []
                for m in range(4):
                    for c in range(2):
                        fns.append(_mm(bufs[c][0][:, m * 128:(m + 1) * 128], KTh[64 * c:64 * c + 64, (4 * G + m) * 128:(4 * G + m + 1) * 128],
                                       QTh[64 * c:64 * c + 64, qs * 128:(qs + 1) * 128], True, True))
                p.grp("pe", fns, reads=[KTh, QTh], writes=[bufs[0][0], bufs[1][0]])
                tbl = tband if G == qs else tfar
                bias = 0.0 if G == qs else -sg * 512.0 * (qs - G - 1)
                for c in range(2):
                    Sp, Sb, Pt = bufs[c]
                    p.grp("dve", lambda e, Sp=Sp, Sb=Sb, tbl=tbl, h=h: e.scalar_tensor_tensor(Sb[:, :], Sp[:, :], 0.125, tbl[:, h, :], ALU.mult, ALU.add),
                          reads=[Sp, tbl], writes=[Sb])
                    p.grp("act", lambda e, Sb=Sb, Pt=Pt, bias=bias: e.activation(Pt[:, :], Sb[:, :], AF.Exp, bias=bias, scale=1.0), reads=[Sb], writes=[Pt])
                return bufs[0][2], bufs[1][2]

            def stage1(un, h=h, sg=sg):
                qs, G, c, fst, lst, lastq = un
                u = unit[0]
                unit[0] += 1
                Sp, Sb, Pt = Sps[u % 4], Sbs[u % NSB], Pts[u % NSB]
                fns = [_mm(Sp[:, m * 128:(m + 1) * 128], KTh[64 * c:64 * c + 64, (4 * G + m) * 128:(4 * G + m + 1) * 128],
                           QTh[64 * c:64 * c + 64, qs * 128:(qs + 1) * 128], True, True) for m in range(4)]
                p.grp("pe", fns, reads=[KTh, QTh], writes=[Sp])
                tbl = tband if G == qs else tfar
                p.grp("dve", lambda e, Sp=Sp, Sb=Sb, tbl=tbl, h=h: e.scalar_tensor_tensor(Sb[:, :], Sp[:, :], 0.125, tbl[:, h, :], ALU.mult, ALU.add),
                      reads=[Sp, tbl], writes=[Sb])
                bias = 0.0 if G == qs else -sg * 512.0 * (qs - G - 1)
                p.grp("act", lambda e, Sb=Sb, Pt=Pt, bias=bias: e.activation(Pt[:, :], Sb[:, :], AF.Exp, bias=bias, scale=1.0), reads=[Sb], writes=[Pt])
                return Pt

            def stage2(un, Pt, h=h):
                qs, G, c, fst, lst, lastq = un
                fns = []
                for m in range(4):
                    fns.append(_mm(Ob[c][:, 0:129], Pt[:, m * 128:(m + 1) * 128], V1h[:, 4 * G + m, 0:129], fst and m == 0, lst and m == 3))
                p.grp("pe", fns, reads=[Pt, V1h], writes=[Ob[c]])
                if lastq:
                    finalize(Ob, 128, Gh[:, qs, :], lambda h=h, qs=qs: MIXT[4 + h, :, qs * 128:(qs + 1) * 128])

            pending = []
            assert len(allu) % 2 == 0
            for i in range(0, len(allu), 2):
                pt0, pt1 = stage1_pair(allu[i], allu[i + 1])
                pending.append((allu[i], pt0))
                pending.append((allu[i + 1], pt1))
                while len(pending) > LOOKAHEAD:
                    stage2(*pending.pop(0))
            while pending:
                stage2(*pending.pop(0))
            for s_ in range(2):
                p.dma("sp", lambda e, s_=s_, h=h: e.dma_start(out=kc[:, :, :], in_=ck[s_, :, h, :].rearrange("(k p) e -> p k e", p=128)), "kc", writes=[kc])
                p.dma("sp", lambda e, s_=s_, h=h: e.dma_start(out=vcf[:, :, :], in_=cv[s_, :, h, :].rearrange("(k p) e -> p k e", p=128)), "vc", writes=[vcf])
                p.dma("sp", lambda e, s_=s_, h=h: e.dma_start(out=KTn[:, :], in_=KT[h, :, S + 32 * s_:S + 32 * s_ + 16]), "ktn", writes=[KTn])
                p.dma("sp", lambda e, s_=s_, h=h: e.dma_start(out=V1n[:, :], in_=V1[h, S + 32 * s_:S + 32 * s_ + 16, :]), "v1n", writes=[V1n])
                p.dma("sp", lambda e, s_=s_, h=h: e.dma_start(out=Gs[:, :], in_=GS[NO + 32 * s_:NO + 32 * s_ + 16, h * 128:(h + 1) * 128]), "gsn", writes=[Gs])
                p.grp("pool", lambda e: e.memset(V1c[:, :, 128:136], 1.0), writes=[V1c])
                p.grp("pool", lambda e: e.tensor_copy(V1c[:, :, 0:128], vcf[:, :, :]), reads=[vcf], writes=[V1c])
                for k4 in range(4):
                    fns = [lambda e, k4=k4, m=m, Ob=Ob: e.transpose(tpf[:, m * 128:(m + 1) * 128], kc[:, 4 * k4 + m, :], identf[:, :]) for m in range(4)]
                    p.grp("pe", fns, reads=[kc, identf], writes=[tpf])
                    alt.copy(p, KTc[:, k4 * 512:(k4 + 1) * 512], tpf[:, :], reads=[tpf], writes=[KTc])
                Ob = Obs
                fcount[0] += 1
                qcol = NO + 32 * s_
                for c in range(2):
                    u = unit[0]
                    unit[0] += 1
                    Sp, Sb, Pt = Sps[u % 4], Sbs[u % NSB], Pts[u % NSB]
                    fns = [_mm(Sp[:, kt * 16:(kt + 1) * 16], KTc[64 * c:64 * c + 64, kt * 128:(kt + 1) * 128], QTh[64 * c:64 * c + 64, qcol:qcol + 16], True, True)
                           for kt in range(16)]
                    fns.append(_mm(Sp[0:16, 256:272], KTn[64 * c:64 * c + 64, :], QTh[64 * c:64 * c + 64, qcol:qcol + 16], True, True))
                    p.grp("pe", fns, reads=[KTc, KTn, QTh], writes=[Sp])
                    p.grp("dve", lambda e, Sp=Sp, Sb=Sb, h=h: e.scalar_tensor_tensor(Sb[:, 0:256], Sp[:, 0:256], 0.125, tsamp[:, h, :], ALU.mult, ALU.add),
                          reads=[Sp, tsamp], writes=[Sb])
                    p.grp("dve", lambda e, Sp=Sp, h=h: e.scalar_tensor_tensor(Sbn[:, :], Sp[0:16, 256:272], 0.125, tsnew[:, h, :], ALU.mult, ALU.add),
                          reads=[Sp, tsnew], writes=[Sbn])
                    p.grp("act", lambda e, Sb=Sb, Pt=Pt: e.activation(Pt[:, 0:256], Sb[:, 0:256], AF.Exp), reads=[Sb], writes=[Pt])
                    p.grp("act", lambda e: e.activation(Ptn[:, :], Sbn[:, :], AF.Exp), reads=[Sbn], writes=[Ptn])
                    fns = [_mm(Ob[c][0:16, 0:129], Pt[:, kt * 16:(kt + 1) * 16], V1c[:, kt, 0:129], kt == 0, False) for kt in range(16)]
                    fns.append(_mm(Ob[c][0:16, 0:129], Ptn[:, :], V1n[:, 0:129], False, True))
                    p.grp("pe", fns, reads=[Pt, Ptn, V1c, V1n], writes=[Ob[c]])
                finalize(Ob, 16, Gs[:, :], lambda h=h, qcol=qcol: MIXT[4 + h, :, qcol:qcol + 16])
        p.emit()

    with ExitStack() as es:
      if 'D' in PASSES:
        p = Prog(nc, es, "D")
        alt = Alt()
        Wo = p.sb("Wo", [128, 16, D], BF16)
        stg = [p.sb("stg%d" % i, [128, D], F32) for i in range(2)]
        mT = [p.sb("mT%d" % i, [128, 16, 128], BF16) for i in range(2)]
        xr = [p.sb("xr%d" % i, [128, D], F32) for i in range(2)]
        hb = [p.sb("hb%d" % i, [128, D], F32) for i in range(2)]
        junk = p.sb("junk", [128, D], BF16)
        finw = p.sb("finw", [128, D], F32, ro=True)
        ss = p.sb("ss", [128, 1], F32)
        rstd = p.sb("rstd", [128, 1], F32)
        accs = [p.ps("acc%d" % i, [128, 512], F32) for i in range(8)]
        p.dma("sp", lambda e: e.dma_start(out=finw[:, :], in_=finw_d[:, :]), "c0", writes=[finw])
        load_weight_bf16(p, w_out, 0, D, Wo, stg, alt)
        p.grp("pool", lambda e: e.memset(mT[0][:, :, :], 0.0), writes=[mT[0]])
        p.grp("pool", lambda e: e.memset(mT[1][:, :, :], 0.0), writes=[mT[1]])
        for t in range(NT + 1):
            samp = t == NT
            ntok = 48 if samp else 128
            m, x_, hh = mT[t % 2], xr[t % 2], hb[t % 2]
            q0 = NO if samp else t * 128
            for s_ in ((0, 1) if samp else (0,)):
                w = 16 if samp else 128
                p.dma("sp", lambda e, m=m, q0=q0, s_=s_, w=w: e.dma_start(
                    out=m[:, :, 32 * s_:32 * s_ + w], in_=MIXT[:, :, q0 + 32 * s_:q0 + 32 * s_ + w].rearrange("c p t -> p c t")), "mt%d" % (t % 2), writes=[m])
            xsrc = xs[:, :] if samp else xa[t, 0, :, :]
            p.dma("sp", lambda e, x_=x_, xsrc=xsrc, ntok=ntok: e.dma_start(out=x_[0:ntok, :], in_=xsrc), "xr%d" % (t % 2), writes=[x_])
            for cg in range(4):
                a = accs[(t % 2) * 4 + cg]
                fns = [_mm(a[0:ntok, :], m[:, ch, 0:ntok], Wo[:, ch, cg * 512:(cg + 1) * 512], ch == 0, ch == 15) for ch in range(16)]
                p.grp("pe", fns, reads=[m, Wo], writes=[a])
                p.grp("dve", lambda e, a=a, hh=hh, x_=x_, cg=cg, ntok=ntok: e.tensor_tensor(
                    hh[0:ntok, cg * 512:(cg + 1) * 512], a[0:ntok, :], x_[0:ntok, cg * 512:(cg + 1) * 512], ALU.add), reads=[a, x_], writes=[hh])
            p.grp("act", lambda e, hh=hh, ntok=ntok: e.activation(junk[0:ntok, :], hh[0:ntok, :], AF.Square, accum_out=ss[0:ntok, :]), reads=[hh], writes=[junk, ss])
            p.grp("act", lambda e, ntok=ntok: e.activation(rstd[0:ntok, :], ss[0:ntok, :], AF.Ln, bias=eps, scale=1.0 / D), reads=[ss], writes=[rstd])
            p.grp("act", lambda e, ntok=ntok: e.activation(rstd[0:ntok, :], rstd[0:ntok, :], AF.Exp, scale=-0.5), reads=[rstd], writes=[rstd])
            p.grp("dve", lambda e, hh=hh, ntok=ntok: e.scalar_tensor_tensor(hh[0:ntok, :], hh[0:ntok, :], rstd[0:ntok, 0:1], finw[0:ntok, :], ALU.mult, ALU.mult),
                  reads=[hh, rstd, finw], writes=[hh])
            dst = y_s[:, :] if samp else y_own[t, :, :]
            p.dma("sp", lambda e, dst=dst, hh=hh, ntok=ntok: e.dma_start(out=dst, in_=hh[0:ntok, :]), "yo%d" % (t % 2), reads=[hh])
        p.emit()
    return nc


_CACHE = {}


def kernel(x_prompt, x_sample, cache_k, cache_v, state_pool, norm_w, w_in, w_pool, pool_scale,
           lambda_q1, lambda_k1, lambda_q2, lambda_k2, subln_w, w_out, final_norm_w):
    f = np.float32
    x_prompt = np.asarray(x_prompt, f)
    x_sample = np.asarray(x_sample, f)
    B, S, _ = x_prompt.shape
    NT = S // 512
    NO = NT * 128
    if NT not in _CACHE:
        _CACHE[NT] = build(NT)
    nc = _CACHE[NT]
    sig = np.array(SIG, np.float64)
    identb = np.eye(128, dtype=np.float32).astype(ml_dtypes.bfloat16)
    identf = np.eye(128, dtype=f)
    bc = lambda v: np.ascontiguousarray(np.broadcast_to(np.asarray(v, f).reshape(1, -1), (128, np.asarray(v).size)))
    lam = np.stack([np.asarray(a, f)[0] for a in (lambda_q1, lambda_k1, lambda_q2, lambda_k2)])
    lam_b = np.ascontiguousarray(np.broadcast_to(lam[None], (128, 4, 64)))
    swl = bc(np.asarray(subln_w, f)[0])
    pscale = np.ascontiguousarray(np.asarray(pool_scale, f)[0].reshape(4, 128).T)
    p_ = np.arange(128)[:, None, None, None]
    q_ = np.arange(128)[None, None, None, :]
    hh_ = sig[None, :, None, None]
    kt_ = np.arange(16)[None, None, :, None]
    qq_ = np.arange(16)[None, None, None, :]
    tsamp = (-hh_ * (2048 + qq_ - 128 * kt_ - p_)).astype(f).reshape(128, H, 256)
    kk_ = np.arange(16)[:, None, None]
    tsnew = (-sig[None, :, None] * np.abs(np.arange(16)[None, None, :] - kk_)).astype(f)
    in_maps = []
    for c in range(8):
        b, j = c // 4, c % 4
        order = [j] + [m for m in range(4) if m != j]
        xb = x_prompt[b].reshape(NT, 4, 128, D)
        xa = np.ascontiguousarray(xb[:, order])
        xh = np.zeros((NT, 16, D), f)
        for T in range(NT):
            st = 512 * T + 128 * j
            if st >= 16:
                xh[T] = x_prompt[b, st - 16:st]
        xs = np.zeros((48, D), f)
        xs[0:16] = x_sample[2 * c]
        xs[32:48] = x_sample[2 * c + 1]
        om = np.array(order)[None, None, :, None]
        qpos = 512 + 128 * j + q_
        kpos = 128 * om + p_
        tfar = (-hh_ * (qpos - kpos)).astype(f).reshape(128, H, 512)
        qpos = 128 * j + q_
        allowed = (kpos // 64) <= (qpos // 64)
        tband = np.where(allowed, -hh_ * np.abs(qpos - kpos), NEG).astype(f).reshape(128, H, 512)
        invc = np.zeros((128, 4, 16), f)
        for g, w in enumerate(WINS):
            for t in range(16):
                invc[:, g, t] = 1.0 / (min(w, t + 1) if j == 0 else w)
        in_maps.append({
            "xa": xa, "xh": xh, "xs": xs,
            "ck": np.ascontiguousarray(np.asarray(cache_k, f)[0, 2 * c:2 * c + 2]),
            "cv": np.ascontiguousarray(np.asarray(cache_v, f)[0, 2 * c:2 * c + 2]),
            "spool": np.ascontiguousarray(np.asarray(state_pool, f)[0, 2 * c:2 * c + 2]),
            "w_in": np.ascontiguousarray(np.asarray(w_in, f)[0]), "w_out": np.ascontiguousarray(np.asarray(w_out, f)[0]),
            "w_pool": np.ascontiguousarray(np.asarray(w_pool, f)[0]),
            "normw": bc(np.asarray(norm_w, f)[0]), "finw": bc(np.asarray(final_norm_w, f)),
            "pscale": pscale, "lam": lam_b, "swl": swl, "tfar": tfar, "tband": tband, "tsamp": tsamp, "tsnew": tsnew,
            "invc": invc, "identb": identb, "identf": identf,
        })
    res = run_bass_kernel_spmd(nc, in_maps, core_ids=list(range(8))).results
    DB = x_sample.shape[0]
    y_prompt = np.zeros((B, S, D), f)
    k_prompt = np.zeros((1, B, S, H, 128), f)
    v_prompt = np.zeros((1, B, S, H, 128), f)
    pool_prompt = np.zeros((1, B, 15, 512), f)
    y_sample = np.zeros((DB, 16, D), f)
    k_sample = np.zeros((1, DB, 16, H, 128), f)
    v_sample = np.zeros((1, DB, 16, H, 128), f)
    pool_sample = np.zeros((1, DB, 15, 512), f)
    for c in range(8):
        b, j = c // 4, c % 4
        r = res[c]
        y_prompt[b].reshape(NT, 4, 128, D)[:, j] = r["y_own"]
        k_prompt[0, b].reshape(NT, 4, 128, 1536)[:, j] = r["k_own"]
        v_prompt[0, b].reshape(NT, 4, 128, 1536)[:, j] = r["v_own"]
        if j == 3:
            pool_prompt[0, b] = r["pool_own"]
        for s_ in range(2):
            y_sample[2 * c + s_] = r["y_s"][32 * s_:32 * s_ + 16]
            k_sample[0, 2 * c + s_] = r["k_s"][32 * s_:32 * s_ + 16].reshape(16, H, 128)
            v_sample[0, 2 * c + s_] = r["v_s"][32 * s_:32 * s_ + 16].reshape(16, H, 128)
            pool_sample[0, 2 * c + s_] = r["pool_s"][s_]
    return (y_prompt, y_sample, k_prompt, v_prompt, pool_prompt, k_sample, v_sample, pool_sample)
```

```python
import math
import os
from contextlib import ExitStack
import numpy as np
import ml_dtypes
import concourse.bass as bass
import concourse.mybir as mybir
from concourse.bass_utils import run_bass_kernel_spmd

F32 = mybir.dt.float32
BF16 = mybir.dt.bfloat16
AF = mybir.ActivationFunctionType
ALU = mybir.AluOpType

D = 2048
H = 12
U0, GP0, Q0, K0, V0, G0 = 0, 512, 1024, 2560, 4096, 5632
WINS = (2, 4, 8, 16)
NEG = -30000.0
SKIP_THRESH = None


def _slopes(n=12):
    def p2(m):
        st = 2.0 ** (-8.0 / m)
        return [st ** (i + 1) for i in range(m)]
    c = 2 ** int(math.floor(math.log2(n)))
    return p2(c) + p2(2 * c)[0::2][: n - c]


SIG = [float(np.float32(s)) for s in _slopes()]


class Buf:
    def __init__(self, t, ro=False):
        self.t = t
        self.wr = None
        self.rd = []
        self.ro = ro

    def __getitem__(self, k):
        return self.t[k]


class Prog:
    ENGS = (("pe", "tensor"), ("act", "scalar"), ("dve", "vector"), ("pool", "gpsimd"), ("sp", "sync"))

    def __init__(self, nc, es, tag):
        self.nc, self.es, self.tag = nc, es, tag
        self.ops = {e: [] for e, _ in self.ENGS}
        self.esem = {e: es.enter_context(nc.semaphore("%s_e_%s" % (tag, e))) for e in ("pe", "act", "dve", "pool")}
        self.ecnt = {e: 0 for e in self.esem}
        self.dsem, self.dcnt = {}, {}

    def sb(self, name, shape, dt, ro=False):
        return Buf(self.es.enter_context(self.nc.sbuf_tensor(self.tag + name, shape, dt)), ro)

    def ps(self, name, shape, dt=F32):
        return Buf(self.es.enter_context(self.nc.psum_tensor(self.tag + name, shape, dt)))

    def _deps(self, reads, writes):
        d = [b.wr for b in reads if b.wr is not None]
        for b in writes:
            d += b.rd
            if b.wr is not None:
                d.append(b.wr)
        return d

    def _upd(self, tok, reads, writes):
        for b in writes:
            b.wr = tok
            b.rd = []
        for b in reads:
            if not b.ro:
                b.rd.append(tok)

    def grp(self, eng, fns, reads=(), writes=()):
        if not isinstance(fns, (list, tuple)):
            fns = [fns]
        deps = self._deps(reads, writes)
        self.ecnt[eng] += 1
        tok = (self.esem[eng], self.ecnt[eng], eng)
        n = len(fns)
        for i, fn in enumerate(fns):
            self.ops[eng].append((fn, deps if i == 0 else [], tok if i == n - 1 else None, 1))
        self._upd(tok, reads, writes)
        return tok

    def dma(self, q, fn, sname, reads=(), writes=()):
        if sname not in self.dsem:
            self.dsem[sname] = self.es.enter_context(self.nc.semaphore("%s_d_%s" % (self.tag, sname)))
            self.dcnt[sname] = 0
        deps = self._deps(reads, writes)
        self.dcnt[sname] += 16
        tok = (self.dsem[sname], self.dcnt[sname], "dma")
        self.ops[q].append((fn, deps, tok, 16))
        self._upd(tok, reads, writes)
        return tok

    def emit(self):
        finals = [(self.esem[e], self.ecnt[e], e) for e in self.esem if self.ecnt[e] > 0]
        finals += [(self.dsem[s], self.dcnt[s], "dma") for s in self.dsem]
        for e, _ in self.ENGS:
            self.ops[e].append((None, [f for f in finals if f[2] != e], None, 0))
        with self.nc.Block() as block:
            for ename, attr in self.ENGS:
                def body(eng, ename=ename):
                    waited = {}
                    for fn, deps, tok, amt in self.ops[ename]:
                        mx = {}
                        for (s, v, src) in deps:
                            if src == "pe" and ename == "pe":
                                continue
                            k = id(s)
                            if k not in mx or mx[k][1] < v:
                                mx[k] = (s, v)
                        for k, (s, v) in mx.items():
                            if waited.get(k, 0) < v:
                                eng.wait_ge(s, v)
                                waited[k] = v
                        if fn is None:
                            continue
                        ins = fn(eng)
                        if tok is not None:
                            ins.then_inc(tok[0], amt)
                getattr(block, attr)(body)


def _mm(out, lhsT, rhs, start, stop):
    return lambda e: e.matmul(out, lhsT, rhs, start=start, stop=stop)


class Alt:
    def __init__(self):
        self.i = 0

    def copy(self, p, out, in_, reads, writes):
        self.i += 1
        if self.i % 2:
            return p.grp("act", lambda e: e.copy(out, in_), reads, writes)
        return p.grp("dve", lambda e: e.tensor_copy(out, in_), reads, writes)


def load_weight_bf16(p, io_w, col0, ncols, Wb, stg, alt, qeng="sp"):
    for ch in range(16):
        s = stg[ch % len(stg)]
        src = io_w[ch * 128:(ch + 1) * 128, col0:col0 + ncols]
        p.dma(qeng, lambda e, s=s, src=src: e.dma_start(out=s[:, 0:ncols], in_=src), "wst%d" % (ch % len(stg)), writes=[s])
        eng = "pool" if ch % 2 else "dve"
        p.grp(eng, lambda e, s=s, ch=ch: e.tensor_copy(Wb[:, ch, :], s[:, 0:ncols]), reads=[s], writes=[Wb])


def rmsnorm_T(p, xsrc_fn, nsub, ntok, xbufs, xns, normw, xnT, tps, ident, ss, rstd, alt, eps, cnt):
    for sub in range(nsub):
        i = cnt[0]
        cnt[0] += 1
        xb = xbufs[i % len(xbufs)]
        xn = xns[i % len(xns)]
        p.dma("sp", lambda e, xb=xb, sub=sub: e.dma_start(out=xb[0:ntok, :], in_=xsrc_fn(sub)), "x%d" % (i % len(xbufs)), writes=[xb])
        p.grp("act", lambda e, xb=xb, xn=xn: e.activation(xn[0:ntok, :], xb[0:ntok, :], AF.Square, accum_out=ss[0:ntok, :]),
              reads=[xb], writes=[xn, ss])
        p.grp("act", lambda e: e.activation(rstd[0:ntok, :], ss[0:ntok, :], AF.Ln, bias=eps, scale=1.0 / D), reads=[ss], writes=[rstd])
        p.grp("act", lambda e: e.activation(rstd[0:ntok, :], rstd[0:ntok, :], AF.Exp, scale=-0.5), reads=[rstd], writes=[rstd])
        p.grp("dve", lambda e, xb=xb, xn=xn: e.scalar_tensor_tensor(xn[0:ntok, :], xb[0:ntok, :], rstd[0:ntok, 0:1], normw[0:ntok, :], ALU.mult, ALU.mult),
              reads=[xb, rstd, normw], writes=[xn])
        tp = tps[i % len(tps)]
        fns = []
        for ch in range(16):
            fns.append(lambda e, tp=tp, xn=xn, ch=ch: e.transpose(tp[ch // 8][:, (ch % 8) * 128:(ch % 8) * 128 + ntok],
                                                                   xn[0:ntok, ch * 128:(ch + 1) * 128], ident[0:ntok, 0:ntok]))
        p.grp("pe", fns, reads=[xn, ident], writes=[tp[0], tp[1]])
        for hf in range(2):
            src = tp[hf][:, :].rearrange("p (c t) -> p c t", c=8)[:, :, 0:ntok]
            dst = xnT[:, hf * 8:(hf + 1) * 8, sub * ntok:(sub + 1) * ntok]
            alt.copy(p, dst, src, reads=[tp[hf]], writes=[xnT])


def build(NT):
    S = NT * 512
    NO = NT * 128
    NB = NT // 4
    KTOK = S + 48
    QTOK = NO + 48
    nc = bass.Bass("TRN2", target_bir_lowering=False)

    def din(name, shape, dt=F32):
        return nc.dram_tensor(name, list(shape), dt, kind="ExternalInput").ap()

    def dout(name, shape, dt=F32):
        return nc.dram_tensor(name, list(shape), dt, kind="ExternalOutput").ap()

    def dscr(name, shape, dt):
        return nc.dram_tensor(name, list(shape), dt).ap()

    xa = din("xa", [NT, 4, 128, D])
    xh = din("xh", [NT, 16, D])
    xs = din("xs", [48, D])
    ck = din("ck", [2, 2048, H, 128])
    cv = din("cv", [2, 2048, H, 128])
    spool = din("spool", [2, 15, 512])
    w_in = din("w_in", [D, 7168])
    w_out = din("w_out", [D, D])
    w_pool = din("w_pool", [4, 128, 128])
    normw_d = din("normw", [128, D])
    finw_d = din("finw", [128, D])
    pscale_d = din("pscale", [128, 4])
    lam_d = din("lam", [128, 4, 64])
    swl_d = din("swl", [128, 128])
    tfar_d = din("tfar", [128, H, 512])
    tband_d = din("tband", [128, H, 512])
    tsamp_d = din("tsamp", [128, H, 256])
    tsnew_d = din("tsnew", [16, H, 16])
    invc_d = din("invc", [128, 4, 16])
    identb_d = din("identb", [128, 128], BF16)
    identf_d = din("identf", [128, 128])

    y_own = dout("y_own", [NT, 128, D])
    y_s = dout("y_s", [48, D])
    k_own = dout("k_own", [NT, 128, 1536])
    v_own = dout("v_own", [NT, 128, 1536])
    pool_own = dout("pool_own", [15, 512])
    k_s = dout("k_s", [48, 1536])
    v_s = dout("v_s", [48, 1536])
    pool_s = dout("pool_s", [2, 15, 512])

    KT = dscr("KT_scr", [H, 128, KTOK], BF16)
    V1 = dscr("V1_scr", [H, KTOK, 136], BF16)
    QT = dscr("QT_scr", [H, 128, QTOK], BF16)
    GS = dscr("G_scr", [QTOK, 1536], F32)
    MIXT = dscr("MIXT_scr", [16, 128, QTOK], BF16)

    eps = 1e-6

    PASSES = os.environ.get('DBG_PASSES', 'ABCD')
    with ExitStack() as es:
      if 'A' in PASSES:
        p = Prog(nc, es, "A")
        alt = Alt()
        Wk = p.sb("Wk", [128, 16, 1536], BF16)
        Wv = p.sb("Wv", [128, 16, 1536], BF16)
        xbufs = [p.sb("xb%d" % i, [128, D], F32) for i in range(2)]
        xns = [p.sb("xn%d" % i, [128, D], BF16) for i in range(2)]
        xnT = p.sb("xnT", [128, 16, 512], BF16)
        kT_st = p.sb("kTst", [128, H, 512], BF16)
        v1_st = p.sb("v1st", [128, 4, H, 136], BF16)
        fo_st = [p.sb("fost%d" % i, [128, 1536], F32) for i in range(2)]
        normw = p.sb("normw", [128, D], F32, ro=True)
        ident = p.sb("ident", [128, 128], BF16, ro=True)
        ss = p.sb("ss", [128, 1], F32)
        rstd = p.sb("rstd", [128, 1], F32)
        tps = [[p.ps("tp%d%d" % (i, k), [128, 1024], BF16) for k in range(2)] for i in range(2)]
        accs = [p.ps("acc%d" % i, [128, 512], F32) for i in range(4)]
        acc_i = [0]

        def nacc():
            acc_i[0] += 1
            return accs[acc_i[0] % 4]

        p.dma("sp", lambda e: e.dma_start(out=normw[:, :], in_=normw_d[:, :]), "c0", writes=[normw])
        p.dma("sp", lambda e: e.dma_start(out=ident[:, :], in_=identb_d[:, :]), "c1", writes=[ident])
        p.grp("pool", lambda e: e.memset(v1_st[:, :, :, 128:136], 1.0), writes=[v1_st])
        load_weight_bf16(p, w_in, K0, 1536, Wk, xbufs, alt)
        load_weight_bf16(p, w_in, V0, 1536, Wv, xbufs, alt)
        cnt = [0]
        LV = int(os.environ.get('DBG_LV', '9'))
        FO = os.environ.get('DBG_FO', 'ad')
        for T in (range(NT + (0 if os.environ.get('DBG_NOSAMP') else 1)) if LV >= 2 else ()):
            samp = (T == NT)
            nsub, ntok = (1, 48) if samp else (4, 128)
            tokw = nsub * ntok
            tok0 = S if samp else T * 512
            if samp:
                srcf = lambda sub: xs[:, :]
            else:
                srcf = lambda sub, T=T: xa[T, sub, :, :]
            rmsnorm_T(p, srcf, nsub, ntok, xbufs, xns, normw, xnT, tps, ident, ss, rstd, alt, eps, cnt)
            if LV < 3:
                continue
            for h in range(H):
                a = nacc()
                fns = [_mm(a[:, 0:tokw], Wk[:, ch, h * 128:(h + 1) * 128], xnT[:, ch, 0:tokw], ch == 0, ch == 15) for ch in range(16)]
                p.grp("pe", fns, reads=[Wk, xnT], writes=[a])
                alt.copy(p, kT_st[:, h, 0:tokw], a[:, 0:tokw], reads=[a], writes=[kT_st])
            p.dma("pool", lambda e, tok0=tok0, tokw=tokw: e.dma_start(
                out=KT[:, :, tok0:tok0 + tokw].rearrange("h p t -> p h t"), in_=kT_st[:, :, 0:tokw]), "kts", reads=[kT_st])
            if LV < 4:
                continue
            for sub in range(nsub):
                own = samp or sub == 0
                for which in ((0, 1) if own else (0,)):
                    W = Wv if which == 0 else Wk
                    fo = fo_st[which]
                    for cg in range(3):
                        a = nacc()
                        fns = [_mm(a[0:ntok, :], xnT[:, ch, sub * ntok:(sub + 1) * ntok], W[:, ch, cg * 512:(cg + 1) * 512], ch == 0, ch == 15)
                               for ch in range(16)]
                        p.grp("pe", fns, reads=[W, xnT], writes=[a])
                        if which == 0:
                            p.grp("dve", lambda e, a=a, sub=sub, cg=cg, ntok=ntok: e.tensor_copy(
                                v1_st[0:ntok, sub, cg * 4:(cg + 1) * 4, 0:128], a[0:ntok, :].rearrange("p (h e) -> p h e", h=4)),
                                reads=[a], writes=[v1_st])
                        if own and LV >= 5 and 'a' in FO:
                            if os.environ.get('DBG_FOENG', 'dve') == 'act':
                                p.grp("act", lambda e, a=a, fo=fo, cg=cg, ntok=ntok: e.copy(fo[0:ntok, cg * 512:(cg + 1) * 512], a[0:ntok, :]),
                                      reads=[a], writes=[fo])
                            else:
                                p.grp("dve", lambda e, a=a, fo=fo, cg=cg, ntok=ntok: e.tensor_copy(fo[0:ntok, cg * 512:(cg + 1) * 512], a[0:ntok, :]),
                                      reads=[a], writes=[fo])
                    if own and LV >= 5 and 'd' in FO:
                        if samp:
                            dst = (v_s if which == 0 else k_s)[:, :]
                        else:
                            dst = (v_own if which == 0 else k_own)[T, :, :]
                        p.dma("sp", lambda e, dst=dst, fo=fo, ntok=ntok: e.dma_start(out=dst, in_=fo[0:ntok, :]), "fo%d" % which, reads=[fo])
            for sub in (range(nsub) if LV >= 6 else ()):
                p.dma("pool", lambda e, sub=sub, tok0=tok0, ntok=ntok: e.dma_start(
                    out=V1[:, tok0 + sub * ntok:tok0 + (sub + 1) * ntok, :].rearrange("h t e -> t h e"),
                    in_=v1_st[0:ntok, sub, :, :]), "v1s", reads=[v1_st])
        p.emit()

    for part in ((1, 2) if 'B' in PASSES else ()):
        with ExitStack() as es:
            p = Prog(nc, es, "B%d" % part)
            alt = Alt()
            Wq = p.sb("Wq", [128, 16, 1536], BF16) if part == 1 else None
            Wg = p.sb("Wg", [128, 16, 1536], BF16) if part == 1 else None
            Wu = p.sb("Wu", [128, 16, 1024], BF16) if part == 2 else None
            xbufs = [p.sb("xb%d" % i, [128, D], F32) for i in range(2)]
            xns = [p.sb("xn%d" % i, [128, D], BF16) for i in range(2)]
            xnT = p.sb("xnT", [128, 16, 512], BF16)
            qT_st = p.sb("qTst", [128, H, 512], BF16) if part == 1 else None
            g_st = p.sb("gst", [128, 1536], F32) if part == 1 else None
            uh = p.sb("uh", [128, 4, 512], F32) if part == 2 else None
            ext = [p.sb("ext%d" % i, [128, 4, 4, 144], F32) for i in range(3)] if part == 2 else None
            dd = p.sb("dd", [128, 4, 512], F32) if part == 2 else None
            ddb = p.sb("ddb", [128, 4, 512], BF16) if part == 2 else None
            sgp = p.sb("sgp", [128, 4, 512], F32) if part == 2 else None
            mp_st = p.sb("mpst", [128, 4, 512], BF16) if part == 2 else None
            ytmp = p.sb("ytmp", [128, 512], F32) if part == 2 else None
            wp_f = p.sb("wpf", [128, 4, 128], F32) if part == 2 else None
            wp_b = p.sb("wpb", [128, 4, 128], BF16, ro=True) if part == 2 else None
            pscale = p.sb("pscale", [128, 4], F32, ro=True) if part == 2 else None
            invc = p.sb("invc", [128, 4, 16], F32, ro=True) if part == 2 else None
            hist = p.sb("hist", [128, 4, 2, 16], F32) if part == 2 else None
            up_st = p.sb("upst", [128, 4, 16], F32) if part == 2 else None
            normw = p.sb("normw", [128, D], F32, ro=True)
            ident = p.sb("ident", [128, 128], BF16, ro=True)
            ss = p.sb("ss", [128, 1], F32)
            rstd = p.sb("rstd", [128, 1], F32)
            tps = [[p.ps("tp%d%d" % (i, k), [128, 1024], BF16) for k in range(2)] for i in range(2)]
            accs = [p.ps("acc%d" % i, [128, 512], F32) for i in range(4)]
            acc_i = [0]

            def nacc():
                acc_i[0] += 1
                return accs[acc_i[0] % 4]

            p.dma("sp", lambda e: e.dma_start(out=normw[:, :], in_=normw_d[:, :]), "c0", writes=[normw])
            p.dma("sp", lambda e: e.dma_start(out=ident[:, :], in_=identb_d[:, :]), "c1", writes=[ident])
            if part == 2:
              p.dma("sp", lambda e: e.dma_start(out=pscale[:, :], in_=pscale_d[:, :]), "c2", writes=[pscale])
            if part == 2:
              p.dma("sp", lambda e: e.dma_start(out=invc[:, :, :], in_=invc_d[:, :, :]), "c3", writes=[invc])
            if part == 2:
              p.dma("sp", lambda e: e.dma_start(out=wp_f[:, :, :], in_=w_pool.rearrange("g c d -> c g d")), "c4", writes=[wp_f])
            if part == 2:
              p.grp("dve", lambda e: e.tensor_copy(wp_b[:, :, :], wp_f[:, :, :]), reads=[wp_f], writes=[wp_b])
            if part == 2:
              p.grp("pool", lambda e: e.memset(hist[:, :, :, :], 0.0), writes=[hist])
              p.grp("pool", lambda e: e.memset(ext[0][:, :, :, :], 0.0), writes=[ext[0]])
              p.grp("pool", lambda e: e.memset(ext[1][:, :, :, :], 0.0), writes=[ext[1]])
              p.grp("pool", lambda e: e.memset(ext[2][:, :, :, :], 0.0), writes=[ext[2]])
            for s_ in (range(2) if part == 2 else ()):
                for g in range(4):
                    p.dma("sp", lambda e, s_=s_, g=g: e.dma_start(
                        out=hist[:, g, s_, 1:16], in_=spool[s_, :, g * 128:(g + 1) * 128].rearrange("t c -> c t"),
                        allow_slow_non_contiguous=True), "c5", writes=[hist])
            if part == 1:
                load_weight_bf16(p, w_in, Q0, 1536, Wq, xbufs, alt)
                load_weight_bf16(p, w_in, G0, 1536, Wg, xbufs, alt)
            else:
                load_weight_bf16(p, w_in, U0, 1024, Wu, xbufs, alt)
            cnt = [0]
            for t in range(-1 if part == 2 else 0, NB + 1):
                halo, samp = (t == -1), (t == NB)
                nsub, ntok = (1, 48) if samp else (4, 128)
                tokw = nsub * ntok
                if halo:
                    nh = max(1, (NT * 16) // 128)
                    nsub = nh
                    tokw = nsub * 128
                    srcf = lambda sub: xh[sub * 8:(sub + 1) * 8, :, :].rearrange("a t d -> (a t) d")
                elif samp:
                    srcf = lambda sub: xs[:, :]
                else:
                    srcf = lambda sub, t=t: xa[4 * t + sub, 0, :, :]
                rmsnorm_T(p, srcf, nsub, ntok, xbufs, xns, normw, xnT, tps, ident, ss, rstd, alt, eps, cnt)
                if halo:
                    for g in range(4):
                        a = nacc()
                        fns = [_mm(a[:, 0:tokw], Wu[:, ch, g * 128:(g + 1) * 128], xnT[:, ch, 0:tokw], ch == 0, ch == 15) for ch in range(16)]
                        p.grp("pe", fns, reads=[Wu, xnT], writes=[a])
                        alt.copy(p, uh[:, g, 0:tokw], a[:, 0:tokw], reads=[a], writes=[uh])
                    continue
                qtok0 = NO if samp else t * 512
                for h in (range(H) if part == 1 else ()):
                    a = nacc()
                    fns = [_mm(a[:, 0:tokw], Wq[:, ch, h * 128:(h + 1) * 128], xnT[:, ch, 0:tokw], ch == 0, ch == 15) for ch in range(16)]
                    p.grp("pe", fns, reads=[Wq, xnT], writes=[a])
                    alt.copy(p, qT_st[:, h, 0:tokw], a[:, 0:tokw], reads=[a], writes=[qT_st])
                if part == 1:
                  p.dma("pool", lambda e, qtok0=qtok0, tokw=tokw: e.dma_start(
                    out=QT[:, :, qtok0:qtok0 + tokw].rearrange("h p t -> p h t"), in_=qT_st[:, :, 0:tokw]), "qts", reads=[qT_st])
                for sub in (range(nsub) if part == 1 else ()):
                    for cg in range(3):
                        a = nacc()
                        fns = [_mm(a[0:ntok, :], xnT[:, ch, sub * ntok:(sub + 1) * ntok], Wg[:, ch, cg * 512:(cg + 1) * 512], ch == 0, ch == 15)
                               for ch in range(16)]
                        p.grp("pe", fns, reads=[Wg, xnT], writes=[a])
                        p.grp("act", lambda e, a=a, cg=cg, ntok=ntok: e.activation(g_st[0:ntok, cg * 512:(cg + 1) * 512], a[0:ntok, :], AF.Silu),
                              reads=[a], writes=[g_st])
                    p.dma("sp", lambda e, sub=sub, qtok0=qtok0, ntok=ntok: e.dma_start(
                        out=GS[qtok0 + sub * ntok:qtok0 + (sub + 1) * ntok, :], in_=g_st[0:ntok, :]), "gs", reads=[g_st])
                if part == 1:
                    continue
                exA, exB, exC = ext
                for g in range(4):
                    a = nacc()
                    fns = [_mm(a[:, 0:tokw], Wu[:, ch, g * 128:(g + 1) * 128], xnT[:, ch, 0:tokw], ch == 0, ch == 15) for ch in range(16)]
                    p.grp("pe", fns, reads=[Wu, xnT], writes=[a])
                    if samp:
                        for s_ in range(2):
                            p.grp("dve", lambda e, a=a, g=g, s_=s_: e.tensor_copy(exA[:, g, s_, 16:32], a[:, 32 * s_:32 * s_ + 16]), reads=[a], writes=[exA])
                            p.grp("dve", lambda e, g=g, s_=s_: e.tensor_copy(exA[:, g, s_, 0:16], hist[:, g, s_, :]), reads=[hist], writes=[exA])
                    else:
                        p.grp("dve", lambda e, a=a, g=g: e.tensor_copy(exA[:, g, :, 16:144], a[:, :].rearrange("p (s t) -> p s t", s=4)), reads=[a], writes=[exA])
                        p.grp("pool", lambda e, g=g, t=t: e.tensor_copy(
                            exA[:, g, :, 0:16], uh[:, g, t * 64:(t + 1) * 64].rearrange("p (s t) -> p s t", s=4)), reads=[uh], writes=[exA])
                    a2 = nacc()
                    fns = [_mm(a2[:, 0:tokw], Wu[:, ch, 512 + g * 128:512 + (g + 1) * 128], xnT[:, ch, 0:tokw], ch == 0, ch == 15) for ch in range(16)]
                    p.grp("pe", fns, reads=[Wu, xnT], writes=[a2])
                    p.grp("act", lambda e, a2=a2, g=g, tokw=tokw: e.activation(sgp[:, g, 0:tokw], a2[:, 0:tokw], AF.Silu), reads=[a2], writes=[sgp])
                nsx, wx = (2, 32) if samp else (4, 144)
                for g in range(4):
                    cur = exA
                    st, si = 1, 0
                    while st < WINS[g]:
                        oth = (exB, exC)[si % 2]
                        si += 1
                        p.grp("dve", lambda e, cur=cur, oth=oth, st=st, g=g, nsx=nsx, wx=wx: e.tensor_tensor(
                            oth[:, g, 0:nsx, st:wx], cur[:, g, 0:nsx, st:wx], cur[:, g, 0:nsx, 0:wx - st], ALU.add), reads=[cur], writes=[oth])
                        cur = oth
                        st *= 2
                    if samp:
                        dview = dd[:, g, 0:64].rearrange("p (s t) -> p s t", s=2)[:, :, 0:16]
                    else:
                        dview = dd[:, g, :].rearrange("p (s t) -> p s t", s=4)
                    p.grp("dve", lambda e, cur=cur, g=g, dview=dview, nsx=nsx, wx=wx: e.scalar_tensor_tensor(
                        dview, cur[:, g, 0:nsx, 16:wx], 1.0 / WINS[g], exA[:, g, 0:nsx, 16:wx], ALU.mult, ALU.subtract), reads=[cur, exA], writes=[dd])
                    if t == 0:
                        p.grp("dve", lambda e, cur=cur, g=g: e.tensor_tensor(dd[:, g, 0:16], cur[:, g, 0, 16:32], invc[:, g, :], ALU.mult), reads=[cur, invc], writes=[dd])
                        p.grp("dve", lambda e, g=g: e.tensor_tensor(dd[:, g, 0:16], dd[:, g, 0:16], exA[:, g, 0, 16:32], ALU.subtract), reads=[dd, exA], writes=[dd])
                    if samp:
                        p.grp("pool", lambda e, g=g: e.memset(ddb[:, g, 0:64], 0.0), writes=[ddb])
                        p.grp("dve", lambda e, g=g, dview=dview: e.tensor_copy(ddb[:, g, 0:64].rearrange("p (s t) -> p s t", s=2)[:, :, 0:16], dview), reads=[dd], writes=[ddb])
                    else:
                        p.grp("dve", lambda e, g=g: e.tensor_copy(ddb[:, g, :], dd[:, g, :]), reads=[dd], writes=[ddb])
                    a = nacc()
                    p.grp("pe", [_mm(a[:, 0:tokw], wp_b[:, g, :], ddb[:, g, 0:tokw], True, True)], reads=[wp_b, ddb], writes=[a])
                    p.grp("dve", lambda e, a=a, g=g, tokw=tokw: e.scalar_tensor_tensor(
                        mp_st[:, g, 0:tokw], a[:, 0:tokw], pscale[:, g:g + 1], sgp[:, g, 0:tokw], ALU.mult, ALU.mult), reads=[a, sgp, pscale], writes=[mp_st])
                p.dma("pool", lambda e, qtok0=qtok0, tokw=tokw: e.dma_start(
                    out=MIXT[0:4, :, qtok0:qtok0 + tokw].rearrange("g p t -> p g t"), in_=mp_st[:, :, 0:tokw]), "mps", reads=[mp_st])
                if t == NB - 1:
                    p.grp("dve", lambda e: e.tensor_copy(up_st[:, :, 0:15], exA[:, :, 3, 129:144]), reads=[exA], writes=[up_st])
                    for g in range(4):
                        p.dma("pool", lambda e, g=g: e.dma_start(out=pool_own[:, g * 128:(g + 1) * 128].rearrange("t c -> c t"),
                                                                 in_=up_st[:, g, 0:15], allow_slow_non_contiguous=True), "ups", reads=[up_st])
                if samp:
                    for s_ in range(2):
                        for g in range(4):
                            p.dma("pool", lambda e, g=g, s_=s_: e.dma_start(out=pool_s[s_, :, g * 128:(g + 1) * 128].rearrange("t c -> c t"),
                                                                            in_=exA[:, g, s_, 17:32], allow_slow_non_contiguous=True), "ups2", reads=[exA])
            p.emit()

    with ExitStack() as es:
      if 'C' in PASSES:
        p = Prog(nc, es, "C")
        alt = Alt()
        KTh = p.sb("KTh", [128, S], BF16)
        V1h = p.sb("V1h", [128, NT * 4, 136], BF16)
        QTh = p.sb("QTh", [128, QTOK], BF16)
        Gh = p.sb("Gh", [128, NT, 128], F32)
        tfar = p.sb("tfar", [128, H, 512], F32, ro=True)
        tband = p.sb("tband", [128, H, 512], F32, ro=True)
        tsamp = p.sb("tsamp", [128, H, 256], F32, ro=True)
        tsnew = p.sb("tsnew", [16, H, 16], F32, ro=True)
        swl = p.sb("swl", [128, 128], F32, ro=True)
        lam = p.sb("lam", [128, 4, 64], F32)
        lamt = p.sb("lamt", [128, 2, 64], F32)
        lam2 = p.sb("lam2", [128, 2], F32)
        nlam = p.sb("nlam", [128, 1], F32, ro=True)
        identb = p.sb("identb", [128, 128], BF16, ro=True)
        identf = p.sb("identf", [128, 128], F32, ro=True)
        LOOKAHEAD = 4
        NSB = 6
        Sbs = [p.sb("Sb%d" % i, [128, 512], F32) for i in range(NSB)]
        Pts = [p.sb("Pt%d" % i, [128, 512], BF16) for i in range(NSB)]
        rr = p.sb("rr", [128, 2], F32)
        t2 = p.sb("t2", [128, 128], F32)
        oo = p.sb("oo", [128, 128], F32)
        junk = p.sb("junk", [128, 128], F32)
        ssq = p.sb("ssq", [128, 1], F32)
        rs2 = p.sb("rs2", [128, 1], F32)
        mixb = p.sb("mixb", [128, 128], BF16)
        mT_st = p.sb("mTst", [128, 128], BF16)
        kc = p.sb("kc", [128, 16, 128], F32)
        vcf = p.sb("vcf", [128, 16, 128], F32)
        KTc = p.sb("KTc", [128, 2048], BF16)
        V1c = p.sb("V1c", [128, 16, 136], BF16)
        KTn = p.sb("KTn", [128, 16], BF16)
        V1n = p.sb("V1n", [16, 136], BF16)
        Gs = p.sb("Gs", [16, 128], F32)
        Sbn = p.sb("Sbn", [16, 16], F32)
        Ptn = p.sb("Ptn", [16, 16], BF16)
        Sps = [p.ps("Sp%d" % i, [128, 512], F32) for i in range(4)]
        Obs = [p.ps("Ob%d" % i, [128, 512], F32) for i in range(2)]
        tpp = p.ps("tpp", [128, 1024], BF16)
        tpf = p.ps("tpf", [128, 512], F32)

        for i, (dst, src) in enumerate(((tfar, tfar_d), (tband, tband_d), (tsamp, tsamp_d), (tsnew, tsnew_d), (lam, lam_d))):
            p.dma("sp", lambda e, dst=dst, src=src: e.dma_start(out=dst[:, :, :], in_=src[:, :, :]), "c%d" % i, writes=[dst])
        for i, (dst, src) in enumerate(((swl, swl_d), (identb, identb_d), (identf, identf_d))):
            p.dma("sp", lambda e, dst=dst, src=src: e.dma_start(out=dst[:, :], in_=src[:, :]), "d%d" % i, writes=[dst])
        p.grp("dve", lambda e: e.tensor_tensor(lamt[:, 0, :], lam[:, 0, :], lam[:, 1, :], ALU.mult), reads=[lam], writes=[lamt])
        p.grp("dve", lambda e: e.tensor_tensor(lamt[:, 1, :], lam[:, 2, :], lam[:, 3, :], ALU.mult), reads=[lamt, lam], writes=[lamt])
        p.grp("dve", lambda e: e.tensor_reduce(lam2[:, :], lamt[:, :, :], mybir.AxisListType.X, ALU.add), reads=[lamt], writes=[lam2])
        p.grp("act", lambda e: e.activation(lam2[:, :], lam2[:, :], AF.Exp), reads=[lam2], writes=[lam2])
        p.grp("dve", lambda e: e.tensor_tensor(nlam[:, :], lam2[:, 1:2], lam2[:, 0:1], ALU.subtract), reads=[lam2], writes=[nlam])
        p.grp("dve", lambda e: e.tensor_scalar(nlam[:, :], nlam[:, :], -0.2, None, ALU.add), reads=[nlam], writes=[nlam])

        unit = [0]
        fcount = [0]

        def finalize(Ob, nq, Gap, mix_dst_fn):
            O1 = Ob[0][0:nq, 0:128]
            O2 = Ob[1][0:nq, 0:128]
            p.grp("dve", lambda e, nq=nq, Ob=Ob: e.reciprocal(rr[0:nq, 0:1], Ob[0][0:nq, 128:129]), reads=[Ob[0]], writes=[rr])
            p.grp("dve", lambda e, nq=nq, Ob=Ob: e.reciprocal(rr[0:nq, 1:2], Ob[1][0:nq, 128:129]), reads=[Ob[1], rr], writes=[rr])
            p.grp("dve", lambda e, nq=nq: e.tensor_tensor(rr[0:nq, 1:2], rr[0:nq, 1:2], nlam[0:nq, :], ALU.mult), reads=[rr, nlam], writes=[rr])
            p.grp("dve", lambda e, nq=nq, O2=O2: e.tensor_scalar(t2[0:nq, :], O2, rr[0:nq, 1:2], None, ALU.mult), reads=[Ob[1], rr], writes=[t2])
            p.grp("dve", lambda e, nq=nq, O1=O1: e.scalar_tensor_tensor(oo[0:nq, :], O1, rr[0:nq, 0:1], t2[0:nq, :], ALU.mult, ALU.add), reads=[Ob[0], rr, t2], writes=[oo])
            p.grp("dve", lambda e, nq=nq: e.scalar_tensor_tensor(junk[0:nq, :], oo[0:nq, :], 1.0, oo[0:nq, :], ALU.mult, ALU.mult, accum_out=ssq[0:nq, :]),
                  reads=[oo], writes=[junk, ssq])
            p.grp("act", lambda e, nq=nq: e.activation(rs2[0:nq, :], ssq[0:nq, :], AF.Ln, bias=1e-5, scale=1.0 / 128), reads=[ssq], writes=[rs2])
            p.grp("act", lambda e, nq=nq: e.activation(rs2[0:nq, :], rs2[0:nq, :], AF.Exp, scale=-0.5, bias=math.log(0.8)), reads=[rs2], writes=[rs2])
            p.grp("dve", lambda e, nq=nq: e.scalar_tensor_tensor(oo[0:nq, :], oo[0:nq, :], rs2[0:nq, 0:1], swl[0:nq, :], ALU.mult, ALU.mult),
                  reads=[oo, rs2, swl], writes=[oo])
            p.grp("dve", lambda e, nq=nq, Gap=Gap: e.tensor_tensor(mixb[0:nq, :], oo[0:nq, :], Gap, ALU.mult), reads=[oo, Gh, Gs], writes=[mixb])
            p.grp("pe", lambda e, nq=nq: e.transpose(tpp[:, 0:nq], mixb[0:nq, :], identb[0:nq, 0:nq]), reads=[mixb, identb], writes=[tpp])
            p.grp("act", lambda e, nq=nq: e.copy(mT_st[:, 0:nq], tpp[:, 0:nq]), reads=[tpp], writes=[mT_st])
            p.dma("pool", lambda e, nq=nq, mix_dst_fn=mix_dst_fn: e.dma_start(out=mix_dst_fn(), in_=mT_st[:, 0:nq]), "mts", reads=[mT_st])

        for h in range(H):
            sg = SIG[h]
            for qd in range(4):
                w = S // 4
                p.dma("sp", lambda e, h=h, qd=qd, w=w: e.dma_start(out=KTh[:, qd * w:(qd + 1) * w], in_=KT[h, :, qd * w:(qd + 1) * w]), "kth", writes=[KTh])
                nk = NT
                p.dma("sp", lambda e, h=h, qd=qd, nk=nk: e.dma_start(
                    out=V1h[:, qd * nk:(qd + 1) * nk, :], in_=V1[h, qd * nk * 128:(qd + 1) * nk * 128, :].rearrange("(k p) e -> p k e", p=128)),
                    "v1h", writes=[V1h])
            p.dma("sp", lambda e, h=h: e.dma_start(out=QTh[:, :], in_=QT[h, :, :]), "qth", writes=[QTh])
            p.dma("sp", lambda e, h=h: e.dma_start(out=Gh[:, :, :], in_=GS[0:NO, h * 128:(h + 1) * 128].rearrange("(q p) e -> p q e", p=128)), "gh", writes=[Gh])
            Ob = Obs
            allu = []
            for qs in range(NT):
                units = []
                for G in range(qs + 1):
                    if SKIP_THRESH is not None and G < qs and sg * 512 * (qs - G - 1) > SKIP_THRESH:
                        continue
                    for c in range(2):
                        units.append((G, c))
                firstc = {}
                lastc = {}
                for idx, (G, c) in enumerate(units):
                    firstc.setdefault(c, idx)
                    lastc[c] = idx
                for idx, (G, c) in enumerate(units):
                    allu.append((qs, G, c, firstc[c] == idx, lastc[c] == idx, idx == len(units) - 1))

            def stage1_pair(un0, un1, h=h, sg=sg):
                qs, G = un0[0], un0[1]
                assert un1[0] == qs and un1[1] == G and un0[2] == 0 and un1[2] == 1
                u0 = unit[0]
                unit[0] += 2
                bufs = [(Sps[(u0 + c) % 4], Sbs[(u0 + c) % NSB], Pts[(u0 + c) % NSB]) for c in range(2)]
                fns = []
                for m in range(4):
                    for c in range(2):
                        fns.append(_mm(bufs[c][0][:, m * 128:(m + 1) * 128], KTh[64 * c:64 * c + 64, (4 * G + m) * 128:(4 * G + m + 1) * 128],
                                       QTh[64 * c:64 * c + 64, qs * 128:(qs + 1) * 128], True, True))
                p.grp("pe", fns, reads=[KTh, QTh], writes=[bufs[0][0], bufs[1][0]])
                tbl = tband if G == qs else tfar
                bias = 0.0 if G == qs else -sg * 512.0 * (qs - G - 1)
                for c in range(2):
                    Sp, Sb, Pt = bufs[c]
                    p.grp("dve", lambda e, Sp=Sp, Sb=Sb, tbl=tbl, h=h: e.scalar_tensor_tensor(Sb[:, :], Sp[:, :], 0.125, tbl[:, h, :], ALU.mult, ALU.add),
                          reads=[Sp, tbl], writes=[Sb])
                    p.grp("act", lambda e, Sb=Sb, Pt=Pt, bias=bias: e.activation(Pt[:, :], Sb[:, :], AF.Exp, bias=bias, scale=1.0), reads=[Sb], writes=[Pt])
                return bufs[0][2], bufs[1][2]

            def stage1(un, h=h, sg=sg):
                qs, G, c, fst, lst, lastq = un
                u = unit[0]
                unit[0] += 1
                Sp, Sb, Pt = Sps[u % 4], Sbs[u % NSB], Pts[u % NSB]
                fns = [_mm(Sp[:, m * 128:(m + 1) * 128], KTh[64 * c:64 * c + 64, (4 * G + m) * 128:(4 * G + m + 1) * 128],
                           QTh[64 * c:64 * c + 64, qs * 128:(qs + 1) * 128], True, True) for m in range(4)]
                p.grp("pe", fns, reads=[KTh, QTh], writes=[Sp])
                tbl = tband if G == qs else tfar
                p.grp("dve", lambda e, Sp=Sp, Sb=Sb, tbl=tbl, h=h: e.scalar_tensor_tensor(Sb[:, :], Sp[:, :], 0.125, tbl[:, h, :], ALU.mult, ALU.add),
                      reads=[Sp, tbl], writes=[Sb])
                bias = 0.0 if G == qs else -sg * 512.0 * (qs - G - 1)
                p.grp("act", lambda e, Sb=Sb, Pt=Pt, bias=bias: e.activation(Pt[:, :], Sb[:, :], AF.Exp, bias=bias, scale=1.0), reads=[Sb], writes=[Pt])
                return Pt

            def stage2(un, Pt, h=h):
                qs, G, c, fst, lst, lastq = un
                fns = []
                for m in range(4):
                    fns.append(_mm(Ob[c][:, 0:129], Pt[:, m * 128:(m + 1) * 128], V1h[:, 4 * G + m, 0:129], fst and m == 0, lst and m == 3))
                p.grp("pe", fns, reads=[Pt, V1h], writes=[Ob[c]])
                if lastq:
                    finalize(Ob, 128, Gh[:, qs, :], lambda h=h, qs=qs: MIXT[4 + h, :, qs * 128:(qs + 1) * 128])

            pending = []
            assert len(allu) % 2 == 0
            for i in range(0, len(allu), 2):
                pt0, pt1 = stage1_pair(allu[i], allu[i + 1])
                pending.append((allu[i], pt0))
                pending.append((allu[i + 1], pt1))
                while len(pending) > LOOKAHEAD:
                    stage2(*pending.pop(0))
            while pending:
                stage2(*pending.pop(0))
            for s_ in range(2):
                p.dma("sp", lambda e, s_=s_, h=h: e.dma_start(out=kc[:, :, :], in_=ck[s_, :, h, :].rearrange("(k p) e -> p k e", p=128)), "kc", writes=[kc])
                p.dma("sp", lambda e, s_=s_, h=h: e.dma_start(out=vcf[:, :, :], in_=cv[s_, :, h, :].rearrange("(k p) e -> p k e", p=128)), "vc", writes=[vcf])
                p.dma("sp", lambda e, s_=s_, h=h: e.dma_start(out=KTn[:, :], in_=KT[h, :, S + 32 * s_:S + 32 * s_ + 16]), "ktn", writes=[KTn])
                p.dma("sp", lambda e, s_=s_, h=h: e.dma_start(out=V1n[:, :], in_=V1[h, S + 32 * s_:S + 32 * s_ + 16, :]), "v1n", writes=[V1n])
                p.dma("sp", lambda e, s_=s_, h=h: e.dma_start(out=Gs[:, :], in_=GS[NO + 32 * s_:NO + 32 * s_ + 16, h * 128:(h + 1) * 128]), "gsn", writes=[Gs])
                p.grp("pool", lambda e: e.memset(V1c[:, :, 128:136], 1.0), writes=[V1c])
                p.grp("pool", lambda e: e.tensor_copy(V1c[:, :, 0:128], vcf[:, :, :]), reads=[vcf], writes=[V1c])
                for k4 in range(4):
                    fns = [lambda e, k4=k4, m=m, Ob=Ob: e.transpose(tpf[:, m * 128:(m + 1) * 128], kc[:, 4 * k4 + m, :], identf[:, :]) for m in range(4)]
                    p.grp("pe", fns, reads=[kc, identf], writes=[tpf])
                    alt.copy(p, KTc[:, k4 * 512:(k4 + 1) * 512], tpf[:, :], reads=[tpf], writes=[KTc])
                Ob = Obs
                fcount[0] += 1
                qcol = NO + 32 * s_
                for c in range(2):
                    u = unit[0]
                    unit[0] += 1
                    Sp, Sb, Pt = Sps[u % 4], Sbs[u % NSB], Pts[u % NSB]
                    fns = [_mm(Sp[:, kt * 16:(kt + 1) * 16], KTc[64 * c:64 * c + 64, kt * 128:(kt + 1) * 128], QTh[64 * c:64 * c + 64, qcol:qcol + 16], True, True)
                           for kt in range(16)]
                    fns.append(_mm(Sp[0:16, 256:272], KTn[64 * c:64 * c + 64, :], QTh[64 * c:64 * c + 64, qcol:qcol + 16], True, True))
                    p.grp("pe", fns, reads=[KTc, KTn, QTh], writes=[Sp])
                    p.grp("dve", lambda e, Sp=Sp, Sb=Sb, h=h: e.scalar_tensor_tensor(Sb[:, 0:256], Sp[:, 0:256], 0.125, tsamp[:, h, :], ALU.mult, ALU.add),
                          reads=[Sp, tsamp], writes=[Sb])
                    p.grp("dve", lambda e, Sp=Sp, h=h: e.scalar_tensor_tensor(Sbn[:, :], Sp[0:16, 256:272], 0.125, tsnew[:, h, :], ALU.mult, ALU.add),
                          reads=[Sp, tsnew], writes=[Sbn])
                    p.grp("act", lambda e, Sb=Sb, Pt=Pt: e.activation(Pt[:, 0:256], Sb[:, 0:256], AF.Exp), reads=[Sb], writes=[Pt])
                    p.grp("act", lambda e: e.activation(Ptn[:, :], Sbn[:, :], AF.Exp), reads=[Sbn], writes=[Ptn])
                    fns = [_mm(Ob[c][0:16, 0:129], Pt[:, kt * 16:(kt + 1) * 16], V1c[:, kt, 0:129], kt == 0, False) for kt in range(16)]
                    fns.append(_mm(Ob[c][0:16, 0:129], Ptn[:, :], V1n[:, 0:129], False, True))
                    p.grp("pe", fns, reads=[Pt, Ptn, V1c, V1n], writes=[Ob[c]])
                finalize(Ob, 16, Gs[:, :], lambda h=h, qcol=qcol: MIXT[4 + h, :, qcol:qcol + 16])
        p.emit()

    with ExitStack() as es:
      if 'D' in PASSES:
        p = Prog(nc, es, "D")
        alt = Alt()
        Wo = p.sb("Wo", [128, 16, D], BF16)
        stg = [p.sb("stg%d" % i, [128, D], F32) for i in range(2)]
        mT = [p.sb("mT%d" % i, [128, 16, 128], BF16) for i in range(2)]
        xr = [p.sb("xr%d" % i, [128, D], F32) for i in range(2)]
        hb = [p.sb("hb%d" % i, [128, D], F32) for i in range(2)]
        junk = p.sb("junk", [128, D], BF16)
        finw = p.sb("finw", [128, D], F32, ro=True)
        ss = p.sb("ss", [128, 1], F32)
        rstd = p.sb("rstd", [128, 1], F32)
        accs = [p.ps("acc%d" % i, [128, 512], F32) for i in range(8)]
        p.dma("sp", lambda e: e.dma_start(out=finw[:, :], in_=finw_d[:, :]), "c0", writes=[finw])
        load_weight_bf16(p, w_out, 0, D, Wo, stg, alt)
        p.grp("pool", lambda e: e.memset(mT[0][:, :, :], 0.0), writes=[mT[0]])
        p.grp("pool", lambda e: e.memset(mT[1][:, :, :], 0.0), writes=[mT[1]])
        for t in range(NT + 1):
            samp = t == NT
            ntok = 48 if samp else 128
            m, x_, hh = mT[t % 2], xr[t % 2], hb[t % 2]
            q0 = NO if samp else t * 128
            for s_ in ((0, 1) if samp else (0,)):
                w = 16 if samp else 128
                p.dma("sp", lambda e, m=m, q0=q0, s_=s_, w=w: e.dma_start(
                    out=m[:, :, 32 * s_:32 * s_ + w], in_=MIXT[:, :, q0 + 32 * s_:q0 + 32 * s_ + w].rearrange("c p t -> p c t")), "mt%d" % (t % 2), writes=[m])
            xsrc = xs[:, :] if samp else xa[t, 0, :, :]
            p.dma("sp", lambda e, x_=x_, xsrc=xsrc, ntok=ntok: e.dma_start(out=x_[0:ntok, :], in_=xsrc), "xr%d" % (t % 2), writes=[x_])
            for cg in range(4):
                a = accs[(t % 2) * 4 + cg]
                fns = [_mm(a[0:ntok, :], m[:, ch, 0:ntok], Wo[:, ch, cg * 512:(cg + 1) * 512], ch == 0, ch == 15) for ch in range(16)]
                p.grp("pe", fns, reads=[m, Wo], writes=[a])
                p.grp("dve", lambda e, a=a, hh=hh, x_=x_, cg=cg, ntok=ntok: e.tensor_tensor(
                    hh[0:ntok, cg * 512:(cg + 1) * 512], a[0:ntok, :], x_[0:ntok, cg * 512:(cg + 1) * 512], ALU.add), reads=[a, x_], writes=[hh])
            p.grp("act", lambda e, hh=hh, ntok=ntok: e.activation(junk[0:ntok, :], hh[0:ntok, :], AF.Square, accum_out=ss[0:ntok, :]), reads=[hh], writes=[junk, ss])
            p.grp("act", lambda e, ntok=ntok: e.activation(rstd[0:ntok, :], ss[0:ntok, :], AF.Ln, bias=eps, scale=1.0 / D), reads=[ss], writes=[rstd])
            p.grp("act", lambda e, ntok=ntok: e.activation(rstd[0:ntok, :], rstd[0:ntok, :], AF.Exp, scale=-0.5), reads=[rstd], writes=[rstd])
            p.grp("dve", lambda e, hh=hh, ntok=ntok: e.scalar_tensor_tensor(hh[0:ntok, :], hh[0:ntok, :], rstd[0:ntok, 0:1], finw[0:ntok, :], ALU.mult, ALU.mult),
                  reads=[hh, rstd, finw], writes=[hh])
            dst = y_s[:, :] if samp else y_own[t, :, :]
            p.dma("sp", lambda e, dst=dst, hh=hh, ntok=ntok: e.dma_start(out=dst, in_=hh[0:ntok, :]), "yo%d" % (t % 2), reads=[hh])
        p.emit()
    return nc


_CACHE = {}


def kernel(x_prompt, x_sample, cache_k, cache_v, state_pool, norm_w, w_in, w_pool, pool_scale,
           lambda_q1, lambda_k1, lambda_q2, lambda_k2, subln_w, w_out, final_norm_w):
    f = np.float32
    x_prompt = np.asarray(x_prompt, f)
    x_sample = np.asarray(x_sample, f)
    B, S, _ = x_prompt.shape
    NT = S // 512
    NO = NT * 128
    if NT not in _CACHE:
        _CACHE[NT] = build(NT)
    nc = _CACHE[NT]
    sig = np.array(SIG, np.float64)
    identb = np.eye(128, dtype=np.float32).astype(ml_dtypes.bfloat16)
    identf = np.eye(128, dtype=f)
    bc = lambda v: np.ascontiguousarray(np.broadcast_to(np.asarray(v, f).reshape(1, -1), (128, np.asarray(v).size)))
    lam = np.stack([np.asarray(a, f)[0] for a in (lambda_q1, lambda_k1, lambda_q2, lambda_k2)])
    lam_b = np.ascontiguousarray(np.broadcast_to(lam[None], (128, 4, 64)))
    swl = bc(np.asarray(subln_w, f)[0])
    pscale = np.ascontiguousarray(np.asarray(pool_scale, f)[0].reshape(4, 128).T)
    p_ = np.arange(128)[:, None, None, None]
    q_ = np.arange(128)[None, None, None, :]
    hh_ = sig[None, :, None, None]
    kt_ = np.arange(16)[None, None, :, None]
    qq_ = np.arange(16)[None, None, None, :]
    tsamp = (-hh_ * (2048 + qq_ - 128 * kt_ - p_)).astype(f).reshape(128, H, 256)
    kk_ = np.arange(16)[:, None, None]
    tsnew = (-sig[None, :, None] * np.abs(np.arange(16)[None, None, :] - kk_)).astype(f)
    in_maps = []
    for c in range(8):
        b, j = c // 4, c % 4
        order = [j] + [m for m in range(4) if m != j]
        xb = x_prompt[b].reshape(NT, 4, 128, D)
        xa = np.ascontiguousarray(xb[:, order])
        xh = np.zeros((NT, 16, D), f)
        for T in range(NT):
            st = 512 * T + 128 * j
            if st >= 16:
                xh[T] = x_prompt[b, st - 16:st]
        xs = np.zeros((48, D), f)
        xs[0:16] = x_sample[2 * c]
        xs[32:48] = x_sample[2 * c + 1]
        om = np.array(order)[None, None, :, None]
        qpos = 512 + 128 * j + q_
        kpos = 128 * om + p_
        tfar = (-hh_ * (qpos - kpos)).astype(f).reshape(128, H, 512)
        qpos = 128 * j + q_
        allowed = (kpos // 64) <= (qpos // 64)
        tband = np.where(allowed, -hh_ * np.abs(qpos - kpos), NEG).astype(f).reshape(128, H, 512)
        invc = np.zeros((128, 4, 16), f)
        for g, w in enumerate(WINS):
            for t in range(16):
                invc[:, g, t] = 1.0 / (min(w, t + 1) if j == 0 else w)
        in_maps.append({
            "xa": xa, "xh": xh, "xs": xs,
            "ck": np.ascontiguousarray(np.asarray(cache_k, f)[0, 2 * c:2 * c + 2]),
            "cv": np.ascontiguousarray(np.asarray(cache_v, f)[0, 2 * c:2 * c + 2]),
            "spool": np.ascontiguousarray(np.asarray(state_pool, f)[0, 2 * c:2 * c + 2]),
            "w_in": np.ascontiguousarray(np.asarray(w_in, f)[0]), "w_out": np.ascontiguousarray(np.asarray(w_out, f)[0]),
            "w_pool": np.ascontiguousarray(np.asarray(w_pool, f)[0]),
            "normw": bc(np.asarray(norm_w, f)[0]), "finw": bc(np.asarray(final_norm_w, f)),
            "pscale": pscale, "lam": lam_b, "swl": swl, "tfar": tfar, "tband": tband, "tsamp": tsamp, "tsnew": tsnew,
            "invc": invc, "identb": identb, "identf": identf,
        })
    res = run_bass_kernel_spmd(nc, in_maps, core_ids=list(range(8))).results
    DB = x_sample.shape[0]
    y_prompt = np.zeros((B, S, D), f)
    k_prompt = np.zeros((1, B, S, H, 128), f)
    v_prompt = np.zeros((1, B, S, H, 128), f)
    pool_prompt = np.zeros((1, B, 15, 512), f)
    y_sample = np.zeros((DB, 16, D), f)
    k_sample = np.zeros((1, DB, 16, H, 128), f)
    v_sample = np.zeros((1, DB, 16, H, 128), f)
    pool_sample = np.zeros((1, DB, 15, 512), f)
    for c in range(8):
        b, j = c // 4, c % 4
        r = res[c]
        y_prompt[b].reshape(NT, 4, 128, D)[:, j] = r["y_own"]
        k_prompt[0, b].reshape(NT, 4, 128, 1536)[:, j] = r["k_own"]
        v_prompt[0, b].reshape(NT, 4, 128, 1536)[:, j] = r["v_own"]
        if j == 3:
            pool_prompt[0, b] = r["pool_own"]
        for s_ in range(2):
            y_sample[2 * c + s_] = r["y_s"][32 * s_:32 * s_ + 16]
            k_sample[0, 2 * c + s_] = r["k_s"][32 * s_:32 * s_ + 16].reshape(16, H, 128)
            v_sample[0, 2 * c + s_] = r["v_s"][32 * s_:32 * s_ + 16].reshape(16, H, 128)
            pool_sample[0, 2 * c + s_] = r["pool_s"][s_]
    return (y_prompt, y_sample, k_prompt, v_prompt, pool_prompt, k_sample, v_sample, pool_sample)
```

```python
import math
import os
from contextlib import ExitStack
import numpy as np
import ml_dtypes
import concourse.bass as bass
import concourse.mybir as mybir
from concourse.bass_utils import run_bass_kernel_spmd

F32 = mybir.dt.float32
BF16 = mybir.dt.bfloat16
AF = mybir.ActivationFunctionType
ALU = mybir.AluOpType

D = 2048
H = 12
U0, GP0, Q0, K0, V0, G0 = 0, 512, 1024, 2560, 4096, 5632
WINS = (2, 4, 8, 16)
NEG = -30000.0
SKIP_THRESH = None


def _slopes(n=12):
    def p2(m):
        st = 2.0 ** (-8.0 / m)
        return [st ** (i + 1) for i in range(m)]
    c = 2 ** int(math.floor(math.log2(n)))
    return p2(c) + p2(2 * c)[0::2][: n - c]


SIG = [float(np.float32(s)) for s in _slopes()]


class Buf:
    def __init__(self, t, ro=False):
        self.t = t
        self.wr = None
        self.rd = []
        self.ro = ro

    def __getitem__(self, k):
        return self.t[k]


class Prog:
    ENGS = (("pe", "tensor"), ("act", "scalar"), ("dve", "vector"), ("pool", "gpsimd"), ("sp", "sync"))

    def __init__(self, nc, es, tag):
        self.nc, self.es, self.tag = nc, es, tag
        self.ops = {e: [] for e, _ in self.ENGS}
        self.esem = {e: es.enter_context(nc.semaphore("%s_e_%s" % (tag, e))) for e in ("pe", "act", "dve", "pool")}
        self.ecnt = {e: 0 for e in self.esem}
        self.dsem, self.dcnt = {}, {}

    def sb(self, name, shape, dt, ro=False):
        return Buf(self.es.enter_context(self.nc.sbuf_tensor(self.tag + name, shape, dt)), ro)

    def ps(self, name, shape, dt=F32):
        return Buf(self.es.enter_context(self.nc.psum_tensor(self.tag + name, shape, dt)))

    def _deps(self, reads, writes):
        d = [b.wr for b in reads if b.wr is not None]
        for b in writes:
            d += b.rd
            if b.wr is not None:
                d.append(b.wr)
        return d

    def _upd(self, tok, reads, writes):
        for b in writes:
            b.wr = tok
            b.rd = []
        for b in reads:
            if not b.ro:
                b.rd.append(tok)

    def grp(self, eng, fns, reads=(), writes=()):
        if not isinstance(fns, (list, tuple)):
            fns = [fns]
        deps = self._deps(reads, writes)
        self.ecnt[eng] += 1
        tok = (self.esem[eng], self.ecnt[eng], eng)
        n = len(fns)
        for i, fn in enumerate(fns):
            self.ops[eng].append((fn, deps if i == 0 else [], tok if i == n - 1 else None, 1))
        self._upd(tok, reads, writes)
        return tok

    def dma(self, q, fn, sname, reads=(), writes=()):
        if sname not in self.dsem:
            self.dsem[sname] = self.es.enter_context(self.nc.semaphore("%s_d_%s" % (self.tag, sname)))
            self.dcnt[sname] = 0
        deps = self._deps(reads, writes)
        self.dcnt[sname] += 16
        tok = (self.dsem[sname], self.dcnt[sname], "dma")
        self.ops[q].append((fn, deps, tok, 16))
        self._upd(tok, reads, writes)
        return tok

    def emit(self):
        finals = [(self.esem[e], self.ecnt[e], e) for e in self.esem if self.ecnt[e] > 0]
        finals += [(self.dsem[s], self.dcnt[s], "dma") for s in self.dsem]
        for e, _ in self.ENGS:
            self.ops[e].append((None, [f for f in finals if f[2] != e], None, 0))
        with self.nc.Block() as block:
            for ename, attr in self.ENGS:
                def body(eng, ename=ename):
                    waited = {}
                    for fn, deps, tok, amt in self.ops[ename]:
                        mx = {}
                        for (s, v, src) in deps:
                            if src == "pe" and ename == "pe":
                                continue
                            k = id(s)
                            if k not in mx or mx[k][1] < v:
                                mx[k] = (s, v)
                        for k, (s, v) in mx.items():
                            if waited.get(k, 0) < v:
                                eng.wait_ge(s, v)
                                waited[k] = v
                        if fn is None:
                            continue
                        ins = fn(eng)
                        if tok is not None:
                            ins.then_inc(tok[0], amt)
                getattr(block, attr)(body)


def _mm(out, lhsT, rhs, start, stop):
    return lambda e: e.matmul(out, lhsT, rhs, start=start, stop=stop)


class Alt:
    def __init__(self):
        self.i = 0

    def copy(self, p, out, in_, reads, writes):
        self.i += 1
        if self.i % 2:
            return p.grp("act", lambda e: e.copy(out, in_), reads, writes)
        return p.grp("dve", lambda e: e.tensor_copy(out, in_), reads, writes)


def load_weight_bf16(p, io_w, col0, ncols, Wb, stg, alt, qeng="sp"):
    for ch in range(16):
        s = stg[ch % len(stg)]
        src = io_w[ch * 128:(ch + 1) * 128, col0:col0 + ncols]
        p.dma(qeng, lambda e, s=s, src=src: e.dma_start(out=s[:, 0:ncols], in_=src), "wst%d" % (ch % len(stg)), writes=[s])
        eng = "pool" if ch % 2 else "dve"
        p.grp(eng, lambda e, s=s, ch=ch: e.tensor_copy(Wb[:, ch, :], s[:, 0:ncols]), reads=[s], writes=[Wb])


def rmsnorm_T(p, xsrc_fn, nsub, ntok, xbufs, xns, normw, xnT, tps, ident, ss, rstd, alt, eps, cnt):
    for sub in range(nsub):
        i = cnt[0]
        cnt[0] += 1
        xb = xbufs[i % len(xbufs)]
        xn = xns[i % len(xns)]
        p.dma("sp", lambda e, xb=xb, sub=sub: e.dma_start(out=xb[0:ntok, :], in_=xsrc_fn(sub)), "x%d" % (i % len(xbufs)), writes=[xb])
        p.grp("act", lambda e, xb=xb, xn=xn: e.activation(xn[0:ntok, :], xb[0:ntok, :], AF.Square, accum_out=ss[0:ntok, :]),
              reads=[xb], writes=[xn, ss])
        p.grp("act", lambda e: e.activation(rstd[0:ntok, :], ss[0:ntok, :], AF.Ln, bias=eps, scale=1.0 / D), reads=[ss], writes=[rstd])
        p.grp("act", lambda e: e.activation(rstd[0:ntok, :], rstd[0:ntok, :], AF.Exp, scale=-0.5), reads=[rstd], writes=[rstd])
        p.grp("dve", lambda e, xb=xb, xn=xn: e.scalar_tensor_tensor(xn[0:ntok, :], xb[0:ntok, :], rstd[0:ntok, 0:1], normw[0:ntok, :], ALU.mult, ALU.mult),
              reads=[xb, rstd, normw], writes=[xn])
        tp = tps[i % len(tps)]
        fns = []
        for ch in range(16):
            fns.append(lambda e, tp=tp, xn=xn, ch=ch: e.transpose(tp[ch // 8][:, (ch % 8) * 128:(ch % 8) * 128 + ntok],
                                                                   xn[0:ntok, ch * 128:(ch + 1) * 128], ident[0:ntok, 0:ntok]))
        p.grp("pe", fns, reads=[xn, ident], writes=[tp[0], tp[1]])
        for hf in range(2):
            src = tp[hf][:, :].rearrange("p (c t) -> p c t", c=8)[:, :, 0:ntok]
            dst = xnT[:, hf * 8:(hf + 1) * 8, sub * ntok:(sub + 1) * ntok]
            alt.copy(p, dst, src, reads=[tp[hf]], writes=[xnT])


def build(NT):
    S = NT * 512
    NO = NT * 128
    NB = NT // 4
    KTOK = S + 48
    QTOK = NO + 48
    nc = bass.Bass("TRN2", target_bir_lowering=False)

    def din(name, shape, dt=F32):
        return nc.dram_tensor(name, list(shape), dt, kind="ExternalInput").ap()

    def dout(name, shape, dt=F32):
        return nc.dram_tensor(name, list(shape), dt, kind="ExternalOutput").ap()

    def dscr(name, shape, dt):
        return nc.dram_tensor(name, list(shape), dt).ap()

    xa = din("xa", [NT, 4, 128, D])
    xh = din("xh", [NT, 16, D])
    xs = din("xs", [48, D])
    ck = din("ck", [2, 2048, H, 128])
    cv = din("cv", [2, 2048, H, 128])
    spool = din("spool", [2, 15, 512])
    w_in = din("w_in", [D, 7168])
    w_out = din("w_out", [D, D])
    w_pool = din("w_pool", [4, 128, 128])
    normw_d = din("normw", [128, D])
    finw_d = din("finw", [128, D])
    pscale_d = din("pscale", [128, 4])
    lam_d = din("lam", [128, 4, 64])
    swl_d = din("swl", [128, 128])
    tfar_d = din("tfar", [128, H, 512])
    tband_d = din("tband", [128, H, 512])
    tsamp_d = din("tsamp", [128, H, 256])
    tsnew_d = din("tsnew", [16, H, 16])
    invc_d = din("invc", [128, 4, 16])
    identb_d = din("identb", [128, 128], BF16)
    identf_d = din("identf", [128, 128])

    y_own = dout("y_own", [NT, 128, D])
    y_s = dout("y_s", [48, D])
    k_own = dout("k_own", [NT, 128, 1536])
    v_own = dout("v_own", [NT, 128, 1536])
    pool_own = dout("pool_own", [15, 512])
    k_s = dout("k_s", [48, 1536])
    v_s = dout("v_s", [48, 1536])
    pool_s = dout("pool_s", [2, 15, 512])

    KT = dscr("KT_scr", [H, 128, KTOK], BF16)
    V1 = dscr("V1_scr", [H, KTOK, 136], BF16)
    QT = dscr("QT_scr", [H, 128, QTOK], BF16)
    GS = dscr("G_scr", [QTOK, 1536], F32)
    MIXT = dscr("MIXT_scr", [16, 128, QTOK], BF16)

    eps = 1e-6

    PASSES = os.environ.get('DBG_PASSES', 'ABCD')
    with ExitStack() as es:
      if 'A' in PASSES:
        p = Prog(nc, es, "A")
        alt = Alt()
        Wk = p.sb("Wk", [128, 16, 1536], BF16)
        Wv = p.sb("Wv", [128, 16, 1536], BF16)
        xbufs = [p.sb("xb%d" % i, [128, D], F32) for i in range(2)]
        xns = [p.sb("xn%d" % i, [128, D], BF16) for i in range(4)]
        xnT = p.sb("xnT", [128, 16, 512], BF16)
        kT_st = p.sb("kTst", [128, H, 512], BF16)
        v1_st = p.sb("v1st", [128, 4, H, 136], BF16)
        fo_st = [p.sb("fost%d" % i, [128, 1536], F32) for i in range(2)]
        normw = p.sb("normw", [128, D], F32, ro=True)
        ident = p.sb("ident", [128, 128], BF16, ro=True)
        ss = p.sb("ss", [128, 1], F32)
        rstd = p.sb("rstd", [128, 1], F32)
        tps = [[p.ps("tp%d%d" % (i, k), [128, 1024], BF16) for k in range(2)] for i in range(2)]
        accs = [p.ps("acc%d" % i, [128, 512], F32) for i in range(4)]
        acc_i = [0]

        def nacc():
            acc_i[0] += 1
            return accs[acc_i[0] % 4]

        p.dma("sp", lambda e: e.dma_start(out=normw[:, :], in_=normw_d[:, :]), "c0", writes=[normw])
        p.dma("sp", lambda e: e.dma_start(out=ident[:, :], in_=identb_d[:, :]), "c1", writes=[ident])
        p.grp("pool", lambda e: e.memset(v1_st[:, :, :, 128:136], 1.0), writes=[v1_st])
        load_weight_bf16(p, w_in, K0, 1536, Wk, xbufs, alt)
        load_weight_bf16(p, w_in, V0, 1536, Wv, xbufs, alt)
        cnt = [0]
        LV = int(os.environ.get('DBG_LV', '9'))
        FO = os.environ.get('DBG_FO', 'ad')
        for T in (range(NT + (0 if os.environ.get('DBG_NOSAMP') else 1)) if LV >= 2 else ()):
            samp = (T == NT)
            nsub, ntok = (1, 48) if samp else (4, 128)
            tokw = nsub * ntok
            tok0 = S if samp else T * 512
            if samp:
                srcf = lambda sub: xs[:, :]
            else:
                srcf = lambda sub, T=T: xa[T, sub, :, :]
            rmsnorm_T(p, srcf, nsub, ntok, xbufs, xns, normw, xnT, tps, ident, ss, rstd, alt, eps, cnt)
            if LV < 3:
                continue
            for h in range(H):
                a = nacc()
                fns = [_mm(a[:, 0:tokw], Wk[:, ch, h * 128:(h + 1) * 128], xnT[:, ch, 0:tokw], ch == 0, ch == 15) for ch in range(16)]
                p.grp("pe", fns, reads=[Wk, xnT], writes=[a])
                alt.copy(p, kT_st[:, h, 0:tokw], a[:, 0:tokw], reads=[a], writes=[kT_st])
            p.dma("pool", lambda e, tok0=tok0, tokw=tokw: e.dma_start(
                out=KT[:, :, tok0:tok0 + tokw].rearrange("h p t -> p h t"), in_=kT_st[:, :, 0:tokw]), "kts", reads=[kT_st])
            if LV < 4:
                continue
            for sub in range(nsub):
                own = samp or sub == 0
                for which in ((0, 1) if own else (0,)):
                    W = Wv if which == 0 else Wk
                    fo = fo_st[which]
                    for cg in range(3):
                        a = nacc()
                        fns = [_mm(a[0:ntok, :], xnT[:, ch, sub * ntok:(sub + 1) * ntok], W[:, ch, cg * 512:(cg + 1) * 512], ch == 0, ch == 15)
                               for ch in range(16)]
                        p.grp("pe", fns, reads=[W, xnT], writes=[a])
                        if which == 0:
                            p.grp("dve", lambda e, a=a, sub=sub, cg=cg, ntok=ntok: e.tensor_copy(
                                v1_st[0:ntok, sub, cg * 4:(cg + 1) * 4, 0:128], a[0:ntok, :].rearrange("p (h e) -> p h e", h=4)),
                                reads=[a], writes=[v1_st])
                        if own and LV >= 5 and 'a' in FO:
                            if os.environ.get('DBG_FOENG', 'dve') == 'act':
                                p.grp("act", lambda e, a=a, fo=fo, cg=cg, ntok=ntok: e.copy(fo[0:ntok, cg * 512:(cg + 1) * 512], a[0:ntok, :]),
                                      reads=[a], writes=[fo])
                            else:
                                p.grp("dve", lambda e, a=a, fo=fo, cg=cg, ntok=ntok: e.tensor_copy(fo[0:ntok, cg * 512:(cg + 1) * 512], a[0:ntok, :]),
                                      reads=[a], writes=[fo])
                    if own and LV >= 5 and 'd' in FO:
                        if samp:
                            dst = (v_s if which == 0 else k_s)[:, :]
                        else:
                            dst = (v_own if which == 0 else k_own)[T, :, :]
                        p.dma("sp", lambda e, dst=dst, fo=fo, ntok=ntok: e.dma_start(out=dst, in_=fo[0:ntok, :]), "fo%d" % which, reads=[fo])
            for sub in (range(nsub) if LV >= 6 else ()):
                p.dma("pool", lambda e, sub=sub, tok0=tok0, ntok=ntok: e.dma_start(
                    out=V1[:, tok0 + sub * ntok:tok0 + (sub + 1) * ntok, :].rearrange("h t e -> t h e"),
                    in_=v1_st[0:ntok, sub, :, :]), "v1s", reads=[v1_st])
        p.emit()

    for part in ((1, 2) if 'B' in PASSES else ()):
        with ExitStack() as es:
            p = Prog(nc, es, "B%d" % part)
            alt = Alt()
            Wq = p.sb("Wq", [128, 16, 1536], BF16) if part == 1 else None
            Wg = p.sb("Wg", [128, 16, 1536], BF16) if part == 1 else None
            Wu = p.sb("Wu", [128, 16, 1024], BF16) if part == 2 else None
            xbufs = [p.sb("xb%d" % i, [128, D], F32) for i in range(2)]
            xns = [p.sb("xn%d" % i, [128, D], BF16) for i in range(4)]
            xnT = p.sb("xnT", [128, 16, 512], BF16)
            qT_st = p.sb("qTst", [128, H, 512], BF16) if part == 1 else None
            g_st = p.sb("gst", [128, 1536], F32) if part == 1 else None
            uh = p.sb("uh", [128, 4, 512], F32) if part == 2 else None
            ext = [p.sb("ext%d" % i, [128, 4, 4, 144], F32) for i in range(3)] if part == 2 else None
            dd = p.sb("dd", [128, 4, 512], F32) if part == 2 else None
            ddb = p.sb("ddb", [128, 4, 512], BF16) if part == 2 else None
            sgp = p.sb("sgp", [128, 4, 512], F32) if part == 2 else None
            mp_st = p.sb("mpst", [128, 4, 512], BF16) if part == 2 else None
            ytmp = p.sb("ytmp", [128, 512], F32) if part == 2 else None
            wp_f = p.sb("wpf", [128, 4, 128], F32) if part == 2 else None
            wp_b = p.sb("wpb", [128, 4, 128], BF16, ro=True) if part == 2 else None
            pscale = p.sb("pscale", [128, 4], F32, ro=True) if part == 2 else None
            invc = p.sb("invc", [128, 4, 16], F32, ro=True) if part == 2 else None
            hist = p.sb("hist", [128, 4, 2, 16], F32) if part == 2 else None
            up_st = p.sb("upst", [128, 4, 16], F32) if part == 2 else None
            normw = p.sb("normw", [128, D], F32, ro=True)
            ident = p.sb("ident", [128, 128], BF16, ro=True)
            ss = p.sb("ss", [128, 1], F32)
            rstd = p.sb("rstd", [128, 1], F32)
            tps = [[p.ps("tp%d%d" % (i, k), [128, 1024], BF16) for k in range(2)] for i in range(2)]
            accs = [p.ps("acc%d" % i, [128, 512], F32) for i in range(4)]
            acc_i = [0]

            def nacc():
                acc_i[0] += 1
                return accs[acc_i[0] % 4]

            p.dma("sp", lambda e: e.dma_start(out=normw[:, :], in_=normw_d[:, :]), "c0", writes=[normw])
            p.dma("sp", lambda e: e.dma_start(out=ident[:, :], in_=identb_d[:, :]), "c1", writes=[ident])
            if part == 2:
              p.dma("sp", lambda e: e.dma_start(out=pscale[:, :], in_=pscale_d[:, :]), "c2", writes=[pscale])
            if part == 2:
              p.dma("sp", lambda e: e.dma_start(out=invc[:, :, :], in_=invc_d[:, :, :]), "c3", writes=[invc])
            if part == 2:
              p.dma("sp", lambda e: e.dma_start(out=wp_f[:, :, :], in_=w_pool.rearrange("g c d -> c g d")), "c4", writes=[wp_f])
            if part == 2:
              p.grp("dve", lambda e: e.tensor_copy(wp_b[:, :, :], wp_f[:, :, :]), reads=[wp_f], writes=[wp_b])
            if part == 2:
              p.grp("pool", lambda e: e.memset(hist[:, :, :, :], 0.0), writes=[hist])
              p.grp("pool", lambda e: e.memset(ext[0][:, :, :, :], 0.0), writes=[ext[0]])
              p.grp("pool", lambda e: e.memset(ext[1][:, :, :, :], 0.0), writes=[ext[1]])
              p.grp("pool", lambda e: e.memset(ext[2][:, :, :, :], 0.0), writes=[ext[2]])
            for s_ in (range(2) if part == 2 else ()):
                for g in range(4):
                    p.dma("sp", lambda e, s_=s_, g=g: e.dma_start(
                        out=hist[:, g, s_, 1:16], in_=spool[s_, :, g * 128:(g + 1) * 128].rearrange("t c -> c t"),
                        allow_slow_non_contiguous=True), "c5", writes=[hist])
            if part == 1:
                load_weight_bf16(p, w_in, Q0, 1536, Wq, xbufs, alt)
                load_weight_bf16(p, w_in, G0, 1536, Wg, xbufs, alt)
            else:
                load_weight_bf16(p, w_in, U0, 1024, Wu, xbufs, alt)
            cnt = [0]
            for t in range(-1 if part == 2 else 0, NB + 1):
                halo, samp = (t == -1), (t == NB)
                nsub, ntok = (1, 48) if samp else (4, 128)
                tokw = nsub * ntok
                if halo:
                    nh = max(1, (NT * 16) // 128)
                    nsub = nh
                    tokw = nsub * 128
                    srcf = lambda sub: xh[sub * 8:(sub + 1) * 8, :, :].rearrange("a t d -> (a t) d")
                elif samp:
                    srcf = lambda sub: xs[:, :]
                else:
                    srcf = lambda sub, t=t: xa[4 * t + sub, 0, :, :]
                rmsnorm_T(p, srcf, nsub, ntok, xbufs, xns, normw, xnT, tps, ident, ss, rstd, alt, eps, cnt)
                if halo:
                    for g in range(4):
                        a = nacc()
                        fns = [_mm(a[:, 0:tokw], Wu[:, ch, g * 128:(g + 1) * 128], xnT[:, ch, 0:tokw], ch == 0, ch == 15) for ch in range(16)]
                        p.grp("pe", fns, reads=[Wu, xnT], writes=[a])
                        alt.copy(p, uh[:, g, 0:tokw], a[:, 0:tokw], reads=[a], writes=[uh])
                    continue
                qtok0 = NO if samp else t * 512
                for h in (range(H) if part == 1 else ()):
                    a = nacc()
                    fns = [_mm(a[:, 0:tokw], Wq[:, ch, h * 128:(h + 1) * 128], xnT[:, ch, 0:tokw], ch == 0, ch == 15) for ch in range(16)]
                    p.grp("pe", fns, reads=[Wq, xnT], writes=[a])
                    alt.copy(p, qT_st[:, h, 0:tokw], a[:, 0:tokw], reads=[a], writes=[qT_st])
                if part == 1:
                  p.dma("pool", lambda e, qtok0=qtok0, tokw=tokw: e.dma_start(
                    out=QT[:, :, qtok0:qtok0 + tokw].rearrange("h p t -> p h t"), in_=qT_st[:, :, 0:tokw]), "qts", reads=[qT_st])
                for sub in (range(nsub) if part == 1 else ()):
                    for cg in range(3):
                        a = nacc()
                        fns = [_mm(a[0:ntok, :], xnT[:, ch, sub * ntok:(sub + 1) * ntok], Wg[:, ch, cg * 512:(cg + 1) * 512], ch == 0, ch == 15)
                               for ch in range(16)]
                        p.grp("pe", fns, reads=[Wg, xnT], writes=[a])
                        p.grp("act", lambda e, a=a, cg=cg, ntok=ntok: e.activation(g_st[0:ntok, cg * 512:(cg + 1) * 512], a[0:ntok, :], AF.Silu),
                              reads=[a], writes=[g_st])
                    p.dma("sp", lambda e, sub=sub, qtok0=qtok0, ntok=ntok: e.dma_start(
                        out=GS[qtok0 + sub * ntok:qtok0 + (sub + 1) * ntok, :], in_=g_st[0:ntok, :]), "gs", reads=[g_st])
                if part == 1:
                    continue
                exA, exB, exC = ext
                for g in range(4):
                    a = nacc()
                    fns = [_mm(a[:, 0:tokw], Wu[:, ch, g * 128:(g + 1) * 128], xnT[:, ch, 0:tokw], ch == 0, ch == 15) for ch in range(16)]
                    p.grp("pe", fns, reads=[Wu, xnT], writes=[a])
                    if samp:
                        for s_ in range(2):
                            p.grp("dve", lambda e, a=a, g=g, s_=s_: e.tensor_copy(exA[:, g, s_, 16:32], a[:, 32 * s_:32 * s_ + 16]), reads=[a], writes=[exA])
                            p.grp("dve", lambda e, g=g, s_=s_: e.tensor_copy(exA[:, g, s_, 0:16], hist[:, g, s_, :]), reads=[hist], writes=[exA])
                    else:
                        p.grp("dve", lambda e, a=a, g=g: e.tensor_copy(exA[:, g, :, 16:144], a[:, :].rearrange("p (s t) -> p s t", s=4)), reads=[a], writes=[exA])
                        p.grp("pool", lambda e, g=g, t=t: e.tensor_copy(
                            exA[:, g, :, 0:16], uh[:, g, t * 64:(t + 1) * 64].rearrange("p (s t) -> p s t", s=4)), reads=[uh], writes=[exA])
                    a2 = nacc()
                    fns = [_mm(a2[:, 0:tokw], Wu[:, ch, 512 + g * 128:512 + (g + 1) * 128], xnT[:, ch, 0:tokw], ch == 0, ch == 15) for ch in range(16)]
                    p.grp("pe", fns, reads=[Wu, xnT], writes=[a2])
                    p.grp("act", lambda e, a2=a2, g=g, tokw=tokw: e.activation(sgp[:, g, 0:tokw], a2[:, 0:tokw], AF.Silu), reads=[a2], writes=[sgp])
                nsx, wx = (2, 32) if samp else (4, 144)
                for g in range(4):
                    cur = exA
                    st, si = 1, 0
                    while st < WINS[g]:
                        oth = (exB, exC)[si % 2]
                        si += 1
                        p.grp("dve", lambda e, cur=cur, oth=oth, st=st, g=g, nsx=nsx, wx=wx: e.tensor_tensor(
                            oth[:, g, 0:nsx, st:wx], cur[:, g, 0:nsx, st:wx], cur[:, g, 0:nsx, 0:wx - st], ALU.add), reads=[cur], writes=[oth])
                        cur = oth
                        st *= 2
                    if samp:
                        dview = dd[:, g, 0:64].rearrange("p (s t) -> p s t", s=2)[:, :, 0:16]
                    else:
                        dview = dd[:, g, :].rearrange("p (s t) -> p s t", s=4)
                    p.grp("dve", lambda e, cur=cur, g=g, dview=dview, nsx=nsx, wx=wx: e.scalar_tensor_tensor(
                        dview, cur[:, g, 0:nsx, 16:wx], 1.0 / WINS[g], exA[:, g, 0:nsx, 16:wx], ALU.mult, ALU.subtract), reads=[cur, exA], writes=[dd])
                    if t == 0:
                        p.grp("dve", lambda e, cur=cur, g=g: e.tensor_tensor(dd[:, g, 0:16], cur[:, g, 0, 16:32], invc[:, g, :], ALU.mult), reads=[cur, invc], writes=[dd])
                        p.grp("dve", lambda e, g=g: e.tensor_tensor(dd[:, g, 0:16], dd[:, g, 0:16], exA[:, g, 0, 16:32], ALU.subtract), reads=[dd, exA], writes=[dd])
                    if samp:
                        p.grp("pool", lambda e, g=g: e.memset(ddb[:, g, 0:64], 0.0), writes=[ddb])
                        p.grp("dve", lambda e, g=g, dview=dview: e.tensor_copy(ddb[:, g, 0:64].rearrange("p (s t) -> p s t", s=2)[:, :, 0:16], dview), reads=[dd], writes=[ddb])
                    else:
                        p.grp("dve", lambda e, g=g: e.tensor_copy(ddb[:, g, :], dd[:, g, :]), reads=[dd], writes=[ddb])
                    a = nacc()
                    p.grp("pe", [_mm(a[:, 0:tokw], wp_b[:, g, :], ddb[:, g, 0:tokw], True, True)], reads=[wp_b, ddb], writes=[a])
                    p.grp("dve", lambda e, a=a, g=g, tokw=tokw: e.scalar_tensor_tensor(
                        mp_st[:, g, 0:tokw], a[:, 0:tokw], pscale[:, g:g + 1], sgp[:, g, 0:tokw], ALU.mult, ALU.mult), reads=[a, sgp, pscale], writes=[mp_st])
                p.dma("pool", lambda e, qtok0=qtok0, tokw=tokw: e.dma_start(
                    out=MIXT[0:4, :, qtok0:qtok0 + tokw].rearrange("g p t -> p g t"), in_=mp_st[:, :, 0:tokw]), "mps", reads=[mp_st])
                if t == NB - 1:
                    p.grp("dve", lambda e: e.tensor_copy(up_st[:, :, 0:15], exA[:, :, 3, 129:144]), reads=[exA], writes=[up_st])
                    for g in range(4):
                        p.dma("pool", lambda e, g=g: e.dma_start(out=pool_own[:, g * 128:(g + 1) * 128].rearrange("t c -> c t"),
                                                                 in_=up_st[:, g, 0:15], allow_slow_non_contiguous=True), "ups", reads=[up_st])
                if samp:
                    for s_ in range(2):
                        for g in range(4):
                            p.dma("pool", lambda e, g=g, s_=s_: e.dma_start(out=pool_s[s_, :, g * 128:(g + 1) * 128].rearrange("t c -> c t"),
                                                                            in_=exA[:, g, s_, 17:32], allow_slow_non_contiguous=True), "ups2", reads=[exA])
            p.emit()

    with ExitStack() as es:
      if 'C' in PASSES:
        p = Prog(nc, es, "C")
        alt = Alt()
        KTh = p.sb("KTh", [128, S], BF16)
        V1h = p.sb("V1h", [128, NT * 4, 136], BF16)
        QTh = p.sb("QTh", [128, QTOK], BF16)
        Gh = p.sb("Gh", [128, NT, 128], F32)
        tfar = p.sb("tfar", [128, H, 512], F32, ro=True)
        tband = p.sb("tband", [128, H, 512], F32, ro=True)
        tsamp = p.sb("tsamp", [128, H, 256], F32, ro=True)
        tsnew = p.sb("tsnew", [16, H, 16], F32, ro=True)
        swl = p.sb("swl", [128, 128], F32, ro=True)
        lam = p.sb("lam", [128, 4, 64], F32)
        lamt = p.sb("lamt", [128, 2, 64], F32)
        lam2 = p.sb("lam2", [128, 2], F32)
        nlam = p.sb("nlam", [128, 1], F32, ro=True)
        identb = p.sb("identb", [128, 128], BF16, ro=True)
        identf = p.sb("identf", [128, 128], F32, ro=True)
        LOOKAHEAD = 4
        NSB = 6
        Sbs = [p.sb("Sb%d" % i, [128, 512], F32) for i in range(NSB)]
        Pts = [p.sb("Pt%d" % i, [128, 512], BF16) for i in range(NSB)]
        rr = p.sb("rr", [128, 2], F32)
        t2 = p.sb("t2", [128, 128], F32)
        oo = p.sb("oo", [128, 128], F32)
        junk = p.sb("junk", [128, 128], F32)
        ssq = p.sb("ssq", [128, 1], F32)
        rs2 = p.sb("rs2", [128, 1], F32)
        mixb = p.sb("mixb", [128, 128], BF16)
        mT_st = p.sb("mTst", [128, 128], BF16)
        kc = p.sb("kc", [128, 16, 128], F32)
        vcf = p.sb("vcf", [128, 16, 128], F32)
        KTc = p.sb("KTc", [128, 2048], BF16)
        V1c = p.sb("V1c", [128, 16, 136], BF16)
        KTn = p.sb("KTn", [128, 16], BF16)
        V1n = p.sb("V1n", [16, 136], BF16)
        Gs = p.sb("Gs", [16, 128], F32)
        Sbn = p.sb("Sbn", [16, 16], F32)
        Ptn = p.sb("Ptn", [16, 16], BF16)
        Sps = [p.ps("Sp%d" % i, [128, 512], F32) for i in range(4)]
        Obs = [p.ps("Ob%d" % i, [128, 512], F32) for i in range(2)]
        tpp = p.ps("tpp", [128, 1024], BF16)
        tpf = p.ps("tpf", [128, 512], F32)

        for i, (dst, src) in enumerate(((tfar, tfar_d), (tband, tband_d), (tsamp, tsamp_d), (tsnew, tsnew_d), (lam, lam_d))):
            p.dma("sp", lambda e, dst=dst, src=src: e.dma_start(out=dst[:, :, :], in_=src[:, :, :]), "c%d" % i, writes=[dst])
        for i, (dst, src) in enumerate(((swl, swl_d), (identb, identb_d), (identf, identf_d))):
            p.dma("sp", lambda e, dst=dst, src=src: e.dma_start(out=dst[:, :], in_=src[:, :]), "d%d" % i, writes=[dst])
        p.grp("dve", lambda e: e.tensor_tensor(lamt[:, 0, :], lam[:, 0, :], lam[:, 1, :], ALU.mult), reads=[lam], writes=[lamt])
        p.grp("dve", lambda e: e.tensor_tensor(lamt[:, 1, :], lam[:, 2, :], lam[:, 3, :], ALU.mult), reads=[lamt, lam], writes=[lamt])
        p.grp("dve", lambda e: e.tensor_reduce(lam2[:, :], lamt[:, :, :], mybir.AxisListType.X, ALU.add), reads=[lamt], writes=[lam2])
        p.grp("act", lambda e: e.activation(lam2[:, :], lam2[:, :], AF.Exp), reads=[lam2], writes=[lam2])
        p.grp("dve", lambda e: e.tensor_tensor(nlam[:, :], lam2[:, 1:2], lam2[:, 0:1], ALU.subtract), reads=[lam2], writes=[nlam])
        p.grp("dve", lambda e: e.tensor_scalar(nlam[:, :], nlam[:, :], -0.2, None, ALU.add), reads=[nlam], writes=[nlam])

        unit = [0]
        fcount = [0]

        def finalize(Ob, nq, Gap, mix_dst_fn):
            O1 = Ob[0][0:nq, 0:128]
            O2 = Ob[1][0:nq, 0:128]
            p.grp("dve", lambda e, nq=nq, Ob=Ob: e.reciprocal(rr[0:nq, 0:1], Ob[0][0:nq, 128:129]), reads=[Ob[0]], writes=[rr])
            p.grp("dve", lambda e, nq=nq, Ob=Ob: e.reciprocal(rr[0:nq, 1:2], Ob[1][0:nq, 128:129]), reads=[Ob[1], rr], writes=[rr])
            p.grp("dve", lambda e, nq=nq: e.tensor_tensor(rr[0:nq, 1:2], rr[0:nq, 1:2], nlam[0:nq, :], ALU.mult), reads=[rr, nlam], writes=[rr])
            p.grp("dve", lambda e, nq=nq, O2=O2: e.tensor_scalar(t2[0:nq, :], O2, rr[0:nq, 1:2], None, ALU.mult), reads=[Ob[1], rr], writes=[t2])
            p.grp("dve", lambda e, nq=nq, O1=O1: e.scalar_tensor_tensor(oo[0:nq, :], O1, rr[0:nq, 0:1], t2[0:nq, :], ALU.mult, ALU.add), reads=[Ob[0], rr, t2], writes=[oo])
            p.grp("dve", lambda e, nq=nq: e.scalar_tensor_tensor(junk[0:nq, :], oo[0:nq, :], 1.0, oo[0:nq, :], ALU.mult, ALU.mult, accum_out=ssq[0:nq, :]),
                  reads=[oo], writes=[junk, ssq])
            p.grp("act", lambda e, nq=nq: e.activation(rs2[0:nq, :], ssq[0:nq, :], AF.Ln, bias=1e-5, scale=1.0 / 128), reads=[ssq], writes=[rs2])
            p.grp("act", lambda e, nq=nq: e.activation(rs2[0:nq, :], rs2[0:nq, :], AF.Exp, scale=-0.5, bias=math.log(0.8)), reads=[rs2], writes=[rs2])
            p.grp("dve", lambda e, nq=nq: e.scalar_tensor_tensor(oo[0:nq, :], oo[0:nq, :], rs2[0:nq, 0:1], swl[0:nq, :], ALU.mult, ALU.mult),
                  reads=[oo, rs2, swl], writes=[oo])
            p.grp("dve", lambda e, nq=nq, Gap=Gap: e.tensor_tensor(mixb[0:nq, :], oo[0:nq, :], Gap, ALU.mult), reads=[oo, Gh, Gs], writes=[mixb])
            p.grp("pe", lambda e, nq=nq: e.transpose(tpp[:, 0:nq], mixb[0:nq, :], identb[0:nq, 0:nq]), reads=[mixb, identb], writes=[tpp])
            p.grp("act", lambda e, nq=nq: e.copy(mT_st[:, 0:nq], tpp[:, 0:nq]), reads=[tpp], writes=[mT_st])
            p.dma("pool", lambda e, nq=nq, mix_dst_fn=mix_dst_fn: e.dma_start(out=mix_dst_fn(), in_=mT_st[:, 0:nq]), "mts", reads=[mT_st])

        for h in range(H):
            sg = SIG[h]
            for qd in range(4):
                w = S // 4
                p.dma("sp", lambda e, h=h, qd=qd, w=w: e.dma_start(out=KTh[:, qd * w:(qd + 1) * w], in_=KT[h, :, qd * w:(qd + 1) * w]), "kth", writes=[KTh])
                nk = NT
                p.dma("sp", lambda e, h=h, qd=qd, nk=nk: e.dma_start(
                    out=V1h[:, qd * nk:(qd + 1) * nk, :], in_=V1[h, qd * nk * 128:(qd + 1) * nk * 128, :].rearrange("(k p) e -> p k e", p=128)),
                    "v1h", writes=[V1h])
            p.dma("sp", lambda e, h=h: e.dma_start(out=QTh[:, :], in_=QT[h, :, :]), "qth", writes=[QTh])
            p.dma("sp", lambda e, h=h: e.dma_start(out=Gh[:, :, :], in_=GS[0:NO, h * 128:(h + 1) * 128].rearrange("(q p) e -> p q e", p=128)), "gh", writes=[Gh])
            Ob = Obs
            allu = []
            for qs in range(NT):
                units = []
                for G in range(qs + 1):
                    if SKIP_THRESH is not None and G < qs and sg * 512 * (qs - G - 1) > SKIP_THRESH:
                        continue
                    for c in range(2):
                        units.append((G, c))
                firstc = {}
                lastc = {}
                for idx, (G, c) in enumerate(units):
                    firstc.setdefault(c, idx)
                    lastc[c] = idx
                for idx, (G, c) in enumerate(units):
                    allu.append((qs, G, c, firstc[c] == idx, lastc[c] == idx, idx == len(units) - 1))

            def stage1_pair(un0, un1, h=h, sg=sg):
                qs, G = un0[0], un0[1]
                assert un1[0] == qs and un1[1] == G and un0[2] == 0 and un1[2] == 1
                u0 = unit[0]
                unit[0] += 2
                bufs = [(Sps[(u0 + c) % 4], Sbs[(u0 + c) % NSB], Pts[(u0 + c) % NSB]) for c in range(2)]
                fns = []
                for m in range(4):
                    for c in range(2):
                        fns.append(_mm(bufs[c][0][:, m * 128:(m + 1) * 128], KTh[64 * c:64 * c + 64, (4 * G + m) * 128:(4 * G + m + 1) * 128],
                                       QTh[64 * c:64 * c + 64, qs * 128:(qs + 1) * 128], True, True))
                p.grp("pe", fns, reads=[KTh, QTh], writes=[bufs[0][0], bufs[1][0]])
                tbl = tband if G == qs else tfar
                bias = 0.0 if G == qs else -sg * 512.0 * (qs - G - 1)
                for c in range(2):
                    Sp, Sb, Pt = bufs[c]
                    p.grp("dve", lambda e, Sp=Sp, Sb=Sb, tbl=tbl, h=h: e.scalar_tensor_tensor(Sb[:, :], Sp[:, :], 0.125, tbl[:, h, :], ALU.mult, ALU.add),
                          reads=[Sp, tbl], writes=[Sb])
                    p.grp("act", lambda e, Sb=Sb, Pt=Pt, bias=bias: e.activation(Pt[:, :], Sb[:, :], AF.Exp, bias=bias, scale=1.0), reads=[Sb], writes=[Pt])
                return bufs[0][2], bufs[1][2]

            def stage1(un, h=h, sg=sg):
                qs, G, c, fst, lst, lastq = un
                u = unit[0]
                unit[0] += 1
                Sp, Sb, Pt = Sps[u % 4], Sbs[u % NSB], Pts[u % NSB]
                fns = [_mm(Sp[:, m * 128:(m + 1) * 128], KTh[64 * c:64 * c + 64, (4 * G + m) * 128:(4 * G + m + 1) * 128],
                           QTh[64 * c:64 * c + 64, qs * 128:(qs + 1) * 128], True, True) for m in range(4)]
                p.grp("pe", fns, reads=[KTh, QTh], writes=[Sp])
                tbl = tband if G == qs else tfar
                p.grp("dve", lambda e, Sp=Sp, Sb=Sb, tbl=tbl, h=h: e.scalar_tensor_tensor(Sb[:, :], Sp[:, :], 0.125, tbl[:, h, :], ALU.mult, ALU.add),
                      reads=[Sp, tbl], writes=[Sb])
                bias = 0.0 if G == qs else -sg * 512.0 * (qs - G - 1)
                p.grp("act", lambda e, Sb=Sb, Pt=Pt, bias=bias: e.activation(Pt[:, :], Sb[:, :], AF.Exp, bias=bias, scale=1.0), reads=[Sb], writes=[Pt])
                return Pt

            def stage2(un, Pt, h=h):
                qs, G, c, fst, lst, lastq = un
                fns = []
                for m in range(4):
                    fns.append(_mm(Ob[c][:, 0:129], Pt[:, m * 128:(m + 1) * 128], V1h[:, 4 * G + m, 0:129], fst and m == 0, lst and m == 3))
                p.grp("pe", fns, reads=[Pt, V1h], writes=[Ob[c]])
                if lastq:
                    finalize(Ob, 128, Gh[:, qs, :], lambda h=h, qs=qs: MIXT[4 + h, :, qs * 128:(qs + 1) * 128])

            pending = []
            assert len(allu) % 2 == 0
            for i in range(0, len(allu), 2):
                pt0, pt1 = stage1_pair(allu[i], allu[i + 1])
                pending.append((allu[i], pt0))
                pending.append((allu[i + 1], pt1))
                while len(pending) > LOOKAHEAD:
                    stage2(*pending.pop(0))
            while pending:
                stage2(*pending.pop(0))
            for s_ in range(2):
                p.dma("sp", lambda e, s_=s_, h=h: e.dma_start(out=kc[:, :, :], in_=ck[s_, :, h, :].rearrange("(k p) e -> p k e", p=128)), "kc", writes=[kc])
                p.dma("sp", lambda e, s_=s_, h=h: e.dma_start(out=vcf[:, :, :], in_=cv[s_, :, h, :].rearrange("(k p) e -> p k e", p=128)), "vc", writes=[vcf])
                p.dma("sp", lambda e, s_=s_, h=h: e.dma_start(out=KTn[:, :], in_=KT[h, :, S + 32 * s_:S + 32 * s_ + 16]), "ktn", writes=[KTn])
                p.dma("sp", lambda e, s_=s_, h=h: e.dma_start(out=V1n[:, :], in_=V1[h, S + 32 * s_:S + 32 * s_ + 16, :]), "v1n", writes=[V1n])
                p.dma("sp", lambda e, s_=s_, h=h: e.dma_start(out=Gs[:, :], in_=GS[NO + 32 * s_:NO + 32 * s_ + 16, h * 128:(h + 1) * 128]), "gsn", writes=[Gs])
                p.grp("pool", lambda e: e.memset(V1c[:, :, 128:136], 1.0), writes=[V1c])
                p.grp("pool", lambda e: e.tensor_copy(V1c[:, :, 0:128], vcf[:, :, :]), reads=[vcf], writes=[V1c])
                for k4 in range(4):
                    fns = [lambda e, k4=k4, m=m, Ob=Ob: e.transpose(tpf[:, m * 128:(m + 1) * 128], kc[:, 4 * k4 + m, :], identf[:, :]) for m in range(4)]
                    p.grp("pe", fns, reads=[kc, identf], writes=[tpf])
                    alt.copy(p, KTc[:, k4 * 512:(k4 + 1) * 512], tpf[:, :], reads=[tpf], writes=[KTc])
                Ob = Obs
                fcount[0] += 1
                qcol = NO + 32 * s_
                for c in range(2):
                    u = unit[0]
                    unit[0] += 1
                    Sp, Sb, Pt = Sps[u % 4], Sbs[u % NSB], Pts[u % NSB]
                    fns = [_mm(Sp[:, kt * 16:(kt + 1) * 16], KTc[64 * c:64 * c + 64, kt * 128:(kt + 1) * 128], QTh[64 * c:64 * c + 64, qcol:qcol + 16], True, True)
                           for kt in range(16)]
                    fns.append(_mm(Sp[0:16, 256:272], KTn[64 * c:64 * c + 64, :], QTh[64 * c:64 * c + 64, qcol:qcol + 16], True, True))
                    p.grp("pe", fns, reads=[KTc, KTn, QTh], writes=[Sp])
                    p.grp("dve", lambda e, Sp=Sp, Sb=Sb, h=h: e.scalar_tensor_tensor(Sb[:, 0:256], Sp[:, 0:256], 0.125, tsamp[:, h, :], ALU.mult, ALU.add),
                          reads=[Sp, tsamp], writes=[Sb])
                    p.grp("dve", lambda e, Sp=Sp, h=h: e.scalar_tensor_tensor(Sbn[:, :], Sp[0:16, 256:272], 0.125, tsnew[:, h, :], ALU.mult, ALU.add),
                          reads=[Sp, tsnew], writes=[Sbn])
                    p.grp("act", lambda e, Sb=Sb, Pt=Pt: e.activation(Pt[:, 0:256], Sb[:, 0:256], AF.Exp), reads=[Sb], writes=[Pt])
                    p.grp("act", lambda e: e.activation(Ptn[:, :], Sbn[:, :], AF.Exp), reads=[Sbn], writes=[Ptn])
                    fns = [_mm(Ob[c][0:16, 0:129], Pt[:, kt * 16:(kt + 1) * 16], V1c[:, kt, 0:129], kt == 0, False) for kt in range(16)]
                    fns.append(_mm(Ob[c][0:16, 0:129], Ptn[:, :], V1n[:, 0:129], False, True))
                    p.grp("pe", fns, reads=[Pt, Ptn, V1c, V1n], writes=[Ob[c]])
                finalize(Ob, 16, Gs[:, :], lambda h=h, qcol=qcol: MIXT[4 + h, :, qcol:qcol + 16])
        p.emit()

    with ExitStack() as es:
      if 'D' in PASSES:
        p = Prog(nc, es, "D")
        alt = Alt()
        Wo = p.sb("Wo", [128, 16, D], BF16)
        stg = [p.sb("stg%d" % i, [128, D], F32) for i in range(2)]
        mT = [p.sb("mT%d" % i, [128, 16, 128], BF16) for i in range(2)]
        xr = [p.sb("xr%d" % i, [128, D], F32) for i in range(2)]
        hb = [p.sb("hb%d" % i, [128, D], F32) for i in range(2)]
        junk = p.sb("junk", [128, D], BF16)
        finw = p.sb("finw", [128, D], F32, ro=True)
        ss = p.sb("ss", [128, 1], F32)
        rstd = p.sb("rstd", [128, 1], F32)
        accs = [p.ps("acc%d" % i, [128, 512], F32) for i in range(8)]
        p.dma("sp", lambda e: e.dma_start(out=finw[:, :], in_=finw_d[:, :]), "c0", writes=[finw])
        load_weight_bf16(p, w_out, 0, D, Wo, stg, alt)
        p.grp("pool", lambda e: e.memset(mT[0][:, :, :], 0.0), writes=[mT[0]])
        p.grp("pool", lambda e: e.memset(mT[1][:, :, :], 0.0), writes=[mT[1]])
        for t in range(NT + 1):
            samp = t == NT
            ntok = 48 if samp else 128
            m, x_, hh = mT[t % 2], xr[t % 2], hb[t % 2]
            q0 = NO if samp else t * 128
            for s_ in ((0, 1) if samp else (0,)):
                w = 16 if samp else 128
                p.dma("sp", lambda e, m=m, q0=q0, s_=s_, w=w: e.dma_start(
                    out=m[:, :, 32 * s_:32 * s_ + w], in_=MIXT[:, :, q0 + 32 * s_:q0 + 32 * s_ + w].rearrange("c p t -> p c t")), "mt%d" % (t % 2), writes=[m])
            xsrc = xs[:, :] if samp else xa[t, 0, :, :]
            p.dma("sp", lambda e, x_=x_, xsrc=xsrc, ntok=ntok: e.dma_start(out=x_[0:ntok, :], in_=xsrc), "xr%d" % (t % 2), writes=[x_])
            for cg in range(4):
                a = accs[(t % 2) * 4 + cg]
                fns = [_mm(a[0:ntok, :], m[:, ch, 0:ntok], Wo[:, ch, cg * 512:(cg + 1) * 512], ch == 0, ch == 15) for ch in range(16)]
                p.grp("pe", fns, reads=[m, Wo], writes=[a])
                p.grp("dve", lambda e, a=a, hh=hh, x_=x_, cg=cg, ntok=ntok: e.tensor_tensor(
                    hh[0:ntok, cg * 512:(cg + 1) * 512], a[0:ntok, :], x_[0:ntok, cg * 512:(cg + 1) * 512], ALU.add), reads=[a, x_], writes=[hh])
            p.grp("act", lambda e, hh=hh, ntok=ntok: e.activation(junk[0:ntok, :], hh[0:ntok, :], AF.Square, accum_out=ss[0:ntok, :]), reads=[hh], writes=[junk, ss])
            p.grp("act", lambda e, ntok=ntok: e.activation(rstd[0:ntok, :], ss[0:ntok, :], AF.Ln, bias=eps, scale=1.0 / D), reads=[ss], writes=[rstd])
            p.grp("act", lambda e, ntok=ntok: e.activation(rstd[0:ntok, :], rstd[0:ntok, :], AF.Exp, scale=-0.5), reads=[rstd], writes=[rstd])
            p.grp("dve", lambda e, hh=hh, ntok=ntok: e.scalar_tensor_tensor(hh[0:ntok, :], hh[0:ntok, :], rstd[0:ntok, 0:1], finw[0:ntok, :], ALU.mult, ALU.mult),
                  reads=[hh, rstd, finw], writes=[hh])
            dst = y_s[:, :] if samp else y_own[t, :, :]
            p.dma("sp", lambda e, dst=dst, hh=hh, ntok=ntok: e.dma_start(out=dst, in_=hh[0:ntok, :]), "yo%d" % (t % 2), reads=[hh])
        p.emit()
    return nc


_CACHE = {}


def kernel(x_prompt, x_sample, cache_k, cache_v, state_pool, norm_w, w_in, w_pool, pool_scale,
           lambda_q1, lambda_k1, lambda_q2, lambda_k2, subln_w, w_out, final_norm_w):
    f = np.float32
    x_prompt = np.asarray(x_prompt, f)
    x_sample = np.asarray(x_sample, f)
    B, S, _ = x_prompt.shape
    NT = S // 512
    NO = NT * 128
    if NT not in _CACHE:
        _CACHE[NT] = build(NT)
    nc = _CACHE[NT]
    sig = np.array(SIG, np.float64)
    identb = np.eye(128, dtype=np.float32).astype(ml_dtypes.bfloat16)
    identf = np.eye(128, dtype=f)
    bc = lambda v: np.ascontiguousarray(np.broadcast_to(np.asarray(v, f).reshape(1, -1), (128, np.asarray(v).size)))
    lam = np.stack([np.asarray(a, f)[0] for a in (lambda_q1, lambda_k1, lambda_q2, lambda_k2)])
    lam_b = np.ascontiguousarray(np.broadcast_to(lam[None], (128, 4, 64)))
    swl = bc(np.asarray(subln_w, f)[0])
    pscale = np.ascontiguousarray(np.asarray(pool_scale, f)[0].reshape(4, 128).T)
    p_ = np.arange(128)[:, None, None, None]
    q_ = np.arange(128)[None, None, None, :]
    hh_ = sig[None, :, None, None]
    kt_ = np.arange(16)[None, None, :, None]
    qq_ = np.arange(16)[None, None, None, :]
    tsamp = (-hh_ * (2048 + qq_ - 128 * kt_ - p_)).astype(f).reshape(128, H, 256)
    kk_ = np.arange(16)[:, None, None]
    tsnew = (-sig[None, :, None] * np.abs(np.arange(16)[None, None, :] - kk_)).astype(f)
    in_maps = []
    for c in range(8):
        b, j = c // 4, c % 4
        order = [j] + [m for m in range(4) if m != j]
        xb = x_prompt[b].reshape(NT, 4, 128, D)
        xa = np.ascontiguousarray(xb[:, order])
        xh = np.zeros((NT, 16, D), f)
        for T in range(NT):
            st = 512 * T + 128 * j
            if st >= 16:
                xh[T] = x_prompt[b, st - 16:st]
        xs = np.zeros((48, D), f)
        xs[0:16] = x_sample[2 * c]
        xs[32:48] = x_sample[2 * c + 1]
        om = np.array(order)[None, None, :, None]
        qpos = 512 + 128 * j + q_
        kpos = 128 * om + p_
        tfar = (-hh_ * (qpos - kpos)).astype(f).reshape(128, H, 512)
        qpos = 128 * j + q_
        allowed = (kpos // 64) <= (qpos // 64)
        tband = np.where(allowed, -hh_ * np.abs(qpos - kpos), NEG).astype(f).reshape(128, H, 512)
        invc = np.zeros((128, 4, 16), f)
        for g, w in enumerate(WINS):
            for t in range(16):
                invc[:, g, t] = 1.0 / (min(w, t + 1) if j == 0 else w)
        in_maps.append({
            "xa": xa, "xh": xh, "xs": xs,
            "ck": np.ascontiguousarray(np.asarray(cache_k, f)[0, 2 * c:2 * c + 2]),
            "cv": np.ascontiguousarray(np.asarray(cache_v, f)[0, 2 * c:2 * c + 2]),
            "spool": np.ascontiguousarray(np.asarray(state_pool, f)[0, 2 * c:2 * c + 2]),
            "w_in": np.ascontiguousarray(np.asarray(w_in, f)[0]), "w_out": np.ascontiguousarray(np.asarray(w_out, f)[0]),
            "w_pool": np.ascontiguousarray(np.asarray(w_pool, f)[0]),
            "normw": bc(np.asarray(norm_w, f)[0]), "finw": bc(np.asarray(final_norm_w, f)),
            "pscale": pscale, "lam": lam_b, "swl": swl, "tfar": tfar, "tband": tband, "tsamp": tsamp, "tsnew": tsnew,
            "invc": invc, "identb": identb, "identf": identf,
        })
    res = run_bass_kernel_spmd(nc, in_maps, core_ids=list(range(8))).results
    DB = x_sample.shape[0]
    y_prompt = np.zeros((B, S, D), f)
    k_prompt = np.zeros((1, B, S, H, 128), f)
    v_prompt = np.zeros((1, B, S, H, 128), f)
    pool_prompt = np.zeros((1, B, 15, 512), f)
    y_sample = np.zeros((DB, 16, D), f)
    k_sample = np.zeros((1, DB, 16, H, 128), f)
    v_sample = np.zeros((1, DB, 16, H, 128), f)
    pool_sample = np.zeros((1, DB, 15, 512), f)
    for c in range(8):
        b, j = c // 4, c % 4
        r = res[c]
        y_prompt[b].reshape(NT, 4, 128, D)[:, j] = r["y_own"]
        k_prompt[0, b].reshape(NT, 4, 128, 1536)[:, j] = r["k_own"]
        v_prompt[0, b].reshape(NT, 4, 128, 1536)[:, j] = r["v_own"]
        if j == 3:
            pool_prompt[0, b] = r["pool_own"]
        for s_ in range(2):
            y_sample[2 * c + s_] = r["y_s"][32 * s_:32 * s_ + 16]
            k_sample[0, 2 * c + s_] = r["k_s"][32 * s_:32 * s_ + 16].reshape(16, H, 128)
            v_sample[0, 2 * c + s_] = r["v_s"][32 * s_:32 * s_ + 16].reshape(16, H, 128)
            pool_sample[0, 2 * c + s_] = r["pool_s"][s_]
    return (y_prompt, y_sample, k_prompt, v_prompt, pool_prompt, k_sample, v_sample, pool_sample)
```

```python
import math
import os
from contextlib import ExitStack
import numpy as np
import ml_dtypes
import concourse.bass as bass
import concourse.mybir as mybir
from concourse.bass_utils import run_bass_kernel_spmd

F32 = mybir.dt.float32
BF16 = mybir.dt.bfloat16
AF = mybir.ActivationFunctionType
ALU = mybir.AluOpType

D = 2048
H = 12
U0, GP0, Q0, K0, V0, G0 = 0, 512, 1024, 2560, 4096, 5632
WINS = (2, 4, 8, 16)
NEG = -30000.0
SKIP_THRESH = None


def _slopes(n=12):
    def p2(m):
        st = 2.0 ** (-8.0 / m)
        return [st ** (i + 1) for i in range(m)]
    c = 2 ** int(math.floor(math.log2(n)))
    return p2(c) + p2(2 * c)[0::2][: n - c]


SIG = [float(np.float32(s)) for s in _slopes()]


class Buf:
    def __init__(self, t, ro=False):
        self.t = t
        self.wr = None
        self.rd = []
        self.ro = ro

    def __getitem__(self, k):
        return self.t[k]


class Prog:
    ENGS = (("pe", "tensor"), ("act", "scalar"), ("dve", "vector"), ("pool", "gpsimd"), ("sp", "sync"))

    def __init__(self, nc, es, tag):
        self.nc, self.es, self.tag = nc, es, tag
        self.ops = {e: [] for e, _ in self.ENGS}
        self.esem = {e: es.enter_context(nc.semaphore("%s_e_%s" % (tag, e))) for e in ("pe", "act", "dve", "pool")}
        self.ecnt = {e: 0 for e in self.esem}
        self.dsem, self.dcnt = {}, {}

    def sb(self, name, shape, dt, ro=False):
        return Buf(self.es.enter_context(self.nc.sbuf_tensor(self.tag + name, shape, dt)), ro)

    def ps(self, name, shape, dt=F32):
        return Buf(self.es.enter_context(self.nc.psum_tensor(self.tag + name, shape, dt)))

    def _deps(self, reads, writes):
        d = [b.wr for b in reads if b.wr is not None]
        for b in writes:
            d += b.rd
            if b.wr is not None:
                d.append(b.wr)
        return d

    def _upd(self, tok, reads, writes):
        for b in writes:
            b.wr = tok
            b.rd = []
        for b in reads:
            if not b.ro:
                b.rd.append(tok)

    def grp(self, eng, fns, reads=(), writes=()):
        if not isinstance(fns, (list, tuple)):
            fns = [fns]
        deps = self._deps(reads, writes)
        self.ecnt[eng] += 1
        tok = (self.esem[eng], self.ecnt[eng], eng)
        n = len(fns)
        for i, fn in enumerate(fns):
            self.ops[eng].append((fn, deps if i == 0 else [], tok if i == n - 1 else None, 1))
        self._upd(tok, reads, writes)
        return tok

    def dma(self, q, fn, sname, reads=(), writes=()):
        if sname not in self.dsem:
            self.dsem[sname] = self.es.enter_context(self.nc.semaphore("%s_d_%s" % (self.tag, sname)))
            self.dcnt[sname] = 0
        deps = self._deps(reads, writes)
        self.dcnt[sname] += 16
        tok = (self.dsem[sname], self.dcnt[sname], "dma")
        self.ops[q].append((fn, deps, tok, 16))
        self._upd(tok, reads, writes)
        return tok

    def emit(self):
        finals = [(self.esem[e], self.ecnt[e], e) for e in self.esem if self.ecnt[e] > 0]
        finals += [(self.dsem[s], self.dcnt[s], "dma") for s in self.dsem]
        for e, _ in self.ENGS:
            self.ops[e].append((None, [f for f in finals if f[2] != e], None, 0))
        with self.nc.Block() as block:
            for ename, attr in self.ENGS:
                def body(eng, ename=ename):
                    waited = {}
                    for fn, deps, tok, amt in self.ops[ename]:
                        mx = {}
                        for (s, v, src) in deps:
                            if src == "pe" and ename == "pe":
                                continue
                            k = id(s)
                            if k not in mx or mx[k][1] < v:
                                mx[k] = (s, v)
                        for k, (s, v) in mx.items():
                            if waited.get(k, 0) < v:
                                eng.wait_ge(s, v)
                                waited[k] = v
                        if fn is None:
                            continue
                        ins = fn(eng)
                        if tok is not None:
                            ins.then_inc(tok[0], amt)
                getattr(block, attr)(body)


def _mm(out, lhsT, rhs, start, stop):
    return lambda e: e.matmul(out, lhsT, rhs, start=start, stop=stop)


class Alt:
    def __init__(self):
        self.i = 0

    def copy(self, p, out, in_, reads, writes):
        self.i += 1
        if self.i % 2:
            return p.grp("act", lambda e: e.copy(out, in_), reads, writes)
        return p.grp("dve", lambda e: e.tensor_copy(out, in_), reads, writes)


def load_weight_bf16(p, io_w, col0, ncols, Wb, stg, alt, qeng="sp"):
    for ch in range(16):
        s = stg[ch % len(stg)]
        src = io_w[ch * 128:(ch + 1) * 128, col0:col0 + ncols]
        p.dma(qeng, lambda e, s=s, src=src: e.dma_start(out=s[:, 0:ncols], in_=src), "wst%d" % (ch % len(stg)), writes=[s])
        eng = "pool" if ch % 2 else "dve"
        p.grp(eng, lambda e, s=s, ch=ch: e.tensor_copy(Wb[:, ch, :], s[:, 0:ncols]), reads=[s], writes=[Wb])


def rmsnorm_T(p, xsrc_fn, nsub, ntok, xbufs, xns, normw, xnT, tps, ident, ss, rstd, alt, eps, cnt):
    st = []
    for sub in range(nsub):
        i = cnt[0]
        cnt[0] += 1
        xb = xbufs[i % len(xbufs)]
        xn = xns[i % len(xns)]
        st.append((i, xn))
        p.dma("sp", lambda e, xb=xb, sub=sub: e.dma_start(out=xb[0:ntok, :], in_=xsrc_fn(sub)), "x%d" % (i % len(xbufs)), writes=[xb])
        p.grp("act", lambda e, xb=xb, xn=xn: e.activation(xn[0:ntok, :], xb[0:ntok, :], AF.Square, accum_out=ss[0:ntok, :]),
              reads=[xb], writes=[xn, ss])
        p.grp("act", lambda e: e.activation(rstd[0:ntok, :], ss[0:ntok, :], AF.Ln, bias=eps, scale=1.0 / D), reads=[ss], writes=[rstd])
        p.grp("act", lambda e: e.activation(rstd[0:ntok, :], rstd[0:ntok, :], AF.Exp, scale=-0.5), reads=[rstd], writes=[rstd])
        p.grp("dve", lambda e, xb=xb, xn=xn: e.scalar_tensor_tensor(xn[0:ntok, :], xb[0:ntok, :], rstd[0:ntok, 0:1], normw[0:ntok, :], ALU.mult, ALU.mult),
              reads=[xb, rstd, normw], writes=[xn])
    yield
    for sub in range(nsub):
        i, xn = st[sub]
        tp = tps[i % len(tps)]
        fns = []
        for ch in range(16):
            fns.append(lambda e, tp=tp, xn=xn, ch=ch: e.transpose(tp[ch // 8][:, (ch % 8) * 128:(ch % 8) * 128 + ntok],
                                                                   xn[0:ntok, ch * 128:(ch + 1) * 128], ident[0:ntok, 0:ntok]))
        p.grp("pe", fns, reads=[xn, ident], writes=[tp[0], tp[1]])
        for hf in range(2):
            src = tp[hf][:, :].rearrange("p (c t) -> p c t", c=8)[:, :, 0:ntok]
            dst = xnT[:, hf * 8:(hf + 1) * 8, sub * ntok:(sub + 1) * ntok]
            alt.copy(p, dst, src, reads=[tp[hf]], writes=[xnT])


def run_tiles(gens):
    n = len(gens)
    if n == 0:
        return
    next(gens[0])
    next(gens[0])
    for i in range(n):
        if i + 1 < n:
            next(gens[i + 1])
        for _ in gens[i]:
            pass
        if i + 1 < n:
            next(gens[i + 1])


def build(NT):
    S = NT * 512
    NO = NT * 128
    NB = NT // 4
    KTOK = S + 48
    QTOK = NO + 48
    nc = bass.Bass("TRN2", target_bir_lowering=False)

    def din(name, shape, dt=F32):
        return nc.dram_tensor(name, list(shape), dt, kind="ExternalInput").ap()

    def dout(name, shape, dt=F32):
        return nc.dram_tensor(name, list(shape), dt, kind="ExternalOutput").ap()

    def dscr(name, shape, dt):
        return nc.dram_tensor(name, list(shape), dt).ap()

    xa = din("xa", [NT, 4, 128, D])
    xh = din("xh", [NT, 16, D])
    xs = din("xs", [48, D])
    ck = din("ck", [2, 2048, H, 128])
    cv = din("cv", [2, 2048, H, 128])
    spool = din("spool", [2, 15, 512])
    w_in = din("w_in", [D, 7168])
    w_out = din("w_out", [D, D])
    w_pool = din("w_pool", [4, 128, 128])
    normw_d = din("normw", [128, D])
    finw_d = din("finw", [128, D])
    pscale_d = din("pscale", [128, 4])
    lam_d = din("lam", [128, 4, 64])
    swl_d = din("swl", [128, 128])
    tfar_d = din("tfar", [128, H, 512])
    tband_d = din("tband", [128, H, 512])
    tsamp_d = din("tsamp", [128, H, 256])
    tsnew_d = din("tsnew", [16, H, 16])
    invc_d = din("invc", [128, 4, 16])
    identb_d = din("identb", [128, 128], BF16)
    identf_d = din("identf", [128, 128])

    y_own = dout("y_own", [NT, 128, D])
    y_s = dout("y_s", [48, D])
    k_own = dout("k_own", [NT, 128, 1536])
    v_own = dout("v_own", [NT, 128, 1536])
    pool_own = dout("pool_own", [15, 512])
    k_s = dout("k_s", [48, 1536])
    v_s = dout("v_s", [48, 1536])
    pool_s = dout("pool_s", [2, 15, 512])

    KT = dscr("KT_scr", [H, 128, KTOK], BF16)
    V1 = dscr("V1_scr", [H, KTOK, 136], BF16)
    QT = dscr("QT_scr", [H, 128, QTOK], BF16)
    GS = dscr("G_scr", [QTOK, 1536], F32)
    MIXT = dscr("MIXT_scr", [16, 128, QTOK], BF16)

    eps = 1e-6

    PASSES = os.environ.get('DBG_PASSES', 'ABCD')
    with ExitStack() as es:
      if 'A' in PASSES:
        p = Prog(nc, es, "A")
        alt = Alt()
        Wk = p.sb("Wk", [128, 16, 1536], BF16)
        Wv = p.sb("Wv", [128, 16, 1536], BF16)
        xbufs = [p.sb("xb%d" % i, [128, D], F32) for i in range(2)]
        xns = [p.sb("xn%d" % i, [128, D], BF16) for i in range(4)]
        xnT = p.sb("xnT", [128, 16, 512], BF16)
        kT_st = p.sb("kTst", [128, H, 512], BF16)
        v1_st = p.sb("v1st", [128, 4, H, 136], BF16)
        fo_st = [p.sb("fost%d" % i, [128, 1536], F32) for i in range(2)]
        normw = p.sb("normw", [128, D], F32, ro=True)
        ident = p.sb("ident", [128, 128], BF16, ro=True)
        ss = p.sb("ss", [128, 1], F32)
        rstd = p.sb("rstd", [128, 1], F32)
        tps = [[p.ps("tp%d%d" % (i, k), [128, 1024], BF16) for k in range(2)] for i in range(2)]
        accs = [p.ps("acc%d" % i, [128, 512], F32) for i in range(4)]
        acc_i = [0]

        def nacc():
            acc_i[0] += 1
            return accs[acc_i[0] % 4]

        p.dma("sp", lambda e: e.dma_start(out=normw[:, :], in_=normw_d[:, :]), "c0", writes=[normw])
        p.dma("sp", lambda e: e.dma_start(out=ident[:, :], in_=identb_d[:, :]), "c1", writes=[ident])
        p.grp("pool", lambda e: e.memset(v1_st[:, :, :, 128:136], 1.0), writes=[v1_st])
        load_weight_bf16(p, w_in, K0, 1536, Wk, xbufs, alt)
        load_weight_bf16(p, w_in, V0, 1536, Wv, xbufs, alt)
        cnt = [0]
        LV = int(os.environ.get('DBG_LV', '9'))
        FO = os.environ.get('DBG_FO', 'ad')
        def tile_gen(T):
            samp = (T == NT)
            nsub, ntok = (1, 48) if samp else (4, 128)
            tokw = nsub * ntok
            tok0 = S if samp else T * 512
            if samp:
                srcf = lambda sub: xs[:, :]
            else:
                srcf = lambda sub, T=T: xa[T, sub, :, :]
            rg = rmsnorm_T(p, srcf, nsub, ntok, xbufs, xns, normw, xnT, tps, ident, ss, rstd, alt, eps, cnt)
            next(rg)
            yield
            for _ in rg:
                pass
            yield
            if LV < 3:
                return
            for h in range(H):
                a = nacc()
                fns = [_mm(a[:, 0:tokw], Wk[:, ch, h * 128:(h + 1) * 128], xnT[:, ch, 0:tokw], ch == 0, ch == 15) for ch in range(16)]
                p.grp("pe", fns, reads=[Wk, xnT], writes=[a])
                alt.copy(p, kT_st[:, h, 0:tokw], a[:, 0:tokw], reads=[a], writes=[kT_st])
            p.dma("pool", lambda e, tok0=tok0, tokw=tokw: e.dma_start(
                out=KT[:, :, tok0:tok0 + tokw].rearrange("h p t -> p h t"), in_=kT_st[:, :, 0:tokw]), "kts", reads=[kT_st])
            if LV < 4:
                return
            for sub in range(nsub):
                own = samp or sub == 0
                for which in ((0, 1) if own else (0,)):
                    W = Wv if which == 0 else Wk
                    fo = fo_st[which]
                    for cg in range(3):
                        a = nacc()
                        fns = [_mm(a[0:ntok, :], xnT[:, ch, sub * ntok:(sub + 1) * ntok], W[:, ch, cg * 512:(cg + 1) * 512], ch == 0, ch == 15)
                               for ch in range(16)]
                        p.grp("pe", fns, reads=[W, xnT], writes=[a])
                        if which == 0:
                            p.grp("dve", lambda e, a=a, sub=sub, cg=cg, ntok=ntok: e.tensor_copy(
                                v1_st[0:ntok, sub, cg * 4:(cg + 1) * 4, 0:128], a[0:ntok, :].rearrange("p (h e) -> p h e", h=4)),
                                reads=[a], writes=[v1_st])
                        if own and LV >= 5 and 'a' in FO:
                            if os.environ.get('DBG_FOENG', 'dve') == 'act':
                                p.grp("act", lambda e, a=a, fo=fo, cg=cg, ntok=ntok: e.copy(fo[0:ntok, cg * 512:(cg + 1) * 512], a[0:ntok, :]),
                                      reads=[a], writes=[fo])
                            else:
                                p.grp("dve", lambda e, a=a, fo=fo, cg=cg, ntok=ntok: e.tensor_copy(fo[0:ntok, cg * 512:(cg + 1) * 512], a[0:ntok, :]),
                                      reads=[a], writes=[fo])
                    if own and LV >= 5 and 'd' in FO:
                        if samp:
                            dst = (v_s if which == 0 else k_s)[:, :]
                        else:
                            dst = (v_own if which == 0 else k_own)[T, :, :]
                        p.dma("sp", lambda e, dst=dst, fo=fo, ntok=ntok: e.dma_start(out=dst, in_=fo[0:ntok, :]), "fo%d" % which, reads=[fo])
            for sub in (range(nsub) if LV >= 6 else ()):
                p.dma("pool", lambda e, sub=sub, tok0=tok0, ntok=ntok: e.dma_start(
                    out=V1[:, tok0 + sub * ntok:tok0 + (sub + 1) * ntok, :].rearrange("h t e -> t h e"),
                    in_=v1_st[0:ntok, sub, :, :]), "v1s", reads=[v1_st])
        run_tiles([tile_gen(T) for T in (range(NT + (0 if os.environ.get('DBG_NOSAMP') else 1)) if LV >= 2 else ())])
        p.emit()

    for part in ((1, 2) if 'B' in PASSES else ()):
        with ExitStack() as es:
            p = Prog(nc, es, "B%d" % part)
            alt = Alt()
            Wq = p.sb("Wq", [128, 16, 1536], BF16) if part == 1 else None
            Wg = p.sb("Wg", [128, 16, 1536], BF16) if part == 1 else None
            Wu = p.sb("Wu", [128, 16, 1024], BF16) if part == 2 else None
            xbufs = [p.sb("xb%d" % i, [128, D], F32) for i in range(2)]
            xns = [p.sb("xn%d" % i, [128, D], BF16) for i in range(4)]
            xnT = p.sb("xnT", [128, 16, 512], BF16)
            qT_st = p.sb("qTst", [128, H, 512], BF16) if part == 1 else None
            g_st = p.sb("gst", [128, 1536], F32) if part == 1 else None
            uh = p.sb("uh", [128, 4, 512], F32) if part == 2 else None
            ext = [p.sb("ext%d" % i, [128, 4, 4, 144], F32) for i in range(3)] if part == 2 else None
            dd = p.sb("dd", [128, 4, 512], F32) if part == 2 else None
            ddb = p.sb("ddb", [128, 4, 512], BF16) if part == 2 else None
            sgp = p.sb("sgp", [128, 4, 512], F32) if part == 2 else None
            mp_st = p.sb("mpst", [128, 4, 512], BF16) if part == 2 else None
            ytmp = p.sb("ytmp", [128, 512], F32) if part == 2 else None
            wp_f = p.sb("wpf", [128, 4, 128], F32) if part == 2 else None
            wp_b = p.sb("wpb", [128, 4, 128], BF16, ro=True) if part == 2 else None
            pscale = p.sb("pscale", [128, 4], F32, ro=True) if part == 2 else None
            invc = p.sb("invc", [128, 4, 16], F32, ro=True) if part == 2 else None
            hist = p.sb("hist", [128, 4, 2, 16], F32) if part == 2 else None
            up_st = p.sb("upst", [128, 4, 16], F32) if part == 2 else None
            normw = p.sb("normw", [128, D], F32, ro=True)
            ident = p.sb("ident", [128, 128], BF16, ro=True)
            ss = p.sb("ss", [128, 1], F32)
            rstd = p.sb("rstd", [128, 1], F32)
            tps = [[p.ps("tp%d%d" % (i, k), [128, 1024], BF16) for k in range(2)] for i in range(2)]
            accs = [p.ps("acc%d" % i, [128, 512], F32) for i in range(4)]
            acc_i = [0]

            def nacc():
                acc_i[0] += 1
                return accs[acc_i[0] % 4]

            p.dma("sp", lambda e: e.dma_start(out=normw[:, :], in_=normw_d[:, :]), "c0", writes=[normw])
            p.dma("sp", lambda e: e.dma_start(out=ident[:, :], in_=identb_d[:, :]), "c1", writes=[ident])
            if part == 2:
              p.dma("sp", lambda e: e.dma_start(out=pscale[:, :], in_=pscale_d[:, :]), "c2", writes=[pscale])
            if part == 2:
              p.dma("sp", lambda e: e.dma_start(out=invc[:, :, :], in_=invc_d[:, :, :]), "c3", writes=[invc])
            if part == 2:
              p.dma("sp", lambda e: e.dma_start(out=wp_f[:, :, :], in_=w_pool.rearrange("g c d -> c g d")), "c4", writes=[wp_f])
            if part == 2:
              p.grp("dve", lambda e: e.tensor_copy(wp_b[:, :, :], wp_f[:, :, :]), reads=[wp_f], writes=[wp_b])
            if part == 2:
              p.grp("pool", lambda e: e.memset(hist[:, :, :, :], 0.0), writes=[hist])
              p.grp("pool", lambda e: e.memset(ext[0][:, :, :, :], 0.0), writes=[ext[0]])
              p.grp("pool", lambda e: e.memset(ext[1][:, :, :, :], 0.0), writes=[ext[1]])
              p.grp("pool", lambda e: e.memset(ext[2][:, :, :, :], 0.0), writes=[ext[2]])
            for s_ in (range(2) if part == 2 else ()):
                for g in range(4):
                    p.dma("sp", lambda e, s_=s_, g=g: e.dma_start(
                        out=hist[:, g, s_, 1:16], in_=spool[s_, :, g * 128:(g + 1) * 128].rearrange("t c -> c t"),
                        allow_slow_non_contiguous=True), "c5", writes=[hist])
            if part == 1:
                load_weight_bf16(p, w_in, Q0, 1536, Wq, xbufs, alt)
                load_weight_bf16(p, w_in, G0, 1536, Wg, xbufs, alt)
            else:
                load_weight_bf16(p, w_in, U0, 1024, Wu, xbufs, alt)
            cnt = [0]
            def tile_gen(t, part=part):
                halo, samp = (t == -1), (t == NB)
                nsub, ntok = (1, 48) if samp else (4, 128)
                tokw = nsub * ntok
                if halo:
                    nh = max(1, (NT * 16) // 128)
                    nsub = nh
                    tokw = nsub * 128
                    srcf = lambda sub: xh[sub * 8:(sub + 1) * 8, :, :].rearrange("a t d -> (a t) d")
                elif samp:
                    srcf = lambda sub: xs[:, :]
                else:
                    srcf = lambda sub, t=t: xa[4 * t + sub, 0, :, :]
                rg = rmsnorm_T(p, srcf, nsub, ntok, xbufs, xns, normw, xnT, tps, ident, ss, rstd, alt, eps, cnt)
                next(rg)
                yield
                for _ in rg:
                    pass
                yield
                if halo:
                    for g in range(4):
                        a = nacc()
                        fns = [_mm(a[:, 0:tokw], Wu[:, ch, g * 128:(g + 1) * 128], xnT[:, ch, 0:tokw], ch == 0, ch == 15) for ch in range(16)]
                        p.grp("pe", fns, reads=[Wu, xnT], writes=[a])
                        alt.copy(p, uh[:, g, 0:tokw], a[:, 0:tokw], reads=[a], writes=[uh])
                    return
                qtok0 = NO if samp else t * 512
                for h in (range(H) if part == 1 else ()):
                    a = nacc()
                    fns = [_mm(a[:, 0:tokw], Wq[:, ch, h * 128:(h + 1) * 128], xnT[:, ch, 0:tokw], ch == 0, ch == 15) for ch in range(16)]
                    p.grp("pe", fns, reads=[Wq, xnT], writes=[a])
                    alt.copy(p, qT_st[:, h, 0:tokw], a[:, 0:tokw], reads=[a], writes=[qT_st])
                if part == 1:
                  p.dma("pool", lambda e, qtok0=qtok0, tokw=tokw: e.dma_start(
                    out=QT[:, :, qtok0:qtok0 + tokw].rearrange("h p t -> p h t"), in_=qT_st[:, :, 0:tokw]), "qts", reads=[qT_st])
                for sub in (range(nsub) if part == 1 else ()):
                    for cg in range(3):
                        a = nacc()
                        fns = [_mm(a[0:ntok, :], xnT[:, ch, sub * ntok:(sub + 1) * ntok], Wg[:, ch, cg * 512:(cg + 1) * 512], ch == 0, ch == 15)
                               for ch in range(16)]
                        p.grp("pe", fns, reads=[Wg, xnT], writes=[a])
                        p.grp("act", lambda e, a=a, cg=cg, ntok=ntok: e.activation(g_st[0:ntok, cg * 512:(cg + 1) * 512], a[0:ntok, :], AF.Silu),
                              reads=[a], writes=[g_st])
                    p.dma("sp", lambda e, sub=sub, qtok0=qtok0, ntok=ntok: e.dma_start(
                        out=GS[qtok0 + sub * ntok:qtok0 + (sub + 1) * ntok, :], in_=g_st[0:ntok, :]), "gs", reads=[g_st])
                if part == 1:
                    return
                exA, exB, exC = ext
                for g in range(4):
                    a = nacc()
                    fns = [_mm(a[:, 0:tokw], Wu[:, ch, g * 128:(g + 1) * 128], xnT[:, ch, 0:tokw], ch == 0, ch == 15) for ch in range(16)]
                    p.grp("pe", fns, reads=[Wu, xnT], writes=[a])
                    if samp:
                        for s_ in range(2):
                            p.grp("dve", lambda e, a=a, g=g, s_=s_: e.tensor_copy(exA[:, g, s_, 16:32], a[:, 32 * s_:32 * s_ + 16]), reads=[a], writes=[exA])
                            p.grp("dve", lambda e, g=g, s_=s_: e.tensor_copy(exA[:, g, s_, 0:16], hist[:, g, s_, :]), reads=[hist], writes=[exA])
                    else:
                        p.grp("dve", lambda e, a=a, g=g: e.tensor_copy(exA[:, g, :, 16:144], a[:, :].rearrange("p (s t) -> p s t", s=4)), reads=[a], writes=[exA])
                        p.grp("pool", lambda e, g=g, t=t: e.tensor_copy(
                            exA[:, g, :, 0:16], uh[:, g, t * 64:(t + 1) * 64].rearrange("p (s t) -> p s t", s=4)), reads=[uh], writes=[exA])
                    a2 = nacc()
                    fns = [_mm(a2[:, 0:tokw], Wu[:, ch, 512 + g * 128:512 + (g + 1) * 128], xnT[:, ch, 0:tokw], ch == 0, ch == 15) for ch in range(16)]
                    p.grp("pe", fns, reads=[Wu, xnT], writes=[a2])
                    p.grp("act", lambda e, a2=a2, g=g, tokw=tokw: e.activation(sgp[:, g, 0:tokw], a2[:, 0:tokw], AF.Silu), reads=[a2], writes=[sgp])
                nsx, wx = (2, 32) if samp else (4, 144)
                for g in range(4):
                    cur = exA
                    st, si = 1, 0
                    while st < WINS[g]:
                        oth = (exB, exC)[si % 2]
                        si += 1
                        p.grp("dve", lambda e, cur=cur, oth=oth, st=st, g=g, nsx=nsx, wx=wx: e.tensor_tensor(
                            oth[:, g, 0:nsx, st:wx], cur[:, g, 0:nsx, st:wx], cur[:, g, 0:nsx, 0:wx - st], ALU.add), reads=[cur], writes=[oth])
                        cur = oth
                        st *= 2
                    if samp:
                        dview = dd[:, g, 0:64].rearrange("p (s t) -> p s t", s=2)[:, :, 0:16]
                    else:
                        dview = dd[:, g, :].rearrange("p (s t) -> p s t", s=4)
                    p.grp("dve", lambda e, cur=cur, g=g, dview=dview, nsx=nsx, wx=wx: e.scalar_tensor_tensor(
                        dview, cur[:, g, 0:nsx, 16:wx], 1.0 / WINS[g], exA[:, g, 0:nsx, 16:wx], ALU.mult, ALU.subtract), reads=[cur, exA], writes=[dd])
                    if t == 0:
                        p.grp("dve", lambda e, cur=cur, g=g: e.tensor_tensor(dd[:, g, 0:16], cur[:, g, 0, 16:32], invc[:, g, :], ALU.mult), reads=[cur, invc], writes=[dd])
                        p.grp("dve", lambda e, g=g: e.tensor_tensor(dd[:, g, 0:16], dd[:, g, 0:16], exA[:, g, 0, 16:32], ALU.subtract), reads=[dd, exA], writes=[dd])
                    if samp:
                        p.grp("pool", lambda e, g=g: e.memset(ddb[:, g, 0:64], 0.0), writes=[ddb])
                        p.grp("dve", lambda e, g=g, dview=dview: e.tensor_copy(ddb[:, g, 0:64].rearrange("p (s t) -> p s t", s=2)[:, :, 0:16], dview), reads=[dd], writes=[ddb])
                    else:
                        p.grp("dve", lambda e, g=g: e.tensor_copy(ddb[:, g, :], dd[:, g, :]), reads=[dd], writes=[ddb])
                    a = nacc()
                    p.grp("pe", [_mm(a[:, 0:tokw], wp_b[:, g, :], ddb[:, g, 0:tokw], True, True)], reads=[wp_b, ddb], writes=[a])
                    p.grp("dve", lambda e, a=a, g=g, tokw=tokw: e.scalar_tensor_tensor(
                        mp_st[:, g, 0:tokw], a[:, 0:tokw], pscale[:, g:g + 1], sgp[:, g, 0:tokw], ALU.mult, ALU.mult), reads=[a, sgp, pscale], writes=[mp_st])
                p.dma("pool", lambda e, qtok0=qtok0, tokw=tokw: e.dma_start(
                    out=MIXT[0:4, :, qtok0:qtok0 + tokw].rearrange("g p t -> p g t"), in_=mp_st[:, :, 0:tokw]), "mps", reads=[mp_st])
                if t == NB - 1:
                    p.grp("dve", lambda e: e.tensor_copy(up_st[:, :, 0:15], exA[:, :, 3, 129:144]), reads=[exA], writes=[up_st])
                    for g in range(4):
                        p.dma("pool", lambda e, g=g: e.dma_start(out=pool_own[:, g * 128:(g + 1) * 128].rearrange("t c -> c t"),
                                                                 in_=up_st[:, g, 0:15], allow_slow_non_contiguous=True), "ups", reads=[up_st])
                if samp:
                    for s_ in range(2):
                        for g in range(4):
                            p.dma("pool", lambda e, g=g, s_=s_: e.dma_start(out=pool_s[s_, :, g * 128:(g + 1) * 128].rearrange("t c -> c t"),
                                                                            in_=exA[:, g, s_, 17:32], allow_slow_non_contiguous=True), "ups2", reads=[exA])
            run_tiles([tile_gen(t) for t in range(-1 if part == 2 else 0, NB + 1)])
            p.emit()

    with ExitStack() as es:
      if 'C' in PASSES:
        p = Prog(nc, es, "C")
        alt = Alt()
        KTh = p.sb("KTh", [128, S], BF16)
        V1h = p.sb("V1h", [128, NT * 4, 136], BF16)
        QTh = p.sb("QTh", [128, QTOK], BF16)
        Gh = p.sb("Gh", [128, NT, 128], F32)
        tfar = p.sb("tfar", [128, H, 512], F32, ro=True)
        tband = p.sb("tband", [128, H, 512], F32, ro=True)
        tsamp = p.sb("tsamp", [128, H, 256], F32, ro=True)
        tsnew = p.sb("tsnew", [16, H, 16], F32, ro=True)
        swl = p.sb("swl", [128, 128], F32, ro=True)
        lam = p.sb("lam", [128, 4, 64], F32)
        lamt = p.sb("lamt", [128, 2, 64], F32)
        lam2 = p.sb("lam2", [128, 2], F32)
        nlam = p.sb("nlam", [128, 1], F32, ro=True)
        identb = p.sb("identb", [128, 128], BF16, ro=True)
        identf = p.sb("identf", [128, 128], F32, ro=True)
        LOOKAHEAD = 4
        NSB = 6
        Sbs = [p.sb("Sb%d" % i, [128, 512], F32) for i in range(NSB)]
        Pts = [p.sb("Pt%d" % i, [128, 512], BF16) for i in range(NSB)]
        rr = p.sb("rr", [128, 2], F32)
        t2 = p.sb("t2", [128, 128], F32)
        oo = p.sb("oo", [128, 128], F32)
        junk = p.sb("junk", [128, 128], F32)
        ssq = p.sb("ssq", [128, 1], F32)
        rs2 = p.sb("rs2", [128, 1], F32)
        mixb = p.sb("mixb", [128, 128], BF16)
        mT_st = p.sb("mTst", [128, 128], BF16)
        kc = p.sb("kc", [128, 16, 128], F32)
        vcf = p.sb("vcf", [128, 16, 128], F32)
        KTc = p.sb("KTc", [128, 2048], BF16)
        V1c = p.sb("V1c", [128, 16, 136], BF16)
        KTn = p.sb("KTn", [128, 16], BF16)
        V1n = p.sb("V1n", [16, 136], BF16)
        Gs = p.sb("Gs", [16, 128], F32)
        Sbn = p.sb("Sbn", [16, 16], F32)
        Ptn = p.sb("Ptn", [16, 16], BF16)
        Sps = [p.ps("Sp%d" % i, [128, 512], F32) for i in range(4)]
        Obs = [p.ps("Ob%d" % i, [128, 512], F32) for i in range(2)]
        tpp = p.ps("tpp", [128, 1024], BF16)
        tpf = p.ps("tpf", [128, 512], F32)

        for i, (dst, src) in enumerate(((tfar, tfar_d), (tband, tband_d), (tsamp, tsamp_d), (tsnew, tsnew_d), (lam, lam_d))):
            p.dma("sp", lambda e, dst=dst, src=src: e.dma_start(out=dst[:, :, :], in_=src[:, :, :]), "c%d" % i, writes=[dst])
        for i, (dst, src) in enumerate(((swl, swl_d), (identb, identb_d), (identf, identf_d))):
            p.dma("sp", lambda e, dst=dst, src=src: e.dma_start(out=dst[:, :], in_=src[:, :]), "d%d" % i, writes=[dst])
        p.grp("dve", lambda e: e.tensor_tensor(lamt[:, 0, :], lam[:, 0, :], lam[:, 1, :], ALU.mult), reads=[lam], writes=[lamt])
        p.grp("dve", lambda e: e.tensor_tensor(lamt[:, 1, :], lam[:, 2, :], lam[:, 3, :], ALU.mult), reads=[lamt, lam], writes=[lamt])
        p.grp("dve", lambda e: e.tensor_reduce(lam2[:, :], lamt[:, :, :], mybir.AxisListType.X, ALU.add), reads=[lamt], writes=[lam2])
        p.grp("act", lambda e: e.activation(lam2[:, :], lam2[:, :], AF.Exp), reads=[lam2], writes=[lam2])
        p.grp("dve", lambda e: e.tensor_tensor(nlam[:, :], lam2[:, 1:2], lam2[:, 0:1], ALU.subtract), reads=[lam2], writes=[nlam])
        p.grp("dve", lambda e: e.tensor_scalar(nlam[:, :], nlam[:, :], -0.2, None, ALU.add), reads=[nlam], writes=[nlam])

        unit = [0]
        fcount = [0]

        def finalize(Ob, nq, Gap, mix_dst_fn):
            O1 = Ob[0][0:nq, 0:128]
            O2 = Ob[1][0:nq, 0:128]
            p.grp("dve", lambda e, nq=nq, Ob=Ob: e.reciprocal(rr[0:nq, 0:1], Ob[0][0:nq, 128:129]), reads=[Ob[0]], writes=[rr])
            p.grp("dve", lambda e, nq=nq, Ob=Ob: e.reciprocal(rr[0:nq, 1:2], Ob[1][0:nq, 128:129]), reads=[Ob[1], rr], writes=[rr])
            p.grp("dve", lambda e, nq=nq: e.tensor_tensor(rr[0:nq, 1:2], rr[0:nq, 1:2], nlam[0:nq, :], ALU.mult), reads=[rr, nlam], writes=[rr])
            p.grp("dve", lambda e, nq=nq, O2=O2: e.tensor_scalar(t2[0:nq, :], O2, rr[0:nq, 1:2], None, ALU.mult), reads=[Ob[1], rr], writes=[t2])
            p.grp("dve", lambda e, nq=nq, O1=O1: e.scalar_tensor_tensor(oo[0:nq, :], O1, rr[0:nq, 0:1], t2[0:nq, :], ALU.mult, ALU.add), reads=[Ob[0], rr, t2], writes=[oo])
            p.grp("dve", lambda e, nq=nq: e.scalar_tensor_tensor(junk[0:nq, :], oo[0:nq, :], 1.0, oo[0:nq, :], ALU.mult, ALU.mult, accum_out=ssq[0:nq, :]),
                  reads=[oo], writes=[junk, ssq])
            p.grp("act", lambda e, nq=nq: e.activation(rs2[0:nq, :], ssq[0:nq, :], AF.Ln, bias=1e-5, scale=1.0 / 128), reads=[ssq], writes=[rs2])
            p.grp("act", lambda e, nq=nq: e.activation(rs2[0:nq, :], rs2[0:nq, :], AF.Exp, scale=-0.5, bias=math.log(0.8)), reads=[rs2], writes=[rs2])
            p.grp("dve", lambda e, nq=nq: e.scalar_tensor_tensor(oo[0:nq, :], oo[0:nq, :], rs2[0:nq, 0:1], swl[0:nq, :], ALU.mult, ALU.mult),
                  reads=[oo, rs2, swl], writes=[oo])
            p.grp("dve", lambda e, nq=nq, Gap=Gap: e.tensor_tensor(mixb[0:nq, :], oo[0:nq, :], Gap, ALU.mult), reads=[oo, Gh, Gs], writes=[mixb])
            p.grp("pe", lambda e, nq=nq: e.transpose(tpp[:, 0:nq], mixb[0:nq, :], identb[0:nq, 0:nq]), reads=[mixb, identb], writes=[tpp])
            p.grp("act", lambda e, nq=nq: e.copy(mT_st[:, 0:nq], tpp[:, 0:nq]), reads=[tpp], writes=[mT_st])
            p.dma("pool", lambda e, nq=nq, mix_dst_fn=mix_dst_fn: e.dma_start(out=mix_dst_fn(), in_=mT_st[:, 0:nq]), "mts", reads=[mT_st])

        for h in range(H):
            sg = SIG[h]
            for qd in range(4):
                w = S // 4
                p.dma("sp", lambda e, h=h, qd=qd, w=w: e.dma_start(out=KTh[:, qd * w:(qd + 1) * w], in_=KT[h, :, qd * w:(qd + 1) * w]), "kth", writes=[KTh])
                nk = NT
                p.dma("sp", lambda e, h=h, qd=qd, nk=nk: e.dma_start(
                    out=V1h[:, qd * nk:(qd + 1) * nk, :], in_=V1[h, qd * nk * 128:(qd + 1) * nk * 128, :].rearrange("(k p) e -> p k e", p=128)),
                    "v1h", writes=[V1h])
            p.dma("sp", lambda e, h=h: e.dma_start(out=QTh[:, :], in_=QT[h, :, :]), "qth", writes=[QTh])
            p.dma("sp", lambda e, h=h: e.dma_start(out=Gh[:, :, :], in_=GS[0:NO, h * 128:(h + 1) * 128].rearrange("(q p) e -> p q e", p=128)), "gh", writes=[Gh])
            Ob = Obs
            allu = []
            for qs in range(NT):
                units = []
                for G in range(qs + 1):
                    if SKIP_THRESH is not None and G < qs and sg * 512 * (qs - G - 1) > SKIP_THRESH:
                        continue
                    for c in range(2):
                        units.append((G, c))
                firstc = {}
                lastc = {}
                for idx, (G, c) in enumerate(units):
                    firstc.setdefault(c, idx)
                    lastc[c] = idx
                for idx, (G, c) in enumerate(units):
                    allu.append((qs, G, c, firstc[c] == idx, lastc[c] == idx, idx == len(units) - 1))

            def stage1_pair(un0, un1, h=h, sg=sg):
                qs, G = un0[0], un0[1]
                assert un1[0] == qs and un1[1] == G and un0[2] == 0 and un1[2] == 1
                u0 = unit[0]
                unit[0] += 2
                bufs = [(Sps[(u0 + c) % 4], Sbs[(u0 + c) % NSB], Pts[(u0 + c) % NSB]) for c in range(2)]
                fns = []
                for m in range(4):
                    for c in range(2):
                        fns.append(_mm(bufs[c][0][:, m * 128:(m + 1) * 128], KTh[64 * c:64 * c + 64, (4 * G + m) * 128:(4 * G + m + 1) * 128],
                                       QTh[64 * c:64 * c + 64, qs * 128:(qs + 1) * 128], True, True))
                p.grp("pe", fns, reads=[KTh, QTh], writes=[bufs[0][0], bufs[1][0]])
                tbl = tband if G == qs else tfar
                bias = 0.0 if G == qs else -sg * 512.0 * (qs - G - 1)
                for c in range(2):
                    Sp, Sb, Pt = bufs[c]
                    p.grp("dve", lambda e, Sp=Sp, Sb=Sb, tbl=tbl, h=h: e.scalar_tensor_tensor(Sb[:, :], Sp[:, :], 0.125, tbl[:, h, :], ALU.mult, ALU.add),
                          reads=[Sp, tbl], writes=[Sb])
                    p.grp("act", lambda e, Sb=Sb, Pt=Pt, bias=bias: e.activation(Pt[:, :], Sb[:, :], AF.Exp, bias=bias, scale=1.0), reads=[Sb], writes=[Pt])
                return bufs[0][2], bufs[1][2]

            def stage1(un, h=h, sg=sg):
                qs, G, c, fst, lst, lastq = un
                u = unit[0]
                unit[0] += 1
                Sp, Sb, Pt = Sps[u % 4], Sbs[u % NSB], Pts[u % NSB]
                fns = [_mm(Sp[:, m * 128:(m + 1) * 128], KTh[64 * c:64 * c + 64, (4 * G + m) * 128:(4 * G + m + 1) * 128],
                           QTh[64 * c:64 * c + 64, qs * 128:(qs + 1) * 128], True, True) for m in range(4)]
                p.grp("pe", fns, reads=[KTh, QTh], writes=[Sp])
                tbl = tband if G == qs else tfar
                p.grp("dve", lambda e, Sp=Sp, Sb=Sb, tbl=tbl, h=h: e.scalar_tensor_tensor(Sb[:, :], Sp[:, :], 0.125, tbl[:, h, :], ALU.mult, ALU.add),
                      reads=[Sp, tbl], writes=[Sb])
                bias = 0.0 if G == qs else -sg * 512.0 * (qs - G - 1)
                p.grp("act", lambda e, Sb=Sb, Pt=Pt, bias=bias: e.activation(Pt[:, :], Sb[:, :], AF.Exp, bias=bias, scale=1.0), reads=[Sb], writes=[Pt])
                return Pt

            def stage2(un, Pt, h=h):
                qs, G, c, fst, lst, lastq = un
                fns = []
                for m in range(4):
                    fns.append(_mm(Ob[c][:, 0:129], Pt[:, m * 128:(m + 1) * 128], V1h[:, 4 * G + m, 0:129], fst and m == 0, lst and m == 3))
                p.grp("pe", fns, reads=[Pt, V1h], writes=[Ob[c]])
                if lastq:
                    finalize(Ob, 128, Gh[:, qs, :], lambda h=h, qs=qs: MIXT[4 + h, :, qs * 128:(qs + 1) * 128])

            pending = []
            assert len(allu) % 2 == 0
            for i in range(0, len(allu), 2):
                pt0, pt1 = stage1_pair(allu[i], allu[i + 1])
                pending.append((allu[i], pt0))
                pending.append((allu[i + 1], pt1))
                while len(pending) > LOOKAHEAD:
                    stage2(*pending.pop(0))
            while pending:
                stage2(*pending.pop(0))
            for s_ in range(2):
                p.dma("sp", lambda e, s_=s_, h=h: e.dma_start(out=kc[:, :, :], in_=ck[s_, :, h, :].rearrange("(k p) e -> p k e", p=128)), "kc", writes=[kc])
                p.dma("sp", lambda e, s_=s_, h=h: e.dma_start(out=vcf[:, :, :], in_=cv[s_, :, h, :].rearrange("(k p) e -> p k e", p=128)), "vc", writes=[vcf])
                p.dma("sp", lambda e, s_=s_, h=h: e.dma_start(out=KTn[:, :], in_=KT[h, :, S + 32 * s_:S + 32 * s_ + 16]), "ktn", writes=[KTn])
                p.dma("sp", lambda e, s_=s_, h=h: e.dma_start(out=V1n[:, :], in_=V1[h, S + 32 * s_:S + 32 * s_ + 16, :]), "v1n", writes=[V1n])
                p.dma("sp", lambda e, s_=s_, h=h: e.dma_start(out=Gs[:, :], in_=GS[NO + 32 * s_:NO + 32 * s_ + 16, h * 128:(h + 1) * 128]), "gsn", writes=[Gs])
                p.grp("pool", lambda e: e.memset(V1c[:, :, 128:136], 1.0), writes=[V1c])
                p.grp("pool", lambda e: e.tensor_copy(V1c[:, :, 0:128], vcf[:, :, :]), reads=[vcf], writes=[V1c])
                for k4 in range(4):
                    fns = [lambda e, k4=k4, m=m, Ob=Ob: e.transpose(tpf[:, m * 128:(m + 1) * 128], kc[:, 4 * k4 + m, :], identf[:, :]) for m in range(4)]
                    p.grp("pe", fns, reads=[kc, identf], writes=[tpf])
                    alt.copy(p, KTc[:, k4 * 512:(k4 + 1) * 512], tpf[:, :], reads=[tpf], writes=[KTc])
                Ob = Obs
                fcount[0] += 1
                qcol = NO + 32 * s_
                for c in range(2):
                    u = unit[0]
                    unit[0] += 1
                    Sp, Sb, Pt = Sps[u % 4], Sbs[u % NSB], Pts[u % NSB]
                    fns = [_mm(Sp[:, kt * 16:(kt + 1) * 16], KTc[64 * c:64 * c + 64, kt * 128:(kt + 1) * 128], QTh[64 * c:64 * c + 64, qcol:qcol + 16], True, True)
                           for kt in range(16)]
                    fns.append(_mm(Sp[0:16, 256:272], KTn[64 * c:64 * c + 64, :], QTh[64 * c:64 * c + 64, qcol:qcol + 16], True, True))
                    p.grp("pe", fns, reads=[KTc, KTn, QTh], writes=[Sp])
                    p.grp("dve", lambda e, Sp=Sp, Sb=Sb, h=h: e.scalar_tensor_tensor(Sb[:, 0:256], Sp[:, 0:256], 0.125, tsamp[:, h, :], ALU.mult, ALU.add),
                          reads=[Sp, tsamp], writes=[Sb])
                    p.grp("dve", lambda e, Sp=Sp, h=h: e.scalar_tensor_tensor(Sbn[:, :], Sp[0:16, 256:272], 0.125, tsnew[:, h, :], ALU.mult, ALU.add),
                          reads=[Sp, tsnew], writes=[Sbn])
                    p.grp("act", lambda e, Sb=Sb, Pt=Pt: e.activation(Pt[:, 0:256], Sb[:, 0:256], AF.Exp), reads=[Sb], writes=[Pt])
                    p.grp("act", lambda e: e.activation(Ptn[:, :], Sbn[:, :], AF.Exp), reads=[Sbn], writes=[Ptn])
                    fns = [_mm(Ob[c][0:16, 0:129], Pt[:, kt * 16:(kt + 1) * 16], V1c[:, kt, 0:129], kt == 0, False) for kt in range(16)]
                    fns.append(_mm(Ob[c][0:16, 0:129], Ptn[:, :], V1n[:, 0:129], False, True))
                    p.grp("pe", fns, reads=[Pt, Ptn, V1c, V1n], writes=[Ob[c]])
                finalize(Ob, 16, Gs[:, :], lambda h=h, qcol=qcol: MIXT[4 + h, :, qcol:qcol + 16])
        p.emit()

    with ExitStack() as es:
      if 'D' in PASSES:
        p = Prog(nc, es, "D")
        alt = Alt()
        Wo = p.sb("Wo", [128, 16, D], BF16)
        stg = [p.sb("stg%d" % i, [128, D], F32) for i in range(4)]
        mT = [p.sb("mT%d" % i, [128, 16, 512], BF16) for i in range(2)]
        mTs = p.sb("mTs", [128, 16, 48], BF16)
        xr = [p.sb("xr%d" % i, [128, D], F32) for i in range(2)]
        hb = [p.sb("hb%d" % i, [128, D], F32) for i in range(2)]
        junk = p.sb("junk", [128, D], BF16)
        finw = p.sb("finw", [128, D], F32, ro=True)
        ss = p.sb("ss", [128, 1], F32)
        rstd = p.sb("rstd", [128, 1], F32)
        accs = [p.ps("acc%d" % i, [128, 512], F32) for i in range(8)]
        p.dma("sp", lambda e: e.dma_start(out=finw[:, :], in_=finw_d[:, :]), "c0", writes=[finw])
        load_weight_bf16(p, w_out, 0, D, Wo, stg, alt)
        p.grp("pool", lambda e: e.memset(mTs[:, :, :], 0.0), writes=[mTs])
        for t in range(NT + 1):
            samp = t == NT
            ntok = 48 if samp else 128
            x_, hh = xr[t % 2], hb[t % 2]
            if samp:
                m, mc0 = mTs, 0
                for s_ in (0, 1):
                    p.dma("sp", lambda e, s_=s_: e.dma_start(
                        out=mTs[:, :, 32 * s_:32 * s_ + 16], in_=MIXT[:, :, NO + 32 * s_:NO + 32 * s_ + 16].rearrange("c p t -> p c t")), "mts", writes=[mTs])
            else:
                m, mc0 = mT[(t // 4) % 2], (t % 4) * 128
                if t % 4 == 0:
                    p.dma("sp", lambda e, m=m, t=t: e.dma_start(
                        out=m[:, :, :], in_=MIXT[:, :, t * 128:t * 128 + 512].rearrange("c p t -> p c t")), "mt%d" % ((t // 4) % 2), writes=[m])
            xsrc = xs[:, :] if samp else xa[t, 0, :, :]
            p.dma("sp", lambda e, x_=x_, xsrc=xsrc, ntok=ntok: e.dma_start(out=x_[0:ntok, :], in_=xsrc), "xr%d" % (t % 2), writes=[x_])
            for cg in range(4):
                a = accs[(t % 2) * 4 + cg]
                fns = [_mm(a[0:ntok, :], m[:, ch, mc0:mc0 + ntok], Wo[:, ch, cg * 512:(cg + 1) * 512], ch == 0, ch == 15) for ch in range(16)]
                p.grp("pe", fns, reads=[m, Wo], writes=[a])
                p.grp("dve", lambda e, a=a, hh=hh, x_=x_, cg=cg, ntok=ntok: e.tensor_tensor(
                    hh[0:ntok, cg * 512:(cg + 1) * 512], a[0:ntok, :], x_[0:ntok, cg * 512:(cg + 1) * 512], ALU.add), reads=[a, x_], writes=[hh])
            p.grp("act", lambda e, hh=hh, ntok=ntok: e.activation(junk[0:ntok, :], hh[0:ntok, :], AF.Square, accum_out=ss[0:ntok, :]), reads=[hh], writes=[junk, ss])
            p.grp("act", lambda e, ntok=ntok: e.activation(rstd[0:ntok, :], ss[0:ntok, :], AF.Ln, bias=eps, scale=1.0 / D), reads=[ss], writes=[rstd])
            p.grp("act", lambda e, ntok=ntok: e.activation(rstd[0:ntok, :], rstd[0:ntok, :], AF.Exp, scale=-0.5), reads=[rstd], writes=[rstd])
            p.grp("dve", lambda e, hh=hh, ntok=ntok: e.scalar_tensor_tensor(hh[0:ntok, :], hh[0:ntok, :], rstd[0:ntok, 0:1], finw[0:ntok, :], ALU.mult, ALU.mult),
                  reads=[hh, rstd, finw], writes=[hh])
            dst = y_s[:, :] if samp else y_own[t, :, :]
            p.dma("sp", lambda e, dst=dst, hh=hh, ntok=ntok: e.dma_start(out=dst, in_=hh[0:ntok, :]), "yo%d" % (t % 2), reads=[hh])
        p.emit()
    return nc


_CACHE = {}


def kernel(x_prompt, x_sample, cache_k, cache_v, state_pool, norm_w, w_in, w_pool, pool_scale,
           lambda_q1, lambda_k1, lambda_q2, lambda_k2, subln_w, w_out, final_norm_w):
    f = np.float32
    x_prompt = np.asarray(x_prompt, f)
    x_sample = np.asarray(x_sample, f)
    B, S, _ = x_prompt.shape
    NT = S // 512
    NO = NT * 128
    if NT not in _CACHE:
        _CACHE[NT] = build(NT)
    nc = _CACHE[NT]
    sig = np.array(SIG, np.float64)
    identb = np.eye(128, dtype=np.float32).astype(ml_dtypes.bfloat16)
    identf = np.eye(128, dtype=f)
    bc = lambda v: np.ascontiguousarray(np.broadcast_to(np.asarray(v, f).reshape(1, -1), (128, np.asarray(v).size)))
    lam = np.stack([np.asarray(a, f)[0] for a in (lambda_q1, lambda_k1, lambda_q2, lambda_k2)])
    lam_b = np.ascontiguousarray(np.broadcast_to(lam[None], (128, 4, 64)))
    swl = bc(np.asarray(subln_w, f)[0])
    pscale = np.ascontiguousarray(np.asarray(pool_scale, f)[0].reshape(4, 128).T)
    p_ = np.arange(128)[:, None, None, None]
    q_ = np.arange(128)[None, None, None, :]
    hh_ = sig[None, :, None, None]
    kt_ = np.arange(16)[None, None, :, None]
    qq_ = np.arange(16)[None, None, None, :]
    tsamp = (-hh_ * (2048 + qq_ - 128 * kt_ - p_)).astype(f).reshape(128, H, 256)
    kk_ = np.arange(16)[:, None, None]
    tsnew = (-sig[None, :, None] * np.abs(np.arange(16)[None, None, :] - kk_)).astype(f)
    in_maps = []
    for c in range(8):
        b, j = c // 4, c % 4
        order = [j] + [m for m in range(4) if m != j]
        xb = x_prompt[b].reshape(NT, 4, 128, D)
        xa = np.ascontiguousarray(xb[:, order])
        xh = np.zeros((NT, 16, D), f)
        for T in range(NT):
            st = 512 * T + 128 * j
            if st >= 16:
                xh[T] = x_prompt[b, st - 16:st]
        xs = np.zeros((48, D), f)
        xs[0:16] = x_sample[2 * c]
        xs[32:48] = x_sample[2 * c + 1]
        om = np.array(order)[None, None, :, None]
        qpos = 512 + 128 * j + q_
        kpos = 128 * om + p_
        tfar = (-hh_ * (qpos - kpos)).astype(f).reshape(128, H, 512)
        qpos = 128 * j + q_
        allowed = (kpos // 64) <= (qpos // 64)
        tband = np.where(allowed, -hh_ * np.abs(qpos - kpos), NEG).astype(f).reshape(128, H, 512)
        invc = np.zeros((128, 4, 16), f)
        for g, w in enumerate(WINS):
            for t in range(16):
                invc[:, g, t] = 1.0 / (min(w, t + 1) if j == 0 else w)
        in_maps.append({
            "xa": xa, "xh": xh, "xs": xs,
            "ck": np.ascontiguousarray(np.asarray(cache_k, f)[0, 2 * c:2 * c + 2]),
            "cv": np.ascontiguousarray(np.asarray(cache_v, f)[0, 2 * c:2 * c + 2]),
            "spool": np.ascontiguousarray(np.asarray(state_pool, f)[0, 2 * c:2 * c + 2]),
            "w_in": np.ascontiguousarray(np.asarray(w_in, f)[0]), "w_out": np.ascontiguousarray(np.asarray(w_out, f)[0]),
            "w_pool": np.ascontiguousarray(np.asarray(w_pool, f)[0]),
            "normw": bc(np.asarray(norm_w, f)[0]), "finw": bc(np.asarray(final_norm_w, f)),
            "pscale": pscale, "lam": lam_b, "swl": swl, "tfar": tfar, "tband": tband, "tsamp": tsamp, "tsnew": tsnew,
            "invc": invc, "identb": identb, "identf": identf,
        })
    res = run_bass_kernel_spmd(nc, in_maps, core_ids=list(range(8))).results
    DB = x_sample.shape[0]
    y_prompt = np.zeros((B, S, D), f)
    k_prompt = np.zeros((1, B, S, H, 128), f)
    v_prompt = np.zeros((1, B, S, H, 128), f)
    pool_prompt = np.zeros((1, B, 15, 512), f)
    y_sample = np.zeros((DB, 16, D), f)
    k_sample = np.zeros((1, DB, 16, H, 128), f)
    v_sample = np.zeros((1, DB, 16, H, 128), f)
    pool_sample = np.zeros((1, DB, 15, 512), f)
    for c in range(8):
        b, j = c // 4, c % 4
        r = res[c]
        y_prompt[b].reshape(NT, 4, 128, D)[:, j] = r["y_own"]
        k_prompt[0, b].reshape(NT, 4, 128, 1536)[:, j] = r["k_own"]
        v_prompt[0, b].reshape(NT, 4, 128, 1536)[:, j] = r["v_own"]
        if j == 3:
            pool_prompt[0, b] = r["pool_own"]
        for s_ in range(2):
            y_sample[2 * c + s_] = r["y_s"][32 * s_:32 * s_ + 16]
            k_sample[0, 2 * c + s_] = r["k_s"][32 * s_:32 * s_ + 16].reshape(16, H, 128)
            v_sample[0, 2 * c + s_] = r["v_s"][32 * s_:32 * s_ + 16].reshape(16, H, 128)
            pool_sample[0, 2 * c + s_] = r["pool_s"][s_]
    return (y_prompt, y_sample, k_prompt, v_prompt, pool_prompt, k_sample, v_sample, pool_sample)
```
